# Optimizing a Trainium2 kernel written in Bass

```python
import math
import jax, jax.numpy as jnp
from jax import lax
import numpy as np

D_MODEL = 1024
BATCH = 4
SEQ = 8192
DEPTH = 1

GRID_W = 64
CTX_LEN = 256
N_HEADS = 16
N_KV_HEADS = 4
HEAD_DIM = D_MODEL // N_HEADS
Q_GROUP = N_HEADS // N_KV_HEADS
WINDOW = 128
BLOCK = 128
ROPE_THETA = 10000.0
N_POOL_GROUPS = 4
POOL_WINDOWS = (2, 4, 8, 16)
POOL_WIDTH = D_MODEL // 2
POOL_GROUP_DIM = POOL_WIDTH // N_POOL_GROUPS
Q_WIDTH = N_HEADS * HEAD_DIM
KV_WIDTH = N_KV_HEADS * HEAD_DIM
N_BRANCHES = 2
GATE_WIDTH = N_BRANCHES * D_MODEL
IN_WIDTH = Q_WIDTH + 2 * KV_WIDTH + POOL_WIDTH + GATE_WIDTH
D_FF = 4 * D_MODEL
N_MOD = 6
ALPHA = (2.0 * DEPTH) ** 0.25
BETA = (8.0 * DEPTH) ** -0.25
LN_EPS = 1e-6
NEG_INF = -1e30

kernel_name = "hybrid_pool_swa_dit_block"


def _layernorm(z, g=None, b=None):
    zf = z.astype(jnp.float32)
    mu = jnp.mean(zf, axis=-1, keepdims=True)
    var = jnp.mean(jnp.square(zf - mu), axis=-1, keepdims=True)
    y = (zf - mu) * lax.rsqrt(var + LN_EPS)
    if g is not None:
        y = y * g.astype(jnp.float32) + b.astype(jnp.float32)
    return y.astype(z.dtype)


def _modulate(z, shift, scale):
    return _layernorm(z) * (1.0 + scale) + shift


def _split_proj(proj):
    B, L, _ = proj.shape
    o = np.cumsum([Q_WIDTH, KV_WIDTH, KV_WIDTH, POOL_WIDTH])
    q = proj[..., :o[0]].reshape(B, L, N_HEADS, HEAD_DIM)
    k = proj[..., o[0]:o[1]].reshape(B, L, N_KV_HEADS, HEAD_DIM)
    v = proj[..., o[1]:o[2]].reshape(B, L, N_KV_HEADS, HEAD_DIM)
    p = proj[..., o[2]:o[3]]
    g = proj[..., o[3]:]
    return q, k, v, p, g


def _axial_rope(z, rows, cols):
    half = HEAD_DIM // 2
    quarter = half // 2
    inv = 1.0 / (ROPE_THETA ** (jnp.arange(quarter, dtype=jnp.float32) / quarter))

    def rot(zh, pos):
        ang = pos.astype(jnp.float32)[:, None] * inv[None, :]
        cos = jnp.cos(ang)[None, :, None, :].astype(zh.dtype)
        sin = jnp.sin(ang)[None, :, None, :].astype(zh.dtype)
        z1, z2 = zh[..., :quarter], zh[..., quarter:]
        return jnp.concatenate([z1 * cos - z2 * sin, z2 * cos + z1 * sin], axis=-1)

    return jnp.concatenate([rot(z[..., :half], rows), rot(z[..., half:], cols)], axis=-1)


def _softmax_with_sink(logits, sink):
    s = jnp.broadcast_to(sink.astype(jnp.float32)[None, :, :, None, None], logits.shape[:-1] + (1,))
    probs = jax.nn.softmax(jnp.concatenate([logits, s], axis=-1), axis=-1)
    return probs[..., :-1]


def _context_attention(q_c, k_c, v_c, sink):
    B, C = q_c.shape[:2]
    qg = q_c.reshape(B, C, N_KV_HEADS, Q_GROUP, HEAD_DIM)
    logits = jnp.einsum('bqkgd,bckd->bkgqc', qg, k_c).astype(jnp.float32) / math.sqrt(HEAD_DIM)
    probs = _softmax_with_sink(logits, sink).astype(v_c.dtype)
    out = jnp.einsum('bkgqc,bckd->bqkgd', probs, v_c)
    return out.reshape(B, C, Q_WIDTH)


def _banded_attention(q, k, v, k_c, v_c, sink):
    B, S = q.shape[:2]
    nb = S // BLOCK
    pad = ((0, 0), (BLOCK, BLOCK), (0, 0), (0, 0))
    k_p = jnp.pad(k, pad)
    v_p = jnp.pad(v, pad)
    qg = q.reshape(B, nb, BLOCK, N_KV_HEADS, Q_GROUP, HEAD_DIM).transpose(1, 0, 2, 3, 4, 5)
    offs_q = jnp.arange(BLOCK)
    offs_k = jnp.arange(3 * BLOCK) - BLOCK
    rel = offs_k[None, :] - offs_q[:, None]
    in_window = jnp.abs(rel) <= WINDOW
    sink_g = sink.reshape(N_KV_HEADS, Q_GROUP)
    scale = 1.0 / math.sqrt(HEAD_DIM)

    def one_block(args):
        qb, b = args
        start = b * BLOCK
        kb = lax.dynamic_slice_in_dim(k_p, start, 3 * BLOCK, axis=1)
        vb = lax.dynamic_slice_in_dim(v_p, start, 3 * BLOCK, axis=1)
        j = start - BLOCK + jnp.arange(3 * BLOCK)
        valid = in_window & ((j >= 0) & (j < S))[None, :]
        l_loc = jnp.einsum('bqkgd,bskd->bkgqs', qb, kb).astype(jnp.float32) * scale
        l_loc = jnp.where(valid, l_loc, NEG_INF)
        l_ctx = jnp.einsum('bqkgd,bckd->bkgqc', qb, k_c).astype(jnp.float32) * scale
        probs = _softmax_with_sink(jnp.concatenate([l_loc, l_ctx], axis=-1), sink_g)
        p_loc = probs[..., :3 * BLOCK].astype(v.dtype)
        p_ctx = probs[..., 3 * BLOCK:].astype(v.dtype)
        out = (jnp.einsum('bkgqs,bskd->bqkgd', p_loc, vb)
               + jnp.einsum('bkgqc,bckd->bqkgd', p_ctx, v_c))
        return out.reshape(B, BLOCK, Q_WIDTH)

    outs = lax.map(one_block, (qg, jnp.arange(nb)))
    return outs.transpose(1, 0, 2, 3).reshape(B, S, Q_WIDTH)


def _multiscale_pool(p):
    B, L, _ = p.shape
    pg = p.reshape(B, L, N_POOL_GROUPS, POOL_GROUP_DIM).astype(jnp.float32)
    cs = jnp.concatenate([jnp.zeros((B, 1, N_POOL_GROUPS, POOL_GROUP_DIM), jnp.float32),
                          jnp.cumsum(pg, axis=1)], axis=1)
    t = jnp.arange(L)
    outs = []
    for gi, w in enumerate(POOL_WINDOWS):
        lo = jnp.clip(t - w // 2, 0, L)
        hi = jnp.clip(t + w // 2, 0, L)
        cs_g = cs[:, :, gi]
        window_sum = jnp.take(cs_g, hi, axis=1) - jnp.take(cs_g, lo, axis=1)
        cnt = (hi - lo).astype(jnp.float32)
        outs.append(window_sum / cnt[None, :, None] - pg[:, :, gi])
    return jnp.stack(outs, axis=2).astype(p.dtype)


def _merge_branches(attn_o, p, g, w_ab, w_pool, pool_scale, w_out):
    B, L, _ = p.shape
    attn_d = attn_o @ w_ab
    pooled = _multiscale_pool(p)
    pool_d = jnp.einsum('blgc,gcd->blgd', pooled, w_pool).reshape(B, L, D_MODEL) * pool_scale
    g_attn, g_pool = g[..., :D_MODEL], g[..., D_MODEL:]
    merged = jax.nn.sigmoid(g_attn) * attn_d + jax.nn.sigmoid(g_pool) * pool_d
    return merged @ w_out


def _mlp_sublayer(z, shift, scale, gate, w1, w2, g, b):
    u = _modulate(z, shift, scale)
    h = jnp.square(jax.nn.relu(u @ w1)) @ w2
    return _layernorm(ALPHA * z + gate * h, g, b)


def setup_inputs(seed: int = 0) -> dict:
    key = jax.random.key(seed)
    ks = jax.random.split(key, 18)
    f32 = jnp.float32
    n = lambda k, shape, s: jax.random.normal(k, shape, f32) * s
    return {
        "x": n(ks[0], (BATCH, SEQ, D_MODEL), 1.0),
        "c": n(ks[1], (BATCH, D_MODEL), 1.0),
        "ctx": n(ks[2], (BATCH, CTX_LEN, D_MODEL), 1.0),
        "c_ctx": n(ks[3], (D_MODEL,), 1.0),
        "w_ada": n(ks[4], (DEPTH, D_MODEL, N_MOD * D_MODEL), D_MODEL ** -0.5),
        "b_ada": n(ks[5], (DEPTH, N_MOD * D_MODEL), 0.01),
        "w_in": n(ks[6], (DEPTH, D_MODEL, IN_WIDTH), D_MODEL ** -0.5),
        "w_attn_branch": n(ks[7], (DEPTH, Q_WIDTH, D_MODEL), Q_WIDTH ** -0.5),
        "w_pool": n(ks[8], (DEPTH, N_POOL_GROUPS, POOL_GROUP_DIM, D_MODEL // N_POOL_GROUPS), POOL_GROUP_DIM ** -0.5),
        "pool_scale": 1.0 + n(ks[9], (DEPTH, D_MODEL), 0.05),
        "attn_sink": n(ks[10], (DEPTH, N_HEADS), 0.5),
        "w_out": n(ks[11], (DEPTH, D_MODEL, D_MODEL), BETA * D_MODEL ** -0.5),
        "ln1_g": 1.0 + n(ks[12], (DEPTH, D_MODEL), 0.02),
        "ln1_b": n(ks[13], (DEPTH, D_MODEL), 0.02),
        "w_mlp_in": n(ks[14], (DEPTH, D_MODEL, D_FF), D_MODEL ** -0.5),
        "w_mlp_out": n(ks[15], (DEPTH, D_FF, D_MODEL), BETA * D_FF ** -0.5),
        "ln2_g": 1.0 + n(ks[16], (DEPTH, D_MODEL), 0.02),
        "ln2_b": n(ks[17], (DEPTH, D_MODEL), 0.02),
    }


def reference(x, c, ctx, c_ctx, w_ada, b_ada, w_in, w_attn_branch, w_pool, pool_scale,
              attn_sink, w_out, ln1_g, ln1_b, w_mlp_in, w_mlp_out, ln2_g, ln2_b):
    S = x.shape[1]
    ROWS = S // GRID_W
    rows = jnp.repeat(jnp.arange(ROWS), GRID_W)
    cols = jnp.tile(jnp.arange(GRID_W), ROWS)

    for l in range(DEPTH):
        mod = jax.nn.silu(c) @ w_ada[l] + b_ada[l]
        mod_c = jax.nn.silu(c_ctx) @ w_ada[l] + b_ada[l]
        sh1, sc1, g1, sh2, sc2, g2 = [m[:, None, :] for m in jnp.split(mod, N_MOD, axis=-1)]
        csh1, csc1, cg1, csh2, csc2, cg2 = jnp.split(mod_c, N_MOD, axis=-1)

        q_c, k_c, v_c, p_c, gt_c = _split_proj(_modulate(ctx, csh1, csc1) @ w_in[l])

        q, k, v, p, gt = _split_proj(_modulate(x, sh1, sc1) @ w_in[l])
        q = _axial_rope(q, rows, cols)
        k = _axial_rope(k, rows, cols)
        attn_o = _banded_attention(q, k, v, k_c, v_c, attn_sink[l])
        y = _merge_branches(attn_o, p, gt, w_attn_branch[l], w_pool[l], pool_scale[l], w_out[l])
        x = _layernorm(ALPHA * x + g1 * y, ln1_g[l], ln1_b[l])
        x = _mlp_sublayer(x, sh2, sc2, g2, w_mlp_in[l], w_mlp_out[l], ln2_g[l], ln2_b[l])

        if l < DEPTH - 1:
            attn_c = _context_attention(q_c, k_c, v_c, attn_sink[l])
            y_c = _merge_branches(attn_c, p_c, gt_c, w_attn_branch[l], w_pool[l], pool_scale[l], w_out[l])
            ctx = _layernorm(ALPHA * ctx + cg1 * y_c, ln1_g[l], ln1_b[l])
            ctx = _mlp_sublayer(ctx, csh2, csc2, cg2, w_mlp_in[l], w_mlp_out[l], ln2_g[l], ln2_b[l])
    return x
```

```python
import math
from contextlib import ExitStack

import numpy as np
import concourse.bass as bass
import concourse.mybir as mybir
from concourse.bass_utils import run_bass_kernel_spmd

F32 = mybir.dt.float32
BF16 = mybir.dt.bfloat16
AF = mybir.ActivationFunctionType
ALU = mybir.AluOpType

D = 1024
SEQ = 8192
NCORES = 8
TOK = 4096
NBLK = 32
NB = 2
SBW = NB * 128
NSB = NBLK // NB
NEXT = NBLK + 2
HEADS = 16
HD = 64
CTX = 256
DFF = 4096
ALPHA = 2.0 ** 0.25
LN_EPS = 1e-6
NEG = -30000.0
GA_OFF, GP_OFF, Q_OFF, K_OFF, V_OFF, P_OFF, WIN_COLS = 0, 1024, 2048, 3072, 3584, 3840, 4352
C_ID, C_ID4, C_MASK, C_PERM, C_AMAT = 0, 128, 640, 1152, 1280
C_COLS = C_AMAT + 36 * 128
RING = 6
EPOCH = 20000
DEBUG = False


class Eng:
    def __init__(self, ctx, name, h, is_pe=False):
        self.ctx, self.name, self.h, self.is_pe = ctx, name, h, is_pe
        self.sems = []
        self.count = 0
        self.known = {}

    def sem_for(self, seq):
        ep = (seq - 1) // EPOCH
        while len(self.sems) <= ep:
            self.sems.append(self.ctx.es.enter_context(self.ctx.nc.semaphore("s_%s_%d" % (self.name, len(self.sems)))))
        return self.sems[ep], seq - ep * EPOCH, ep


class Tile:
    def __init__(self, name, psum=False, multi=False):
        self.name = name
        self.psum = psum
        self.multi = multi
        self.writers = {}
        self.readers = {}
        self.dsem = None
        self.dcnt = 0
        self.dw = 0
        self.da = 0


class Ctx:
    def __init__(self, nc, es):
        self.nc, self.es = nc, es
        self.pe = Eng(self, "pe", nc.tensor, True)
        self.act = Eng(self, "act", nc.scalar)
        self.dve = Eng(self, "dve", nc.vector)
        self.pool = Eng(self, "pool", nc.gpsimd)
        self.sp = Eng(self, "sp", nc.sync)
        self.engs = [self.pe, self.act, self.dve, self.pool, self.sp]
        self.dma_tiles = []

    def wait_eng(self, eng, e, n):
        if n <= 0:
            return
        if e is eng and eng.is_pe:
            return
        sem, val, ep = e.sem_for(n)
        key = (e.name, ep)
        if eng.known.get(key, 0) >= val:
            return
        eng.h.wait_ge(sem, val)
        eng.known[key] = val

    def wait_dma(self, eng, t, val):
        if val <= 0:
            return
        key = ("d", id(t))
        if eng.known.get(key, 0) >= val:
            return
        eng.h.wait_ge(t.dsem, val)
        eng.known[key] = val

    def _pre(self, eng, reads, writes):
        for t in reads:
            for e, n in t.writers.items():
                self.wait_eng(eng, e, n)
            if t.psum:
                for e, n in t.readers.items():
                    if e is not eng:
                        self.wait_eng(eng, e, n)
            self.wait_dma(eng, t, t.dw)
        for t in writes:
            if not t.multi:
                for e, n in t.writers.items():
                    self.wait_eng(eng, e, n)
            for e, n in t.readers.items():
                self.wait_eng(eng, e, n)
            self.wait_dma(eng, t, t.da)

    def op(self, eng, fn, reads=(), writes=()):
        self._pre(eng, reads, writes)
        ins = fn()
        eng.count += 1
        seq = eng.count
        sem, _, _ = eng.sem_for(seq)
        ins.then_inc(sem, 1)
        for t in reads:
            if t not in writes:
                t.readers[eng] = seq
        for t in writes:
            if t.multi:
                t.writers[eng] = seq
            else:
                t.writers = {eng: seq}
                t.readers = {}
        return ins

    def dma(self, q, out_ap, in_ap, tile, load, extra_reads=(), **kw):
        if tile.dsem is None:
            tile.dsem = self.es.enter_context(self.nc.semaphore("d_%s" % tile.name))
            self.dma_tiles.append(tile)
        if load:
            self._pre(q, extra_reads, [tile])
        else:
            self._pre(q, [tile] + list(extra_reads), [])
        ins = q.h.dma_start(out=out_ap, in_=in_ap, **kw)
        tile.dcnt += 16
        ins.then_inc(tile.dsem, 16)
        if load:
            tile.dw = tile.da = tile.dcnt
            tile.writers = {}
            tile.readers = {}
        else:
            tile.da = tile.dcnt
        return ins

    def barrier(self, engs=None):
        engs = engs or self.engs
        for e in engs:
            for f in self.engs:
                if f is not self.sp and f is not e:
                    self.wait_eng(e, f, f.count)
            for t in self.dma_tiles:
                self.wait_dma(e, t, t.dcnt)


class Ring:
    def __init__(self, items):
        self.items, self.i = items, 0

    def next(self):
        it = self.items[self.i % len(self.items)]
        self.i += 1
        return it


class _Stop(Exception):
    pass


def build_program(level=3, nsb_run=NSB):
    nc = bass.Bass("TRN2", target_bir_lowering=False)
    try:
        _build(nc, level, nsb_run)
    except _Stop:
        pass
    return nc


STOPAT = None
VARIANT = None
ROPE_ADD_DVE = False


def _build(nc, level, nsb_run):

    def din(name, shape, dt=F32):
        return nc.dram_tensor(name, list(shape), dt, kind="ExternalInput").ap()

    x_main = din("x_main", [TOK, D])
    x_halo = din("x_halo", [256, D])
    ctx_in = din("ctx_in", [CTX, D])
    ccT_in = din("ccT", [128, 8, 2])
    w_ada = din("w_ada", [D, 6 * D])
    b_adaT_in = din("b_adaT", [128, 48])
    b_ada = din("b_ada", [1, 6 * D])
    w_in_p = din("w_in_p", [D, WIN_COLS])
    w_ab = din("w_ab", [D, D])
    w_pool = din("w_pool", [4, 128, 256])
    pool_scale = din("pool_scale", [1, D])
    attn_sink = din("attn_sink", [1, HEADS])
    w_out = din("w_out", [D, D])
    ln1_g = din("ln1_g", [1, D])
    ln1_b = din("ln1_b", [1, D])
    w_mlp_in = din("w_mlp_in", [D, DFF])
    w_mlp_out = din("w_mlp_out", [DFF, D])
    ln2_g = din("ln2_g", [1, D])
    ln2_b = din("ln2_b", [1, D])
    rope_tab = din("rope_tab", [128, 2, NEXT * 128])
    cst_in = din("cst", [128, C_COLS])
    out = nc.dram_tensor("out", [TOK, D], F32, kind="ExternalOutput").ap()
    mt_scr = nc.dram_tensor("mt_scr", [NSB, 128, 8 * SBW], BF16, kind="Internal").ap()
    x1_scr = nc.dram_tensor("x1_scr", [TOK, D], F32, kind="Internal").ap()
    dbg = {}
    if DEBUG:
        dbg["modT"] = nc.dram_tensor("dbg_modT", [128, 96], F32, kind="ExternalOutput").ap()
        dbg["uT0"] = nc.dram_tensor("dbg_uT0", [128, 8 * SBW], BF16, kind="ExternalOutput").ap()
        dbg["QT0"] = nc.dram_tensor("dbg_QT0", [128, 8 * SBW], BF16, kind="ExternalOutput").ap()
        dbg["KT1"] = nc.dram_tensor("dbg_KT1", [128, 512], BF16, kind="ExternalOutput").ap()
        dbg["V1"] = nc.dram_tensor("dbg_V1", [128, 4 * 65], BF16, kind="ExternalOutput").ap()
        dbg["OT0"] = nc.dram_tensor("dbg_OT0", [128, 8 * SBW], BF16, kind="ExternalOutput").ap()
        dbg["PL0"] = nc.dram_tensor("dbg_PL0", [128, 4 * SBW], BF16, kind="ExternalOutput").ap()
        dbg["mt"] = nc.dram_tensor("dbg_mt", [128, 8 * SBW], BF16, kind="ExternalOutput").ap()
        dbg["x1"] = nc.dram_tensor("dbg_x1", [256, D], F32, kind="ExternalOutput").ap()

    with ExitStack() as es:
        cx = Ctx(nc, es)
        PE, ACT, DVE, POOL, SP = cx.pe, cx.act, cx.dve, cx.pool, cx.sp

        def sbuf(stack, name, shape, dt):
            return stack.enter_context(nc.sbuf_tensor(name, list(shape), dt))

        banks = []
        for i in range(8):
            t = es.enter_context(nc.psum_tensor("bank%d" % i, [128, 512], F32))
            banks.append((Tile("bank%d" % i, psum=True), t))

        cst = sbuf(es, "cst_sb", [128, C_COLS], BF16)
        cstT = Tile("cst")
        ident = cst[:, C_ID:C_ID + 128]
        ident4 = cst[:, C_ID4:C_ID4 + 512]
        perm = cst[:, C_PERM:C_PERM + 128]

        def mask_ap(i):
            return cst[:, C_MASK + i * 128:C_MASK + (i + 1) * 128]

        def amat_ap(var, rel, g):
            o = C_AMAT + ((var * 3 + rel) * 4 + g) * 128
            return cst[:, o:o + 128]

        modT = sbuf(es, "modT", [128, 48, 2], F32)
        modTt = Tile("modT")
        g_scr = nc.dram_tensor("g_scr", [2, D], F32, kind="Internal").ap()
        mhalf = sbuf(es, "mhalf", [128, 1], F32)
        expsink = sbuf(es, "expsink", [128, HEADS], F32)
        miscT = Tile("misc")
        dbg_stage = None

        def dump(name, tile, ap):
            if DEBUG and name in dbg:
                cx.dma(SP, dbg[name], ap, tile, load=False)

        ln_sm = []
        for i in range(8):
            ln_sm.append((Tile("lnsm%d" % i), sbuf(es, "lnst%d" % i, [128, 2, 6], F32), sbuf(es, "lnmv%d" % i, [128, 4], F32)))
        ln_ring = Ring(ln_sm)

        def ln_stats(xt, xap):
            t, st, mv = ln_ring.next()
            cx.op(DVE, lambda: nc.vector.bn_stats(out=st[:, 0, :], in_=xap[:, 0:512]), [xt], [t])
            cx.op(DVE, lambda: nc.vector.bn_stats(out=st[:, 1, :], in_=xap[:, 512:1024]), [xt, t], [t])
            cx.op(DVE, lambda: nc.vector.bn_aggr(out=mv[:, 0:2], in_=st[:, :, :].rearrange("p a b -> p (a b)")), [t], [t])
            cx.op(POOL, lambda: nc.gpsimd.tensor_scalar(out=mv[:, 2:3], in0=mv[:, 1:2], scalar1=LN_EPS, scalar2=None, op0=ALU.add), [t], [t])
            cx.op(POOL, lambda: nc.gpsimd.tensor_tensor(out=mv[:, 2:3], in0=mv[:, 2:3], in1=mhalf[:, :], op=ALU.pow), [t, miscT], [t])
            cx.op(POOL, lambda: nc.gpsimd.tensor_tensor(out=mv[:, 3:4], in0=mv[:, 0:1], in1=mv[:, 2:3], op=ALU.mult), [t], [t])
            cx.op(POOL, lambda: nc.gpsimd.tensor_scalar(out=mv[:, 3:4], in0=mv[:, 3:4], scalar1=-1.0, scalar2=None, op0=ALU.mult), [t], [t])
            return t, mv[:, 2:3], mv[:, 3:4]

        def cast_load(dst_fn, src_fn, ncols, tile):
            c0 = 0
            while c0 < ncols:
                w = min(2048, ncols - c0)
                cx.dma(POOL, dst_fn(c0, w), src_fn(c0, w), tile, load=True)
                c0 += w

        cast_load(lambda c0, w: cst[:, c0:c0 + w], lambda c0, w: cst_in[:, c0:c0 + w], C_COLS, cstT)
        cx.op(POOL, lambda: nc.gpsimd.memset(mhalf[:, :], -0.5), [], [miscT])
        cx.dma(SP, expsink[:, :], attn_sink[0:1, :].partition_broadcast(128), miscT, load=True)
        cx.op(ACT, lambda: nc.scalar.activation(out=expsink[:, :], in_=expsink[:, :], func=AF.Exp), [miscT], [miscT])

        sw = ExitStack()
        sw.__enter__()
        Win = sbuf(sw, "Win", [128, 8, WIN_COLS], BF16)
        WinT = Tile("Win", multi=True)
        Wab = sbuf(sw, "Wab", [128, 8, D], BF16)
        WabT = Tile("Wab", multi=True)
        Wpl = sbuf(sw, "Wpl", [128, 4, 256], BF16)
        WplT = Tile("Wpl")
        with ExitStack() as s0:
            ccT = sbuf(s0, "ccT_sb", [128, 8, 2], F32)
            cth = sbuf(s0, "cth", [128, 8, 2], F32)
            siluT = sbuf(s0, "siluT", [128, 8, 2], BF16)
            badaT = sbuf(s0, "badaT", [128, 48], F32)
            brow = sbuf(s0, "brow", [1, 2, D], F32)
            grow = sbuf(s0, "grow", [1, 2, D], F32)
            growT = Tile("grow")
            s0T = Tile("s0")
            wsl = [(Tile("wada%d" % i, multi=True), sbuf(s0, "wada%d" % i, [128, 8, 512], BF16)) for i in range(2)]
            stq = Ring([(Tile("stq%d" % i), sbuf(s0, "stq%d" % i, [128, 2048], F32)) for i in range(5)])
            cast_engs = Ring([DVE, ACT])

            def stage_cast(dst_ap, src_ap, dtile, shape3=None):
                st_t, st = stq.next()
                w = 1
                for d_ in dst_ap.shape[1:]:
                    w *= d_
                sv = st[:, 0:w]
                if shape3 is not None:
                    sv = sv.rearrange("p (k n) -> p k n", k=shape3)
                cx.dma(SP, sv, src_ap, st_t, load=True)
                e = cast_engs.next()
                if e is ACT:
                    cx.op(ACT, lambda: nc.scalar.copy(out=dst_ap, in_=sv), [st_t], [dtile])
                elif e is DVE:
                    cx.op(DVE, lambda: nc.vector.tensor_copy(out=dst_ap, in_=sv), [st_t], [dtile])
                else:
                    cx.op(POOL, lambda: nc.gpsimd.tensor_copy(out=dst_ap, in_=sv), [st_t], [dtile])

            cx.dma(SP, ccT[:, :, :], ccT_in[:, :, :], s0T, load=True)
            cx.dma(SP, badaT[:, :], b_adaT_in[:, :], s0T, load=True)
            cx.dma(SP, brow[:, 0, :], b_ada[0:1, 2 * D:3 * D], s0T, load=True)
            cx.dma(SP, brow[:, 1, :], b_ada[0:1, 5 * D:6 * D], s0T, load=True)
            cx.op(ACT, lambda: nc.scalar.activation(out=cth[:, :, :], in_=ccT[:, :, :], func=AF.Tanh, scale=0.5), [s0T], [s0T])
            cx.op(DVE, lambda: nc.vector.scalar_tensor_tensor(out=cth[:, :, :], in0=cth[:, :, :], scalar=1.0, in1=ccT[:, :, :], op0=ALU.add, op1=ALU.mult), [s0T], [s0T])
            cx.op(DVE, lambda: nc.vector.tensor_scalar(out=siluT[:, :, :], in0=cth[:, :, :], scalar1=0.5, scalar2=None, op0=ALU.mult), [s0T], [s0T])
            w_ada_v = w_ada.rearrange("(kc p) n -> p kc n", p=128)
            w_in_v = w_in_p.rearrange("(kc p) n -> p kc n", p=128)
            w_ab_v = w_ab.rearrange("(kc p) n -> p kc n", p=128)
            bmod_t, bmod = banks[0]
            brw_t, brw = banks[1]

            def load_win():
                for kc in range(8):
                    c0 = 0
                    while c0 < WIN_COLS:
                        w = min(2048, WIN_COLS - c0)
                        stage_cast(Win[:, kc, c0:c0 + w], w_in_v[:, kc, c0:c0 + w], WinT)
                        c0 += w

            def load_wab():
                for kc in range(0, 8, 2):
                    stage_cast(Wab[:, kc:kc + 2, :], w_ab_v[:, kc:kc + 2, :], WabT, shape3=2)

            for pc in range(12):
                if pc == 4:
                    load_win()
                wt, wtile = wsl[pc % 2]
                for h in range(2):
                    stage_cast(wtile[:, :, h * 256:(h + 1) * 256], w_ada_v[:, :, pc * 512 + h * 256:pc * 512 + (h + 1) * 256], wt, shape3=8)
                for mm in range(4):
                    m = pc * 4 + mm
                    for kc in range(8):
                        cx.op(PE, lambda: nc.tensor.matmul(bmod[:, 2 * m:2 * m + 2], lhsT=wtile[:, kc, mm * 128:(mm + 1) * 128],
                                                           rhs=siluT[:, kc, :], start=(kc == 0), stop=(kc == 7)),
                              [wt, s0T], [bmod_t])
                if pc in (4, 5, 10, 11):
                    for kc in range(8):
                        cx.op(PE, lambda: nc.tensor.matmul(brw[0:2, :], lhsT=siluT[:, kc, :], rhs=wtile[:, kc, :],
                                                           start=(kc == 0), stop=(kc == 7)), [wt, s0T], [brw_t])
                    bi = 0 if pc < 6 else 1
                    hf = pc % 2
                    cx.op(DVE, lambda: nc.vector.tensor_tensor(out=grow[0:1, bi, hf * 512:(hf + 1) * 512], in0=brw[0:1, :],
                                                               in1=brow[0:1, bi, hf * 512:(hf + 1) * 512], op=ALU.add),
                          [brw_t, s0T], [growT])
            load_wab()
            cx.op(DVE, lambda: nc.vector.tensor_tensor(out=modT[:, :, :], in0=bmod[:, 0:96].rearrange("p (m j) -> p m j", j=2),
                                                       in1=badaT[:, :].unsqueeze(2).to_broadcast([128, 48, 2]), op=ALU.add),
                  [bmod_t, s0T], [modTt])
            cx.op(DVE, lambda: nc.vector.tensor_scalar(out=modT[:, 8:16, :], in0=modT[:, 8:16, :], scalar1=1.0, scalar2=None, op0=ALU.add), [modTt], [modTt])
            cx.op(DVE, lambda: nc.vector.tensor_scalar(out=modT[:, 32:40, :], in0=modT[:, 32:40, :], scalar1=1.0, scalar2=None, op0=ALU.add), [modTt], [modTt])
            dump("modT", modTt, modT[:, :, :].rearrange("p m j -> p (m j)"))
            cx.dma(SP, g_scr[0:1, :], grow[0:1, 0, :], growT, load=False)
            cx.dma(SP, g_scr[1:2, :], grow[0:1, 1, :], growT, load=False)
            wpf = sbuf(s0, "wpf", [128, 4, 256], F32)
            psb = sbuf(s0, "psb", [128, D], F32)
            s1T = Tile("s1")
            cx.dma(SP, wpf[:, :, :], w_pool.rearrange("g c n -> c g n"), s1T, load=True)
            cx.dma(SP, psb[:, :], pool_scale[0:1, :].partition_broadcast(128), s1T, load=True)
            cx.op(DVE, lambda: nc.vector.tensor_tensor(out=Wpl[:, :, :], in0=wpf[:, :, :],
                                                       in1=psb[:, :].rearrange("p (g n) -> p g n", g=4), op=ALU.mult), [s1T], [WplT])
            cx.barrier()
            if level == 0:
                sw.__exit__(None, None, None)
                raise _Stop()

        with ExitStack() as sa:
            def stopat(label):
                if STOPAT == label:
                    cx.barrier()
                    raise _Stop()

            stopat("w")
            NXS = 2
            xsl = [(Tile("xs%d" % i), sbuf(sa, "xs%d" % i, [128, D], F32)) for i in range(NXS)]
            xhl = [(Tile("xh%d" % i), sbuf(sa, "xh%d" % i, [128, D], BF16)) for i in range(2)]
            uTl = [(Tile("uT%d" % i), sbuf(sa, "uT%d" % i, [128, 8, SBW], BF16)) for i in range(3)]
            QTl = [(Tile("QT%d" % i), sbuf(sa, "QT%d" % i, [128, 8, SBW], BF16)) for i in range(2)]
            KTb = sbuf(sa, "KTb", [128, RING, 2, 4, 128], BF16)
            KTt = [Tile("KT%d" % i) for i in range(RING)]
            Vb = sbuf(sa, "Vb", [128, RING, 4, 65], BF16)
            Vt = [Tile("V%d" % i) for i in range(RING)]
            Pb = sbuf(sa, "Pb", [128, RING, 512], BF16)
            Pt = [Tile("P%d" % i) for i in range(RING)]
            KcT = sbuf(sa, "KcT", [128, 2, 4, CTX], BF16)
            KcTt = Tile("KcT")
            Vcb = sbuf(sa, "Vcb", [128, 2, 4, 65], BF16)
            Vct = [Tile("Vc%d" % i) for i in range(2)]
            NPT = 10
            PTb = sbuf(sa, "PTb", [128, NPT, 512], BF16)
            PTr = Ring([(Tile("PT%d" % i), PTb[:, i, :]) for i in range(NPT)])
            Obl = [(Tile("Ob%d" % i), sbuf(sa, "Ob%d" % i, [128, HEADS, HD], BF16)) for i in range(2)]
            OTt, OT = Tile("OT"), sbuf(sa, "OT", [128, 8, SBW], BF16)
            PLt, PL = Tile("PL"), sbuf(sa, "PL", [128, 4, SBW], BF16)
            mTl = [(Tile("mT%d" % i), sbuf(sa, "mT%d" % i, [128, 8, SBW], BF16)) for i in range(2)]
            zbr = Ring([(Tile("zb%d" % i), sbuf(sa, "zb%d" % i, [128, SBW], BF16)) for i in range(3)])
            t1r = Ring([(Tile("t1%d" % i), sbuf(sa, "t1%d" % i, [128, SBW], F32)) for i in range(3)])
            t2r = Ring([(Tile("t2%d" % i), sbuf(sa, "t2%d" % i, [128, SBW], F32)) for i in range(2)])
            rtl = [(Tile("rt%d" % i), sbuf(sa, "rt%d" % i, [128, 2, SBW], F32)) for i in range(3)]
            tgr = Ring([(Tile("tg%d" % i), sbuf(sa, "tg%d" % i, [128, SBW], F32)) for i in range(4)])
            u1r = Ring([(Tile("u1%d" % i), sbuf(sa, "u1%d" % i, [128, SBW], F32)) for i in range(4)])
            denr = Ring([(Tile("den%d" % i), sbuf(sa, "den%d" % i, [128, 8], F32)) for i in range(2)])
            proj = Ring(banks[0:4])
            STr = Ring(banks[4:6])
            Or = Ring(banks[6:8])

            for i in range(RING):
                cx.op(POOL, lambda: nc.gpsimd.memset(Vb[:, i, :, 64:65], 1.0), [], [Vt[i]])
                cx.op(POOL, lambda: nc.gpsimd.memset(KTb[:, i, :, :, :], 0.0), [], [KTt[i]])
            cx.op(POOL, lambda: nc.gpsimd.memset(KcT[:, :, :, :], 0.0), [], [KcTt])
            for i in range(2):
                cx.op(POOL, lambda: nc.gpsimd.memset(Vcb[:, i, :, 64:65], 1.0), [], [Vct[i]])

            def x_rows(e):
                if e == 0:
                    return x_halo[0:128, :]
                if e == NEXT - 1:
                    return x_halo[128:256, :]
                return x_main[(e - 1) * 128:e * 128, :]

            xs_ctr = [0]
            pending_x = {}

            def issue_x(key, src_ap):
                i = xs_ctr[0] % NXS
                xs_ctr[0] += 1
                t, tl = xsl[i]
                cx.dma(SP, tl[:, :], src_ap, t, load=True)
                pending_x[key] = (t, tl)

            xh_ctr = [0]

            def ln_steps(key, uTt, uT, col0, j):
                stt_ = {}

                def f_stats():
                    xt, xtile = pending_x.pop(key)
                    stt_["x"] = (xt, xtile)
                    stt_["ln"] = ln_stats(xt, xtile)

                def f_hat():
                    xt, xtile = stt_["x"]
                    st, rstd, nmr = stt_["ln"]
                    ht, htile = xhl[xh_ctr[0] % 2]
                    xh_ctr[0] += 1
                    stt_["h"] = (ht, htile)
                    cx.op(ACT, lambda: nc.scalar.activation(out=htile[:, :], in_=xtile[:, :], func=AF.Identity, bias=nmr, scale=rstd), [xt, st], [ht])

                def f_tr():
                    ht, htile = stt_["h"]
                    TRt, TRb = proj.next()
                    TRv = TRb[:, :].bitcast(BF16)
                    stt_["tr"] = (TRt, TRv)
                    for c in range(8):
                        cx.op(PE, lambda: nc.tensor.transpose(TRv[:, c * 128:(c + 1) * 128], htile[:, c * 128:(c + 1) * 128], ident), [ht, cstT], [TRt])

                def f_mod():
                    TRt, TRv = stt_["tr"]
                    for c in range(8):
                        cx.op(DVE, lambda: nc.vector.tensor_scalar(out=uT[:, c, col0:col0 + 128], in0=TRv[:, c * 128:(c + 1) * 128],
                                                                   scalar1=modT[:, 8 + c, j:j + 1], scalar2=modT[:, c, j:j + 1],
                                                                   op0=ALU.mult, op1=ALU.add), [TRt, modTt], [uTt])
                def f_trmod():
                    f_tr()
                    f_mod()
                return [f_stats, None, f_hat, None, f_trmod, None, None, None]

            def ln_modulate(key, uTt, uT, col0, j):
                for f in ln_steps(key, uTt, uT, col0, j):
                    if f is not None:
                        f()

            def proj_fm(uTt, uT, n, off):
                pt, pb = proj.next()
                for kc in range(8):
                    cx.op(PE, lambda: nc.tensor.matmul(pb[:, 0:n], lhsT=Win[:, kc, off:off + 128], rhs=uT[:, kc, 0:n],
                                                       start=(kc == 0), stop=(kc == 7)), [WinT, uTt], [pt])
                return pt, pb

            def rope_front(uTt, uT, n, off, rtt, rt):
                pt, pb = proj_fm(uTt, uT, n, off)
                zt, zb = zbr.next()
                cx.op(ACT, lambda: nc.scalar.copy(out=zb[:, 0:n], in_=pb[:, 0:n]), [pt], [zt])
                t1t, t1 = t1r.next()
                cx.op(DVE, lambda: nc.vector.tensor_tensor(out=t1[:, 0:n], in0=pb[:, 0:n], in1=rt[:, 0, 0:n], op=ALU.mult), [pt, rtt], [t1t])
                return (zt, zb, t1t, t1)

            def rope_back(st_, n, rtt, rt, dsts):
                zt, zb, t1t, t1 = st_
                p2t, p2b = proj.next()
                cx.op(PE, lambda: nc.tensor.matmul(p2b[:, 0:n], lhsT=perm, rhs=zb[:, 0:n], start=True, stop=True), [zt, cstT], [p2t])
                t2t, t2 = t2r.next()
                cx.op(DVE, lambda: nc.vector.tensor_tensor(out=t2[:, 0:n], in0=p2b[:, 0:n], in1=rt[:, 1, 0:n], op=ALU.mult), [p2t, rtt], [t2t])
                for (dt_, dap, c0, w, p0, p1) in dsts:
                    cx.op(POOL, lambda: nc.gpsimd.tensor_tensor(out=dap, in0=t1[p0:p1, c0:c0 + w], in1=t2[p0:p1, c0:c0 + w], op=ALU.add), [t1t, t2t], [dt_])

            def rope_many(uTt, uT, n, rtt, rt, jobs):
                prev = None
                for (off, dsts) in jobs:
                    cur = (rope_front(uTt, uT, n, off, rtt, rt), dsts)
                    if prev is not None:
                        rope_back(prev[0], n, rtt, rt, prev[1])
                    prev = cur
                if prev is not None:
                    rope_back(prev[0], n, rtt, rt, prev[1])

            def v_block(uTt, uT, col0, vt, vap):
                pt, pb = proj.next()
                for kc in range(8):
                    cx.op(PE, lambda: nc.tensor.matmul(pb[:, 0:256], lhsT=uT[:, kc, col0:col0 + 128], rhs=Win[:, kc, V_OFF:V_OFF + 256],
                                                       start=(kc == 0), stop=(kc == 7)), [WinT, uTt], [pt])
                cx.op(ACT, lambda: nc.scalar.copy(out=vap, in_=pb[:, 0:256].rearrange("p (g d) -> p g d", g=4)), [pt], [vt])

            def p_block(uTt, uT, col0, ptile, pap):
                pt, pb = proj.next()
                for kc in range(8):
                    cx.op(PE, lambda: nc.tensor.matmul(pb[:, 0:512], lhsT=uT[:, kc, col0:col0 + 128], rhs=Win[:, kc, P_OFF:P_OFF + 512],
                                                       start=(kc == 0), stop=(kc == 7)), [WinT, uTt], [pt])
                cx.op(DVE, lambda: nc.vector.tensor_copy(out=pap, in_=pb[:, 0:512]), [pt], [ptile])

            uct, uc = uTl[1]
            for i in range(2):
                issue_x(("c", i), ctx_in[i * 128:(i + 1) * 128, :])
            stopat("ms")
            def run_zip(lists):
                k = 0
                while any(k < len(l_) for l_ in lists):
                    for l_ in lists:
                        if k < len(l_) and l_[k] is not None:
                            l_[k]()
                    k += 1

            run_zip([ln_steps(("c", i), uct, uc, i * 128, 1) for i in range(2)])
            stopat("ctxln")

            def ctx_jobs():
                jobs = []
                for g in range(4):
                    def fk(g=g):
                        pt, pb = proj_fm(uct, uc, CTX, K_OFF + g * 128)
                        cx.op(ACT, lambda: nc.scalar.copy(out=KcT[0:64, 0, g, :], in_=pb[0:64, 0:CTX]), [pt], [KcTt])
                        cx.op(ACT, lambda: nc.scalar.copy(out=KcT[64:128, 1, g, :], in_=pb[64:128, 0:CTX]), [pt], [KcTt])
                    jobs.append(fk)
                for i in range(2):
                    jobs.append(lambda i=i: v_block(uct, uc, i * 128, Vct[i], Vcb[:, i, :, 0:64]))
                return jobs

            def sb_blocks(s):
                if s == -1:
                    return [0]
                if s == NSB:
                    return [NEXT - 1]
                return [1 + NB * s + i for i in range(NB)]

            def issue_loads(s):
                bl = sb_blocks(s)
                for e in bl:
                    issue_x(("x", e), x_rows(e))
                rtt, rt = rtl[(s + 3) % 3]
                n = len(bl) * 128
                cx.dma(SP, rt[:, :, 0:n], rope_tab[:, :, bl[0] * 128:bl[0] * 128 + n], rtt, load=True)

            def lnA_steps(s):
                bl = sb_blocks(s)
                uTt, uT = uTl[(s + 3) % 3]
                steps = [lambda: issue_loads(s)]
                for i, e in enumerate(bl):
                    steps += ln_steps(("x", e), uTt, uT, i * 128, 0)
                return steps

            def projA_jobs(s):
                bl = sb_blocks(s)
                n = len(bl) * 128
                main = 0 <= s < NSB
                uTt, uT = uTl[(s + 3) % 3]
                rtt, rt = rtl[(s + 3) % 3]
                hold = {"prev": None}
                jobs = []

                def rope_job(off, dsts):
                    def f():
                        cur = (rope_front(uTt, uT, n, off, rtt, rt), dsts)
                        if hold["prev"] is not None:
                            rope_back(hold["prev"][0], n, rtt, rt, hold["prev"][1])
                        hold["prev"] = cur
                    return f

                def rope_flush():
                    if hold["prev"] is not None:
                        rope_back(hold["prev"][0], n, rtt, rt, hold["prev"][1])
                        hold["prev"] = None

                for g in range(4):
                    dsts = []
                    for i, e in enumerate(bl):
                        dsts.append((KTt[e % RING], KTb[0:64, e % RING, 0, g, :], i * 128, 128, 0, 64))
                        dsts.append((KTt[e % RING], KTb[64:128, e % RING, 1, g, :], i * 128, 128, 64, 128))
                    jobs.append(rope_job(K_OFF + g * 128, dsts))
                jobs.append(rope_flush)
                for i, e in enumerate(bl):
                    jobs.append(lambda i=i, e=e: v_block(uTt, uT, i * 128, Vt[e % RING], Vb[:, e % RING, :, 0:64]))
                for i, e in enumerate(bl):
                    jobs.append(lambda i=i, e=e: p_block(uTt, uT, i * 128, Pt[e % RING], Pb[:, e % RING, :]))
                if main:
                    qt, q = QTl[s % 2]
                    for c in range(8):
                        jobs.append(rope_job(Q_OFF + c * 128, [(qt, q[:, c, 0:n], 0, n, 0, 128)]))
                    jobs.append(rope_flush)
                return jobs

            def projA(s):
                for f in projA_jobs(s):
                    f()

            def pooling(s):
                for i, e in enumerate(sb_blocks(s)):
                    j = e - 1
                    var = 0 if j == 0 else (2 if j == NBLK - 1 else 1)
                    pt, pb = proj.next()
                    for g in range(4):
                        for rel in range(3):
                            ee = e - 1 + rel
                            cx.op(PE, lambda: nc.tensor.matmul(pb[:, g * 128:(g + 1) * 128], lhsT=Pb[:, ee % RING, g * 128:(g + 1) * 128],
                                                               rhs=amat_ap(var, rel, g), start=(rel == 0), stop=(rel == 2)),
                                  [Pt[ee % RING], cstT], [pt])
                    cx.op(ACT, lambda: nc.scalar.copy(out=PL[:, :, i * 128:(i + 1) * 128], in_=pb[:, 0:512].rearrange("p (g t) -> p g t", g=4)), [pt], [PLt])

            def att_S(s, i, g, between=None):
                qt, q = QTl[s % 2]
                e = sb_blocks(s)[i]
                j = e - 1
                c0 = i * 128
                srcs = [
                    (KTt[(e - 1) % RING], KTb[:, (e - 1) % RING, :, g, :], Vt[(e - 1) % RING], Vb[:, (e - 1) % RING, g, :], mask_ap(0 if j == 0 else 1)),
                    (KTt[e % RING], KTb[:, e % RING, :, g, :], Vt[e % RING], Vb[:, e % RING, g, :], None),
                    (KTt[(e + 1) % RING], KTb[:, (e + 1) % RING, :, g, :], Vt[(e + 1) % RING], Vb[:, (e + 1) % RING, g, :], mask_ap(3 if j == NBLK - 1 else 2)),
                    (KcTt, KcT[:, :, g, 0:128], Vct[0], Vcb[:, 0, g, :], None),
                    (KcTt, KcT[:, :, g, 128:256], Vct[1], Vcb[:, 1, g, :], None),
                ]
                pts = []
                for kbi, (kt, kap, vt, vap, mk) in enumerate(srcs):
                    if between is not None and kbi in (2, 4):
                        between()
                    stt, stb = STr.next()
                    cx.op(PE, lambda: nc.tensor.matmul(stb[:, 0:256], lhsT=kap[:, 0, :], rhs=q[:, 2 * g:2 * g + 2, c0:c0 + 128],
                                                       start=True, stop=True, skip_group_check=True), [kt, qt], [stt])
                    cx.op(PE, lambda: nc.tensor.matmul(stb[:, 256:512], lhsT=kap[:, 1, :], rhs=q[:, 2 * g:2 * g + 2, c0:c0 + 128],
                                                       start=True, stop=True, skip_group_check=True), [kt, qt], [stt])
                    ptt, ptb = PTr.next()
                    cx.op(ACT, lambda: nc.scalar.activation(out=ptb, in_=stb[:, 0:512], func=AF.Exp, scale=0.125), [stt], [ptt])
                    if mk is not None:
                        cx.op(POOL, lambda: nc.gpsimd.tensor_tensor(out=ptb.rearrange("p (c q) -> p c q", c=4), in0=ptb.rearrange("p (c q) -> p c q", c=4),
                                                                    in1=mk.unsqueeze(1).to_broadcast([128, 4, 128]), op=ALU.mult), [ptt, cstT], [ptt])
                    pts.append((ptt, ptb, vt, vap))
                return pts

            def att_PV(s, i, g, pts):
                obt, ob = Obl[i % 2]
                ot, obk = Or.next()
                ov = obk[:, 0:260].rearrange("p (c d) -> p c d", d=65)
                for cb in range(4):
                    for kb, (ptt, ptb, vt, vap) in enumerate(pts):
                        cx.op(PE, lambda: nc.tensor.matmul(ov[:, cb, :], lhsT=ptb[:, cb * 128:(cb + 1) * 128], rhs=vap,
                                                           start=(kb == 0), stop=(kb == 4)), [ptt, vt], [ot])
                dt_, den = denr.next()
                cx.op(DVE, lambda: nc.vector.tensor_tensor(out=den[:, 0:4], in0=ov[:, :, 64], in1=expsink[:, 4 * g:4 * g + 4], op=ALU.add), [ot, miscT], [dt_])
                cx.op(DVE, lambda: nc.vector.reciprocal(out=den[:, 4:8], in_=den[:, 0:4]), [dt_], [dt_])
                cx.op(DVE, lambda: nc.vector.tensor_tensor(out=ob[:, 4 * g:4 * g + 4, :], in0=ov[:, :, 0:64],
                                                           in1=den[:, 4:8].unsqueeze(2).to_broadcast([128, 4, 64]), op=ALU.mult), [ot, dt_], [obt])

            def att_T(s, i):
                obt, ob = Obl[i % 2]
                c0 = i * 128
                obf = ob[:, :, :].rearrange("p h d -> p (h d)")
                TRt, TRb = proj.next()
                TRv = TRb[:, :].bitcast(BF16)
                for c in range(8):
                    cx.op(PE, lambda: nc.tensor.transpose(TRv[:, c * 128:(c + 1) * 128], obf[:, c * 128:(c + 1) * 128], ident), [obt, cstT], [TRt])
                cx.op(ACT, lambda: nc.scalar.copy(out=OT[:, :, c0:c0 + 128], in_=TRv[:, :].rearrange("p (c t) -> p c t", c=8)), [TRt], [OTt])

            def attention(s, pend, fill):
                units = [(i, g) for i in range(NB) for g in range(4)]
                prev = None
                nfill = len(fill)
                def one_fill():
                    if fill:
                        fill.pop(0)()

                pend_T = None
                for ui, (i, g) in enumerate(units):
                    if ui == 4:
                        while len(fill) > nfill - 7 and fill:
                            fill.pop(0)()
                    pts = att_S(s, i, g, between=one_fill)
                    if pend_T is not None:
                        pend_T[1] -= 1
                        if pend_T[1] == 0:
                            att_T(s, pend_T[0])
                            pend_T = None
                    if prev is not None:
                        pi, pg, ppts = prev
                        att_PV(s, pi, pg, ppts)
                        if pg == 3:
                            pend_T = [pi, 2]
                    prev = (i, g, pts)
                    if ui < 4:
                        one_fill()
                    run_steps_a(pend, 2)
                pi, pg, ppts = prev
                att_PV(s, pi, pg, ppts)
                for _ in range(3):
                    one_fill()
                att_T(s, pi)

            def run_steps_a(q_, n_):
                for _ in range(n_):
                    if q_:
                        f = q_.pop(0)
                        if f is not None:
                            f()

            mt_pending = []

            def merge(s):
                uTt, uT = uTl[(s + 3) % 3]
                mt, mT = mTl[s % 2]
                for m in range(8):
                    tgs = []
                    for a, off in enumerate((GA_OFF, GP_OFF)):
                        tgt, tg = tgr.next()
                        pt, pb = proj_fm(uTt, uT, SBW, off + m * 128)
                        cx.op(ACT, lambda: nc.scalar.activation(out=tg[:, :], in_=pb[:, 0:SBW], func=AF.Tanh, scale=0.5), [pt], [tgt])
                        tgs.append((tgt, tg))
                    u1t, u1 = u1r.next()
                    u2t_, u2_ = u1r.next()
                    pt, pb = proj.next()
                    for c in range(8):
                        cx.op(PE, lambda: nc.tensor.matmul(pb[:, 0:SBW], lhsT=Wab[:, c, m * 128:(m + 1) * 128], rhs=OT[:, c, :],
                                                           start=(c == 0), stop=(c == 7)), [WabT, OTt], [pt])
                    cx.op(DVE, lambda: nc.vector.scalar_tensor_tensor(out=u1[:, :], in0=tgs[0][1][:, :], scalar=1.0, in1=pb[:, 0:SBW],
                                                                      op0=ALU.add, op1=ALU.mult), [tgs[0][0], pt], [u1t])
                    pt, pb = proj.next()
                    g = m // 2
                    cx.op(PE, lambda: nc.tensor.matmul(pb[:, 0:SBW], lhsT=Wpl[:, g, (m % 2) * 128:(m % 2) * 128 + 128], rhs=PL[:, g, :],
                                                       start=True, stop=True), [WplT, PLt], [pt])
                    cx.op(DVE, lambda: nc.vector.scalar_tensor_tensor(out=u2_[:, :], in0=tgs[1][1][:, :], scalar=1.0, in1=pb[:, 0:SBW],
                                                                      op0=ALU.add, op1=ALU.mult), [tgs[1][0], pt], [u2t_])
                    cx.op(POOL, lambda: nc.gpsimd.tensor_tensor(out=mT[:, m, :], in0=u1[:, :], in1=u2_[:, :], op=ALU.add), [u1t, u2t_], [mt])
                mt_pending.append(lambda: cx.dma(SP, mt_scr[s], mT[:, :, :].rearrange("p c t -> p (c t)"), mt, load=False))

            stopat("ctx")
            def jobs_with(jobs, pend_, k_):
                for jb in jobs:
                    jb()
                    run_steps_a(pend_, k_)

            pend0 = lnA_steps(-1) + lnA_steps(0)
            jobs_with(ctx_jobs(), pend0, 3)
            run_steps_a(pend0, len(pend0))
            pend1 = lnA_steps(1)
            jobs_with(projA_jobs(-1), pend1, 1)
            jobs_with(projA_jobs(0), pend1, 1)
            run_steps_a(pend1, len(pend1))
            if DEBUG:
                dump("uT0", uTl[0][0], uTl[0][1][:, :, :].rearrange("p c t -> p (c t)"))
                dump("QT0", QTl[0][0], QTl[0][1][:, :, :].rearrange("p c t -> p (c t)"))
                dump("KT1", KTt[1], KTb[:, 1, 0, :, :].rearrange("p g t -> p (g t)"))
                dump("V1", Vt[1], Vb[:, 1, :, :].rearrange("p g d -> p (g d)"))
            if level == 1 and nsb_run == 0:
                cx.barrier()
                raise _Stop()
            for s in range(nsb_run):
                pend = lnA_steps(s + 2) if s + 2 <= NSB else []
                fill = projA_jobs(s + 1)
                while mt_pending:
                    mt_pending.pop(0)()
                attention(s, pend, fill)
                while fill:
                    fill.pop(0)()
                pooling(s)
                merge(s)
                run_steps_a(pend, len(pend))
                if DEBUG and s == 0:
                    dump("OT0", OTt, OT[:, :, :].rearrange("p c t -> p (c t)"))
                    dump("PL0", PLt, PL[:, :, :].rearrange("p g t -> p (g t)"))
                    dump("mt", mTl[0][0], mTl[0][1][:, :, :].rearrange("p c t -> p (c t)"))
            while mt_pending:
                mt_pending.pop(0)()
            cx.barrier()
            if level == 1:
                raise _Stop()

        sw.__exit__(None, None, None)

        with ExitStack() as sbc:
            W1 = sbuf(sbc, "W1", [128, 8, DFF], BF16)
            W1T = Tile("W1")
            W2 = sbuf(sbc, "W2", [128, 32, D], BF16)
            W2T = Tile("W2")
            lng = sbuf(sbc, "lng", [128, D], F32)
            lnb = sbuf(sbc, "lnb", [128, D], F32)
            lnT = Tile("lnT")

            def bcast_row(ri, dst, dstT, scale):
                cx.dma(SP, dst[:, :], g_scr[ri:ri + 1, :].partition_broadcast(128), dstT, load=True)
                if scale != 1.0:
                    cx.op(ACT, lambda: nc.scalar.mul(out=dst[:, :], in_=dst[:, :], mul=scale), [dstT], [dstT])

            with ExitStack() as sb_:
                Wout = sbuf(sb_, "Wout", [128, 8, D], BF16)
                WoutTk = [Tile("Wout%d" % i) for i in range(8)]
                g1b = sbuf(sb_, "g1b", [128, D], F32)
                g1bT = Tile("g1b")
                stg = [(Tile("stg%d" % i), sbuf(sb_, "stg%d" % i, [128, D], F32)) for i in range(2)]
                mll = [(Tile("ml%d" % i), sbuf(sb_, "ml%d" % i, [128, 8, SBW], BF16)) for i in range(2)]
                NXB = 5
                xbl = [(Tile("xb%d" % i), sbuf(sb_, "xb%d" % i, [128, D], F32)) for i in range(NXB)]
                bcast_row(0, g1b, g1bT, 0.5)
                cx.dma(SP, lng[:, :], ln1_g[0:1, :].partition_broadcast(128), lnT, load=True)
                cx.dma(SP, lnb[:, :], ln1_b[0:1, :].partition_broadcast(128), lnT, load=True)
                w_out_v = w_out.rearrange("(kc p) n -> p kc n", p=128)
                for kc in range(8):
                    st_t, st_ = stg[kc % 2]
                    cx.dma(SP, st_[:, :], w_out_v[:, kc, :], st_t, load=True)
                    if kc % 2 == 0:
                        cx.op(DVE, lambda: nc.vector.tensor_tensor(out=Wout[:, kc, :], in0=st_[:, :], in1=g1b[:, :], op=ALU.mult), [st_t, g1bT], [WoutTk[kc]])
                    else:
                        cx.op(POOL, lambda: nc.gpsimd.tensor_tensor(out=Wout[:, kc, :], in0=st_[:, :], in1=g1b[:, :], op=ALU.mult), [st_t, g1bT], [WoutTk[kc]])
                w1_v = w_mlp_in.rearrange("(kc p) n -> p kc n", p=128)
                w2_v = w_mlp_out.rearrange("(kc p) n -> p kc n", p=128)
                w2_jobs = list(range(32))
                w1_jobs = [(kc, q) for kc in range(8) for q in range(4)]
                stg_ctr = [8]

                def w_job():
                    for _ in range(1):
                        if w1_jobs:
                            kc, q = w1_jobs.pop(0)
                            st_t, st_ = stg[stg_ctr[0] % 2]
                            stg_ctr[0] += 1
                            cx.dma(SP, st_[:, :], w1_v[:, kc, q * 1024:(q + 1) * 1024], st_t, load=True)
                            cx.op(ACT, lambda: nc.scalar.copy(out=W1[:, kc, q * 1024:(q + 1) * 1024], in_=st_[:, :]), [st_t], [W1T])
                        elif w2_jobs:
                            c = w2_jobs.pop(0)
                            st_t, st_ = stg[stg_ctr[0] % 2]
                            stg_ctr[0] += 1
                            cx.dma(SP, st_[:, :], w2_v[:, c, :], st_t, load=True)
                            cx.op(ACT, lambda: nc.scalar.copy(out=W2[:, c, :], in_=st_[:, :]), [st_t], [W2T])

                projB = Ring(banks[0:8])
                xb_ctr = [0]

                def blockB_steps(s_, i, mlt, ml):
                    r0 = (s_ * NB + i) * 128
                    xt, xb = xbl[xb_ctr[0] % NXB]
                    xb_ctr[0] += 1
                    st = {}

                    def head():
                        cx.dma(SP, xb[:, :], x_main[r0:r0 + 128, :], xt, load=True)
                        for hf in range(2):
                            pt, pb = projB.next()
                            for kc in range(8):
                                cx.op(PE, lambda: nc.tensor.matmul(pb[:, 0:512], lhsT=ml[:, kc, i * 128:(i + 1) * 128], rhs=Wout[:, kc, hf * 512:(hf + 1) * 512],
                                                                   start=(kc == 0), stop=(kc == 7)), [mlt, WoutTk[kc]], [pt])
                            cx.op(DVE, lambda: nc.vector.scalar_tensor_tensor(out=xb[:, hf * 512:(hf + 1) * 512], in0=xb[:, hf * 512:(hf + 1) * 512], scalar=ALPHA,
                                                                              in1=pb[:, 0:512], op0=ALU.mult, op1=ALU.add), [xt, pt], [xt])

                    def s_stats():
                        st["ln"] = ln_stats(xt, xb)

                    def s_norm():
                        t_, rstd, nmr = st["ln"]
                        cx.op(ACT, lambda: nc.scalar.activation(out=xb[:, :], in_=xb[:, :], func=AF.Identity, bias=nmr, scale=rstd), [xt, t_], [xt])

                    def s_mul():
                        cx.op(DVE, lambda: nc.vector.tensor_tensor(out=xb[:, 0:512], in0=xb[:, 0:512], in1=lng[:, 0:512], op=ALU.mult), [xt, lnT], [xt])
                        cx.op(POOL, lambda: nc.gpsimd.tensor_tensor(out=xb[:, 512:1024], in0=xb[:, 512:1024], in1=lng[:, 512:1024], op=ALU.mult), [xt, lnT], [xt])

                    def s_add():
                        cx.op(POOL, lambda: nc.gpsimd.tensor_tensor(out=xb[:, :], in0=xb[:, :], in1=lnb[:, :], op=ALU.add), [xt, lnT], [xt])

                    def s_store():
                        cx.dma(SP, x1_scr[r0:r0 + 128, :], xb[:, :], xt, load=False)
                        if DEBUG and s_ == 0:
                            cx.dma(SP, dbg["x1"][i * 128:(i + 1) * 128, :], xb[:, :], xt, load=False)

                    return head, [s_stats, None, None, s_norm, None, s_mul, None, s_add, None, None, None, s_store]

                active = []
                for s in range(NSB):
                    mlt, ml = mll[s % 2]
                    cx.dma(SP, ml[:, :, :].rearrange("p c t -> p (c t)"), mt_scr[s], mlt, load=True)
                    for i in range(NB):
                        head, steps = blockB_steps(s, i, mlt, ml)
                        head()
                        for st_ in active:
                            for _ in range(3):
                                if st_:
                                    f_ = st_.pop(0)
                                    if f_ is not None:
                                        f_()
                        active = [a for a in active if a]
                        active.append(steps)
                        w_job()
                        w_job()
                while active:
                    for st_ in active:
                        if st_:
                            f_ = st_.pop(0)
                            if f_ is not None:
                                f_()
                    active = [a for a in active if a]
                while w2_jobs or w1_jobs:
                    w_job()
                cx.barrier()
                if level == 2:
                    raise _Stop()

            with ExitStack() as sc:
                cx.dma(SP, lng[:, :], ln2_g[0:1, :].partition_broadcast(128), lnT, load=True)
                cx.dma(SP, lnb[:, :], ln2_b[0:1, :].partition_broadcast(128), lnT, load=True)
                g2b = sbuf(sc, "g2b", [128, D], F32)
                g2bT = Tile("g2b")
                bcast_row(1, g2b, g2bT, 1.0)
                zr = Ring([(Tile("zt%d" % i), sbuf(sc, "zt%d" % i, [128, 512], F32)) for i in range(4)])
                NXC = 6
                xcl = [(Tile("xc%d" % i), sbuf(sc, "xc%d" % i, [128, D], F32)) for i in range(NXC)]
                xhc = [(Tile("xhc%d" % i), sbuf(sc, "xhc%d" % i, [128, D], BF16)) for i in range(2)]
                u2l = [(Tile("u2%d" % i), sbuf(sc, "u2%d" % i, [128, 8, SBW], BF16)) for i in range(2)]
                rlr = Ring([(Tile("rl%d" % i), sbuf(sc, "rl%d" % i, [128, SBW], F32)) for i in range(4)])
                hr = Ring([(Tile("h%d" % i), sbuf(sc, "h%d" % i, [128, SBW], BF16)) for i in range(6)])
                projC = Ring(banks[0:4])
                acc = banks[4:8]
                xc_ctr = [0]
                loaded = {}

                def load_sb(s):
                    for i in range(NB):
                        r0 = (s * NB + i) * 128
                        xt, xc = xcl[xc_ctr[0] % NXC]
                        xc_ctr[0] += 1
                        cx.dma(SP, xc[:, :], x1_scr[r0:r0 + 128, :], xt, load=True)
                        loaded[(s, i)] = (xt, xc)

                load_sb(0)
                xh_c = [0]
                from collections import deque
                DSK = 3

                def interleave(lists):
                    out_ = []
                    k = 0
                    while any(k < len(l_) for l_ in lists):
                        for l_ in lists:
                            if k < len(l_):
                                out_.append(l_[k])
                        k += 1
                    return out_

                def lnmod_steps(s_):
                    u2t, u2 = u2l[s_ % 2]
                    steps = []
                    for i in range(NB):
                        xt, xc = loaded[(s_, i)]
                        stt_ = {}

                        def f_stats(xt=xt, xc=xc, stt_=stt_):
                            stt_["ln"] = ln_stats(xt, xc)

                        def f_hat(xt=xt, xc=xc, stt_=stt_):
                            t_, rstd, nmr = stt_["ln"]
                            ht, htile = xhc[xh_c[0] % 2]
                            xh_c[0] += 1
                            stt_["h"] = (ht, htile)
                            cx.op(ACT, lambda: nc.scalar.activation(out=htile[:, :], in_=xc[:, :], func=AF.Identity, bias=nmr, scale=rstd), [xt, t_], [ht])

                        def f_tr(stt_=stt_):
                            ht, htile = stt_["h"]
                            TRt, TRb = projC.next()
                            TRv = TRb[:, :].bitcast(BF16)
                            stt_["tr"] = (TRt, TRv)
                            for c in range(8):
                                cx.op(PE, lambda: nc.tensor.transpose(TRv[:, c * 128:(c + 1) * 128], htile[:, c * 128:(c + 1) * 128], ident), [ht, cstT], [TRt])

                        def f_mod(i=i, stt_=stt_):
                            TRt, TRv = stt_["tr"]
                            for c in range(8):
                                cx.op(DVE, lambda: nc.vector.tensor_scalar(out=u2[:, c, i * 128:(i + 1) * 128], in0=TRv[:, c * 128:(c + 1) * 128],
                                                                           scalar1=modT[:, 32 + c, 0:1], scalar2=modT[:, 24 + c, 0:1],
                                                                           op0=ALU.mult, op1=ALU.add), [TRt, modTt], [u2t])
                        steps.append((f_stats, f_hat, f_tr, f_mod))
                    slots = [None] * (14 * len(steps) + 8)
                    for b_, (a_, h_, t_, m_) in enumerate(steps):
                        o_ = 14 * b_
                        slots[o_], slots[o_ + 11], slots[o_ + 15], slots[o_ + 17] = a_, h_, t_, m_
                    return slots

                def fin_steps(s_):
                    z_steps, steps_all = [], []
                    for i in range(NB):
                        steps = []
                        xt, xc = loaded.pop((s_, i))
                        r0 = (s_ * NB + i) * 128
                        stt_ = {}
                        for hf in range(2):
                            zst = {}

                            def f_z(i=i, hf=hf, zst=zst):
                                at, ab = acc[i * 2 + hf]
                                zt_, zz = zr.next()
                                zst["z"] = (zt_, zz)
                                cx.op(DVE, lambda: nc.vector.tensor_tensor(out=zz[:, :], in0=ab[:, 0:512], in1=g2b[:, hf * 512:(hf + 1) * 512], op=ALU.mult), [at, g2bT], [zt_])

                            def f_z2(hf=hf, xt=xt, xc=xc, zst=zst):
                                zt_, zz = zst["z"]
                                cx.op(DVE, lambda: nc.vector.scalar_tensor_tensor(out=xc[:, hf * 512:(hf + 1) * 512], in0=xc[:, hf * 512:(hf + 1) * 512], scalar=ALPHA,
                                                                                  in1=zz[:, :], op0=ALU.mult, op1=ALU.add), [xt, zt_], [xt])
                            z_steps.append(f_z)
                            steps.append(f_z2)
                            steps.append(None)

                        def f_stats(xt=xt, xc=xc, stt_=stt_):
                            stt_["ln"] = ln_stats(xt, xc)

                        def f_norm(xt=xt, xc=xc, stt_=stt_):
                            t_, rstd, nmr = stt_["ln"]
                            cx.op(ACT, lambda: nc.scalar.activation(out=xc[:, :], in_=xc[:, :], func=AF.Identity, bias=nmr, scale=rstd), [xt, t_], [xt])

                        def f_mul(xt=xt, xc=xc):
                            cx.op(DVE, lambda: nc.vector.tensor_tensor(out=xc[:, 0:512], in0=xc[:, 0:512], in1=lng[:, 0:512], op=ALU.mult), [xt, lnT], [xt])
                            cx.op(POOL, lambda: nc.gpsimd.tensor_tensor(out=xc[:, 512:1024], in0=xc[:, 512:1024], in1=lng[:, 512:1024], op=ALU.mult), [xt, lnT], [xt])

                        def f_add(xt=xt, xc=xc):
                            cx.op(POOL, lambda: nc.gpsimd.tensor_tensor(out=xc[:, :], in0=xc[:, :], in1=lnb[:, :], op=ALU.add), [xt, lnT], [xt])

                        def f_store(xt=xt, xc=xc, r0=r0):
                            cx.dma(SP, out[r0:r0 + 128, :], xc[:, :], xt, load=False)
                        steps += [f_stats] + [None] * 5 + [f_norm] + [None] * 2 + [f_mul] + [None] * 2 + [f_add] + [None] * 4 + [f_store]
                        steps_all.append(steps)
                    return z_steps, interleave(steps_all)

                def run_steps(q, n):
                    for _ in range(n):
                        if q:
                            f = q.popleft()
                            if f is not None:
                                f()

                q_pre = deque(lnmod_steps(0))
                run_steps(q_pre, len(q_pre))
                pend = deque()
                for s in range(NSB):
                    if s + 1 < NSB:
                        load_sb(s + 1)
                    u2t, u2 = u2l[s % 2]
                    if s + 1 < NSB:
                        pend.extend(lnmod_steps(s + 1))
                    hs = {}
                    for c in range(32 + DSK):
                        if c < 32:
                            pt, pb = projC.next()
                            for kc in range(8):
                                cx.op(PE, lambda: nc.tensor.matmul(pb[:, 0:SBW], lhsT=W1[:, kc, c * 128:(c + 1) * 128], rhs=u2[:, kc, :],
                                                                   start=(kc == 0), stop=(kc == 7)), [W1T, u2t], [pt])
                            rt_, rl = rlr.next()
                            cx.op(ACT, lambda: nc.scalar.activation(out=rl[:, :], in_=pb[:, 0:SBW], func=AF.Relu), [pt], [rt_])
                            ht_, hh = hr.next()
                            cx.op(DVE, lambda: nc.vector.tensor_tensor(out=hh[:, :], in0=pb[:, 0:SBW], in1=rl[:, :], op=ALU.mult), [pt, rt_], [ht_])
                            hs[c] = (ht_, hh)
                        cc = c - DSK
                        if cc >= 0:
                            ht_, hh = hs.pop(cc)
                            for i in range(NB):
                                for hf in range(2):
                                    at, ab = acc[i * 2 + hf]
                                    cx.op(PE, lambda: nc.tensor.matmul(ab[:, 0:512], lhsT=hh[:, i * 128:(i + 1) * 128], rhs=W2[:, cc, hf * 512:(hf + 1) * 512],
                                                                       start=(cc == 0), stop=(cc == 31)), [ht_, W2T], [at])
                        run_steps(pend, 3)
                    z_steps, f_steps = fin_steps(s)
                    if s + 1 < NSB:
                        for f in z_steps:
                            f()
                        rest = deque(f_steps)
                        run_steps(pend, len(pend))
                        pend = rest
                    else:
                        for f in z_steps:
                            f()
                        run_steps(pend, len(pend))
                        pend = deque(f_steps)
                        run_steps(pend, len(pend))
                cx.barrier([SP])


def _const_pack(core):
    half = core % 2
    cst = np.zeros((128, C_COLS), np.float32)
    cst[:, C_ID:C_ID + 128] = np.eye(128, dtype=np.float32)
    cst[:, C_ID4:C_ID4 + 512] = np.tile(np.eye(128, dtype=np.float32), (1, 4))
    qi = np.arange(128)[:, None]
    ki = np.arange(128)[None, :]
    kk = np.arange(128)[:, None]
    qq = np.arange(128)[None, :]
    prev_mid = np.where(kk >= qq, 1.0, 0.0).astype(np.float32)
    next_mid = np.where(kk <= qq, 1.0, 0.0).astype(np.float32)
    allneg = np.zeros((128, 128), np.float32)
    masks = [allneg if half == 0 else prev_mid, prev_mid, next_mid, allneg if half == 1 else next_mid]
    for i, m in enumerate(masks):
        cst[:, C_MASK + i * 128:C_MASK + (i + 1) * 128] = m
    pm = np.zeros((128, 128), np.float32)
    for m_ in range(128):
        d = m_ % 64
        partner = d + 16 if (d % 32) < 16 else d - 16
        pm[(m_ // 64) * 64 + partner, m_] = 1.0
    cst[:, C_PERM:C_PERM + 128] = pm
    start = half * TOK
    for var in range(3):
        j = 0 if var == 0 else (NBLK - 1 if var == 2 else 5)
        base = start + j * 128
        for g, w in enumerate((2, 4, 8, 16)):
            T = base + np.arange(128)
            lo = np.clip(T - w // 2, 0, SEQ)
            hi = np.clip(T + w // 2, 0, SEQ)
            cnt = (hi - lo).astype(np.float32)
            for rel in range(3):
                Tp = base + (rel - 1) * 128 + np.arange(128)
                inwin = (Tp[:, None] >= lo[None, :]) & (Tp[:, None] < hi[None, :])
                A = np.where(inwin, 1.0 / cnt[None, :], 0.0).astype(np.float32)
                if rel == 1:
                    A = A - np.eye(128, dtype=np.float32)
                o = C_AMAT + ((var * 3 + rel) * 4 + g) * 128
                cst[:, o:o + 128] = A
    return cst


def _rope_tab(core):
    half = core % 2
    t = half * TOK - 128 + np.arange(NEXT * 128)
    t = np.clip(t, 0, SEQ - 1)
    rows = (t // 64).astype(np.float64)
    cols = (t % 64).astype(np.float64)
    inv = 1.0 / (10000.0 ** (np.arange(16, dtype=np.float64) / 16.0))
    tab = np.zeros((128, 2, NEXT * 128), np.float32)
    for p in range(128):
        d = p % 64
        pos = rows if d < 32 else cols
        i = d % 16
        sign = -1.0 if (d % 32) < 16 else 1.0
        ang = pos * inv[i]
        tab[p, 0] = np.cos(ang).astype(np.float32)
        tab[p, 1] = (sign * np.sin(ang)).astype(np.float32)
    return tab


_NC_CACHE = {}


def make_in_maps(x, c, ctx, c_ctx, w_ada, b_ada, w_in, w_attn_branch, w_pool, pool_scale, attn_sink, w_out,
                 ln1_g, ln1_b, w_mlp_in, w_mlp_out, ln2_g, ln2_b, cores=None):
    f = lambda a: np.ascontiguousarray(np.asarray(a, dtype=np.float32))
    x, c, ctx, c_ctx = f(x), f(c), f(ctx), f(c_ctx)
    w_ada, b_ada, w_in = f(w_ada)[0], f(b_ada), f(w_in)[0]
    cols = []
    cols.append(w_in[:, 2048:4096])
    for cc in range(8):
        g, r = cc // 2, cc % 2
        h0, h1 = 4 * g + r, 4 * g + 2 + r
        cols.append(w_in[:, h0 * 64:(h0 + 1) * 64])
        cols.append(w_in[:, h1 * 64:(h1 + 1) * 64])
    for g in range(4):
        kg = w_in[:, 1024 + g * 64:1024 + (g + 1) * 64]
        cols.append(kg)
        cols.append(kg)
    cols.append(w_in[:, 1280:1536])
    cols.append(w_in[:, 1536:2048])
    w_in_p = np.ascontiguousarray(np.concatenate(cols, axis=1))
    assert w_in_p.shape == (D, WIN_COLS)
    b_adaT = np.ascontiguousarray(b_ada[0].reshape(48, 128).T)
    shared = {
        "w_ada": w_ada, "b_adaT": b_adaT, "b_ada": b_ada, "w_in_p": w_in_p,
        "w_ab": f(w_attn_branch)[0], "w_pool": f(w_pool)[0], "pool_scale": f(pool_scale),
        "attn_sink": f(attn_sink), "w_out": f(w_out)[0], "ln1_g": f(ln1_g), "ln1_b": f(ln1_b),
        "w_mlp_in": f(w_mlp_in)[0], "w_mlp_out": f(w_mlp_out)[0], "ln2_g": f(ln2_g), "ln2_b": f(ln2_b),
    }
    in_maps = []
    for core in (range(NCORES) if cores is None else cores):
        b, half = core // 2, core % 2
        t0 = half * TOK
        halo = np.zeros((256, D), np.float32)
        if half == 1:
            halo[0:128] = x[b, t0 - 128:t0]
        else:
            halo[128:256] = x[b, t0 + TOK:t0 + TOK + 128]
        cc2 = np.stack([c[b], c_ctx], axis=1)
        ccT = np.ascontiguousarray(cc2.reshape(8, 128, 2).transpose(1, 0, 2))
        m = dict(shared)
        m.update({
            "x_main": np.ascontiguousarray(x[b, t0:t0 + TOK]), "x_halo": halo, "ctx_in": np.ascontiguousarray(ctx[b]),
            "ccT": ccT, "rope_tab": _rope_tab(core), "cst": _const_pack(core),
        })
        in_maps.append(m)
    return in_maps


def kernel(x, c, ctx, c_ctx, w_ada, b_ada, w_in, w_attn_branch, w_pool, pool_scale, attn_sink, w_out,
           ln1_g, ln1_b, w_mlp_in, w_mlp_out, ln2_g, ln2_b):
    in_maps = make_in_maps(x, c, ctx, c_ctx, w_ada, b_ada, w_in, w_attn_branch, w_pool, pool_scale, attn_sink, w_out,
                           ln1_g, ln1_b, w_mlp_in, w_mlp_out, ln2_g, ln2_b)
    if "nc" not in _NC_CACHE:
        _NC_CACHE["nc"] = build_program()
    res = run_bass_kernel_spmd(_NC_CACHE["nc"], in_maps, core_ids=list(range(NCORES)))
    _NC_CACHE["last"] = res
    outp = np.empty((4, SEQ, D), np.float32)
    for core in range(NCORES):
        b, half = core // 2, core % 2
        outp[b, half * TOK:(half + 1) * TOK] = res.results[core]["out"]
    return outp
```

```python
import math
from contextlib import ExitStack

import numpy as np
import concourse.bass as bass
import concourse.mybir as mybir
from concourse.bass_utils import run_bass_kernel_spmd

F32 = mybir.dt.float32
BF16 = mybir.dt.bfloat16
AF = mybir.ActivationFunctionType
ALU = mybir.AluOpType

D = 1024
SEQ = 8192
NCORES = 8
TOK = 4096
NBLK = 32
NB = 2
SBW = NB * 128
NSB = NBLK // NB
NEXT = NBLK + 2
HEADS = 16
HD = 64
CTX = 256
DFF = 4096
ALPHA = 2.0 ** 0.25
LN_EPS = 1e-6
NEG = -30000.0
GA_OFF, GP_OFF, Q_OFF, K_OFF, V_OFF, P_OFF, WIN_COLS = 0, 1024, 2048, 3072, 3584, 3840, 4352
C_ID, C_ID4, C_MASK, C_PERM, C_AMAT = 0, 128, 640, 1152, 1280
C_COLS = C_AMAT + 36 * 128
RING = 6
EPOCH = 20000
DEBUG = False


class Eng:
    def __init__(self, ctx, name, h, is_pe=False):
        self.ctx, self.name, self.h, self.is_pe = ctx, name, h, is_pe
        self.sems = []
        self.count = 0
        self.known = {}

    def sem_for(self, seq):
        ep = (seq - 1) // EPOCH
        while len(self.sems) <= ep:
            self.sems.append(self.ctx.es.enter_context(self.ctx.nc.semaphore("s_%s_%d" % (self.name, len(self.sems)))))
        return self.sems[ep], seq - ep * EPOCH, ep


class Tile:
    def __init__(self, name, psum=False, multi=False):
        self.name = name
        self.psum = psum
        self.multi = multi
        self.writers = {}
        self.readers = {}
        self.dsem = None
        self.dcnt = 0
        self.dw = 0
        self.da = 0


class Ctx:
    def __init__(self, nc, es):
        self.nc, self.es = nc, es
        self.pe = Eng(self, "pe", nc.tensor, True)
        self.act = Eng(self, "act", nc.scalar)
        self.dve = Eng(self, "dve", nc.vector)
        self.pool = Eng(self, "pool", nc.gpsimd)
        self.sp = Eng(self, "sp", nc.sync)
        self.engs = [self.pe, self.act, self.dve, self.pool, self.sp]
        self.dma_tiles = []

    def wait_eng(self, eng, e, n):
        if n <= 0:
            return
        if e is eng and eng.is_pe:
            return
        sem, val, ep = e.sem_for(n)
        key = (e.name, ep)
        if eng.known.get(key, 0) >= val:
            return
        eng.h.wait_ge(sem, val)
        eng.known[key] = val

    def wait_dma(self, eng, t, val):
        if val <= 0:
            return
        key = ("d", id(t))
        if eng.known.get(key, 0) >= val:
            return
        eng.h.wait_ge(t.dsem, val)
        eng.known[key] = val

    def _pre(self, eng, reads, writes):
        for t in reads:
            for e, n in t.writers.items():
                self.wait_eng(eng, e, n)
            if t.psum:
                for e, n in t.readers.items():
                    if e is not eng:
                        self.wait_eng(eng, e, n)
            self.wait_dma(eng, t, t.dw)
        for t in writes:
            if not t.multi:
                for e, n in t.writers.items():
                    self.wait_eng(eng, e, n)
            for e, n in t.readers.items():
                self.wait_eng(eng, e, n)
            self.wait_dma(eng, t, t.da)

    def op(self, eng, fn, reads=(), writes=()):
        self._pre(eng, reads, writes)
        ins = fn()
        eng.count += 1
        seq = eng.count
        sem, _, _ = eng.sem_for(seq)
        ins.then_inc(sem, 1)
        for t in reads:
            if t not in writes:
                t.readers[eng] = seq
        for t in writes:
            if t.multi:
                t.writers[eng] = seq
            else:
                t.writers = {eng: seq}
                t.readers = {}
        return ins

    def dma(self, q, out_ap, in_ap, tile, load, extra_reads=(), **kw):
        if tile.dsem is None:
            tile.dsem = self.es.enter_context(self.nc.semaphore("d_%s" % tile.name))
            self.dma_tiles.append(tile)
        if load:
            self._pre(q, extra_reads, [tile])
        else:
            self._pre(q, [tile] + list(extra_reads), [])
        ins = q.h.dma_start(out=out_ap, in_=in_ap, **kw)
        tile.dcnt += 16
        ins.then_inc(tile.dsem, 16)
        if load:
            tile.dw = tile.da = tile.dcnt
            tile.writers = {}
            tile.readers = {}
        else:
            tile.da = tile.dcnt
        return ins

    def barrier(self, engs=None):
        engs = engs or self.engs
        for e in engs:
            for f in self.engs:
                if f is not self.sp and f is not e:
                    self.wait_eng(e, f, f.count)
            for t in self.dma_tiles:
                self.wait_dma(e, t, t.dcnt)


class Ring:
    def __init__(self, items):
        self.items, self.i = items, 0

    def next(self):
        it = self.items[self.i % len(self.items)]
        self.i += 1
        return it


class _Stop(Exception):
    pass


def build_program(level=3, nsb_run=NSB):
    nc = bass.Bass("TRN2", target_bir_lowering=False)
    try:
        _build(nc, level, nsb_run)
    except _Stop:
        pass
    return nc


STOPAT = None
VARIANT = None
ROPE_ADD_DVE = False


def _build(nc, level, nsb_run):

    def din(name, shape, dt=F32):
        return nc.dram_tensor(name, list(shape), dt, kind="ExternalInput").ap()

    x_main = din("x_main", [TOK, D])
    x_halo = din("x_halo", [256, D])
    ctx_in = din("ctx_in", [CTX, D])
    ccT_in = din("ccT", [128, 8, 2])
    w_ada = din("w_ada", [D, 6 * D])
    b_adaT_in = din("b_adaT", [128, 48])
    b_ada = din("b_ada", [1, 6 * D])
    w_in_p = din("w_in_p", [D, WIN_COLS])
    w_ab = din("w_ab", [D, D])
    w_pool = din("w_pool", [4, 128, 256])
    pool_scale = din("pool_scale", [1, D])
    attn_sink = din("attn_sink", [1, HEADS])
    w_out = din("w_out", [D, D])
    ln1_g = din("ln1_g", [1, D])
    ln1_b = din("ln1_b", [1, D])
    w_mlp_in = din("w_mlp_in", [D, DFF])
    w_mlp_out = din("w_mlp_out", [DFF, D])
    ln2_g = din("ln2_g", [1, D])
    ln2_b = din("ln2_b", [1, D])
    rope_tab = din("rope_tab", [128, 2, NEXT * 128])
    cst_in = din("cst", [128, C_COLS])
    out = nc.dram_tensor("out", [TOK, D], F32, kind="ExternalOutput").ap()
    mt_scr = nc.dram_tensor("mt_scr", [NSB, 128, 8 * SBW], BF16, kind="Internal").ap()
    x1_scr = nc.dram_tensor("x1_scr", [TOK, D], F32, kind="Internal").ap()
    dbg = {}
    if DEBUG:
        dbg["modT"] = nc.dram_tensor("dbg_modT", [128, 96], F32, kind="ExternalOutput").ap()
        dbg["uT0"] = nc.dram_tensor("dbg_uT0", [128, 8 * SBW], BF16, kind="ExternalOutput").ap()
        dbg["QT0"] = nc.dram_tensor("dbg_QT0", [128, 8 * SBW], BF16, kind="ExternalOutput").ap()
        dbg["KT1"] = nc.dram_tensor("dbg_KT1", [128, 512], BF16, kind="ExternalOutput").ap()
        dbg["V1"] = nc.dram_tensor("dbg_V1", [128, 4 * 65], BF16, kind="ExternalOutput").ap()
        dbg["OT0"] = nc.dram_tensor("dbg_OT0", [128, 8 * SBW], BF16, kind="ExternalOutput").ap()
        dbg["PL0"] = nc.dram_tensor("dbg_PL0", [128, 4 * SBW], BF16, kind="ExternalOutput").ap()
        dbg["mt"] = nc.dram_tensor("dbg_mt", [128, 8 * SBW], BF16, kind="ExternalOutput").ap()
        dbg["x1"] = nc.dram_tensor("dbg_x1", [256, D], F32, kind="ExternalOutput").ap()

    with ExitStack() as es:
        cx = Ctx(nc, es)
        PE, ACT, DVE, POOL, SP = cx.pe, cx.act, cx.dve, cx.pool, cx.sp

        def sbuf(stack, name, shape, dt):
            return stack.enter_context(nc.sbuf_tensor(name, list(shape), dt))

        banks = []
        for i in range(8):
            t = es.enter_context(nc.psum_tensor("bank%d" % i, [128, 512], F32))
            banks.append((Tile("bank%d" % i, psum=True), t))

        cst = sbuf(es, "cst_sb", [128, C_COLS], BF16)
        cstT = Tile("cst")
        ident = cst[:, C_ID:C_ID + 128]
        ident4 = cst[:, C_ID4:C_ID4 + 512]
        perm = cst[:, C_PERM:C_PERM + 128]

        def mask_ap(i):
            return cst[:, C_MASK + i * 128:C_MASK + (i + 1) * 128]

        def amat_ap(var, rel, g):
            o = C_AMAT + ((var * 3 + rel) * 4 + g) * 128
            return cst[:, o:o + 128]

        modT = sbuf(es, "modT", [128, 48, 2], F32)
        modTt = Tile("modT")
        g_scr = nc.dram_tensor("g_scr", [2, D], F32, kind="Internal").ap()
        mhalf = sbuf(es, "mhalf", [128, 1], F32)
        expsink = sbuf(es, "expsink", [128, HEADS], F32)
        miscT = Tile("misc")
        dbg_stage = None

        def dump(name, tile, ap):
            if DEBUG and name in dbg:
                cx.dma(SP, dbg[name], ap, tile, load=False)

        ln_sm = []
        for i in range(8):
            ln_sm.append((Tile("lnsm%d" % i), sbuf(es, "lnst%d" % i, [128, 2, 6], F32), sbuf(es, "lnmv%d" % i, [128, 4], F32)))
        ln_ring = Ring(ln_sm)

        def ln_stats(xt, xap):
            t, st, mv = ln_ring.next()
            cx.op(DVE, lambda: nc.vector.bn_stats(out=st[:, 0, :], in_=xap[:, 0:512]), [xt], [t])
            cx.op(DVE, lambda: nc.vector.bn_stats(out=st[:, 1, :], in_=xap[:, 512:1024]), [xt, t], [t])
            cx.op(DVE, lambda: nc.vector.bn_aggr(out=mv[:, 0:2], in_=st[:, :, :].rearrange("p a b -> p (a b)")), [t], [t])
            cx.op(POOL, lambda: nc.gpsimd.tensor_scalar(out=mv[:, 2:3], in0=mv[:, 1:2], scalar1=LN_EPS, scalar2=None, op0=ALU.add), [t], [t])
            cx.op(POOL, lambda: nc.gpsimd.tensor_tensor(out=mv[:, 2:3], in0=mv[:, 2:3], in1=mhalf[:, :], op=ALU.pow), [t, miscT], [t])
            cx.op(POOL, lambda: nc.gpsimd.tensor_tensor(out=mv[:, 3:4], in0=mv[:, 0:1], in1=mv[:, 2:3], op=ALU.mult), [t], [t])
            cx.op(POOL, lambda: nc.gpsimd.tensor_scalar(out=mv[:, 3:4], in0=mv[:, 3:4], scalar1=-1.0, scalar2=None, op0=ALU.mult), [t], [t])
            return t, mv[:, 2:3], mv[:, 3:4]

        def cast_load(dst_fn, src_fn, ncols, tile):
            c0 = 0
            while c0 < ncols:
                w = min(2048, ncols - c0)
                cx.dma(POOL, dst_fn(c0, w), src_fn(c0, w), tile, load=True)
                c0 += w

        cast_load(lambda c0, w: cst[:, c0:c0 + w], lambda c0, w: cst_in[:, c0:c0 + w], C_COLS, cstT)
        cx.op(POOL, lambda: nc.gpsimd.memset(mhalf[:, :], -0.5), [], [miscT])
        cx.dma(SP, expsink[:, :], attn_sink[0:1, :].partition_broadcast(128), miscT, load=True)
        cx.op(ACT, lambda: nc.scalar.activation(out=expsink[:, :], in_=expsink[:, :], func=AF.Exp), [miscT], [miscT])

        sw = ExitStack()
        sw.__enter__()
        Win = sbuf(sw, "Win", [128, 8, WIN_COLS], BF16)
        WinT = Tile("Win", multi=True)
        Wab = sbuf(sw, "Wab", [128, 8, D], BF16)
        WabT = Tile("Wab", multi=True)
        Wpl = sbuf(sw, "Wpl", [128, 4, 256], BF16)
        WplT = Tile("Wpl")
        with ExitStack() as s0:
            ccT = sbuf(s0, "ccT_sb", [128, 8, 2], F32)
            cth = sbuf(s0, "cth", [128, 8, 2], F32)
            siluT = sbuf(s0, "siluT", [128, 8, 2], BF16)
            badaT = sbuf(s0, "badaT", [128, 48], F32)
            brow = sbuf(s0, "brow", [1, 2, D], F32)
            grow = sbuf(s0, "grow", [1, 2, D], F32)
            growT = Tile("grow")
            s0T = Tile("s0")
            wsl = [(Tile("wada%d" % i, multi=True), sbuf(s0, "wada%d" % i, [128, 8, 512], BF16)) for i in range(2)]
            stq = Ring([(Tile("stq%d" % i), sbuf(s0, "stq%d" % i, [128, 2048], F32)) for i in range(5)])
            cast_engs = Ring([DVE, ACT])

            def stage_cast(dst_ap, src_ap, dtile, shape3=None):
                st_t, st = stq.next()
                w = 1
                for d_ in dst_ap.shape[1:]:
                    w *= d_
                sv = st[:, 0:w]
                if shape3 is not None:
                    sv = sv.rearrange("p (k n) -> p k n", k=shape3)
                cx.dma(SP, sv, src_ap, st_t, load=True)
                e = cast_engs.next()
                if e is ACT:
                    cx.op(ACT, lambda: nc.scalar.copy(out=dst_ap, in_=sv), [st_t], [dtile])
                elif e is DVE:
                    cx.op(DVE, lambda: nc.vector.tensor_copy(out=dst_ap, in_=sv), [st_t], [dtile])
                else:
                    cx.op(POOL, lambda: nc.gpsimd.tensor_copy(out=dst_ap, in_=sv), [st_t], [dtile])

            cx.dma(SP, ccT[:, :, :], ccT_in[:, :, :], s0T, load=True)
            cx.dma(SP, badaT[:, :], b_adaT_in[:, :], s0T, load=True)
            cx.dma(SP, brow[:, 0, :], b_ada[0:1, 2 * D:3 * D], s0T, load=True)
            cx.dma(SP, brow[:, 1, :], b_ada[0:1, 5 * D:6 * D], s0T, load=True)
            cx.op(ACT, lambda: nc.scalar.activation(out=cth[:, :, :], in_=ccT[:, :, :], func=AF.Tanh, scale=0.5), [s0T], [s0T])
            cx.op(DVE, lambda: nc.vector.scalar_tensor_tensor(out=cth[:, :, :], in0=cth[:, :, :], scalar=1.0, in1=ccT[:, :, :], op0=ALU.add, op1=ALU.mult), [s0T], [s0T])
            cx.op(DVE, lambda: nc.vector.tensor_scalar(out=siluT[:, :, :], in0=cth[:, :, :], scalar1=0.5, scalar2=None, op0=ALU.mult), [s0T], [s0T])
            w_ada_v = w_ada.rearrange("(kc p) n -> p kc n", p=128)
            w_in_v = w_in_p.rearrange("(kc p) n -> p kc n", p=128)
            w_ab_v = w_ab.rearrange("(kc p) n -> p kc n", p=128)
            bmod_t, bmod = banks[0]
            brw_t, brw = banks[1]

            def load_win():
                for kc in range(8):
                    c0 = 0
                    while c0 < WIN_COLS:
                        w = min(2048, WIN_COLS - c0)
                        stage_cast(Win[:, kc, c0:c0 + w], w_in_v[:, kc, c0:c0 + w], WinT)
                        c0 += w

            def load_wab():
                for kc in range(0, 8, 2):
                    stage_cast(Wab[:, kc:kc + 2, :], w_ab_v[:, kc:kc + 2, :], WabT, shape3=2)

            for pc in range(12):
                if pc == 4:
                    load_win()
                wt, wtile = wsl[pc % 2]
                for h in range(2):
                    stage_cast(wtile[:, :, h * 256:(h + 1) * 256], w_ada_v[:, :, pc * 512 + h * 256:pc * 512 + (h + 1) * 256], wt, shape3=8)
                for mm in range(4):
                    m = pc * 4 + mm
                    for kc in range(8):
                        cx.op(PE, lambda: nc.tensor.matmul(bmod[:, 2 * m:2 * m + 2], lhsT=wtile[:, kc, mm * 128:(mm + 1) * 128],
                                                           rhs=siluT[:, kc, :], start=(kc == 0), stop=(kc == 7)),
                              [wt, s0T], [bmod_t])
                if pc in (4, 5, 10, 11):
                    for kc in range(8):
                        cx.op(PE, lambda: nc.tensor.matmul(brw[0:2, :], lhsT=siluT[:, kc, :], rhs=wtile[:, kc, :],
                                                           start=(kc == 0), stop=(kc == 7)), [wt, s0T], [brw_t])
                    bi = 0 if pc < 6 else 1
                    hf = pc % 2
                    cx.op(DVE, lambda: nc.vector.tensor_tensor(out=grow[0:1, bi, hf * 512:(hf + 1) * 512], in0=brw[0:1, :],
                                                               in1=brow[0:1, bi, hf * 512:(hf + 1) * 512], op=ALU.add),
                          [brw_t, s0T], [growT])
            load_wab()
            cx.op(DVE, lambda: nc.vector.tensor_tensor(out=modT[:, :, :], in0=bmod[:, 0:96].rearrange("p (m j) -> p m j", j=2),
                                                       in1=badaT[:, :].unsqueeze(2).to_broadcast([128, 48, 2]), op=ALU.add),
                  [bmod_t, s0T], [modTt])
            cx.op(DVE, lambda: nc.vector.tensor_scalar(out=modT[:, 8:16, :], in0=modT[:, 8:16, :], scalar1=1.0, scalar2=None, op0=ALU.add), [modTt], [modTt])
            cx.op(DVE, lambda: nc.vector.tensor_scalar(out=modT[:, 32:40, :], in0=modT[:, 32:40, :], scalar1=1.0, scalar2=None, op0=ALU.add), [modTt], [modTt])
            dump("modT", modTt, modT[:, :, :].rearrange("p m j -> p (m j)"))
            cx.dma(SP, g_scr[0:1, :], grow[0:1, 0, :], growT, load=False)
            cx.dma(SP, g_scr[1:2, :], grow[0:1, 1, :], growT, load=False)
            wpf = sbuf(s0, "wpf", [128, 4, 256], F32)
            psb = sbuf(s0, "psb", [128, D], F32)
            s1T = Tile("s1")
            cx.dma(SP, wpf[:, :, :], w_pool.rearrange("g c n -> c g n"), s1T, load=True)
            cx.dma(SP, psb[:, :], pool_scale[0:1, :].partition_broadcast(128), s1T, load=True)
            cx.op(DVE, lambda: nc.vector.tensor_tensor(out=Wpl[:, :, :], in0=wpf[:, :, :],
                                                       in1=psb[:, :].rearrange("p (g n) -> p g n", g=4), op=ALU.mult), [s1T], [WplT])
            cx.barrier()
            if level == 0:
                sw.__exit__(None, None, None)
                raise _Stop()

        with ExitStack() as sa:
            def stopat(label):
                if STOPAT == label:
                    cx.barrier()
                    raise _Stop()

            stopat("w")
            NXS = 2
            xsl = [(Tile("xs%d" % i), sbuf(sa, "xs%d" % i, [128, D], F32)) for i in range(NXS)]
            xhl = [(Tile("xh%d" % i), sbuf(sa, "xh%d" % i, [128, D], BF16)) for i in range(2)]
            uTl = [(Tile("uT%d" % i), sbuf(sa, "uT%d" % i, [128, 8, SBW], BF16)) for i in range(3)]
            QTl = [(Tile("QT%d" % i), sbuf(sa, "QT%d" % i, [128, 8, SBW], BF16)) for i in range(2)]
            KTb = sbuf(sa, "KTb", [128, RING, 2, 4, 128], BF16)
            KTt = [Tile("KT%d" % i) for i in range(RING)]
            Vb = sbuf(sa, "Vb", [128, RING, 4, 65], BF16)
            Vt = [Tile("V%d" % i) for i in range(RING)]
            Pb = sbuf(sa, "Pb", [128, RING, 512], BF16)
            Pt = [Tile("P%d" % i) for i in range(RING)]
            KcT = sbuf(sa, "KcT", [128, 2, 4, CTX], BF16)
            KcTt = Tile("KcT")
            Vcb = sbuf(sa, "Vcb", [128, 2, 4, 65], BF16)
            Vct = [Tile("Vc%d" % i) for i in range(2)]
            NPT = 10
            PTb = sbuf(sa, "PTb", [128, NPT, 512], BF16)
            PTr = Ring([(Tile("PT%d" % i), PTb[:, i, :]) for i in range(NPT)])
            Obl = [(Tile("Ob%d" % i), sbuf(sa, "Ob%d" % i, [128, HEADS, HD], BF16)) for i in range(2)]
            OTt, OT = Tile("OT"), sbuf(sa, "OT", [128, 8, SBW], BF16)
            PLt, PL = Tile("PL"), sbuf(sa, "PL", [128, 4, SBW], BF16)
            mTl = [(Tile("mT%d" % i), sbuf(sa, "mT%d" % i, [128, 8, SBW], BF16)) for i in range(2)]
            zbr = Ring([(Tile("zb%d" % i), sbuf(sa, "zb%d" % i, [128, SBW], BF16)) for i in range(3)])
            t1r = Ring([(Tile("t1%d" % i), sbuf(sa, "t1%d" % i, [128, SBW], F32)) for i in range(3)])
            t2r = Ring([(Tile("t2%d" % i), sbuf(sa, "t2%d" % i, [128, SBW], F32)) for i in range(2)])
            rtl = [(Tile("rt%d" % i), sbuf(sa, "rt%d" % i, [128, 2, SBW], F32)) for i in range(3)]
            tgr = Ring([(Tile("tg%d" % i), sbuf(sa, "tg%d" % i, [128, SBW], F32)) for i in range(4)])
            u1r = Ring([(Tile("u1%d" % i), sbuf(sa, "u1%d" % i, [128, SBW], F32)) for i in range(4)])
            denr = Ring([(Tile("den%d" % i), sbuf(sa, "den%d" % i, [128, 8], F32)) for i in range(2)])
            proj = Ring(banks[0:4])
            STr = Ring(banks[4:6])
            Or = Ring(banks[6:8])

            for i in range(RING):
                cx.op(POOL, lambda: nc.gpsimd.memset(Vb[:, i, :, 64:65], 1.0), [], [Vt[i]])
                cx.op(POOL, lambda: nc.gpsimd.memset(KTb[:, i, :, :, :], 0.0), [], [KTt[i]])
            cx.op(POOL, lambda: nc.gpsimd.memset(KcT[:, :, :, :], 0.0), [], [KcTt])
            for i in range(2):
                cx.op(POOL, lambda: nc.gpsimd.memset(Vcb[:, i, :, 64:65], 1.0), [], [Vct[i]])

            def x_rows(e):
                if e == 0:
                    return x_halo[0:128, :]
                if e == NEXT - 1:
                    return x_halo[128:256, :]
                return x_main[(e - 1) * 128:e * 128, :]

            xs_ctr = [0]
            pending_x = {}

            def issue_x(key, src_ap):
                i = xs_ctr[0] % NXS
                xs_ctr[0] += 1
                t, tl = xsl[i]
                cx.dma(SP, tl[:, :], src_ap, t, load=True)
                pending_x[key] = (t, tl)

            xh_ctr = [0]

            def ln_steps(key, uTt, uT, col0, j):
                stt_ = {}

                def f_stats():
                    xt, xtile = pending_x.pop(key)
                    stt_["x"] = (xt, xtile)
                    stt_["ln"] = ln_stats(xt, xtile)

                def f_hat():
                    xt, xtile = stt_["x"]
                    st, rstd, nmr = stt_["ln"]
                    ht, htile = xhl[xh_ctr[0] % 2]
                    xh_ctr[0] += 1
                    stt_["h"] = (ht, htile)
                    cx.op(ACT, lambda: nc.scalar.activation(out=htile[:, :], in_=xtile[:, :], func=AF.Identity, bias=nmr, scale=rstd), [xt, st], [ht])

                def f_tr():
                    ht, htile = stt_["h"]
                    TRt, TRb = proj.next()
                    TRv = TRb[:, :].bitcast(BF16)
                    stt_["tr"] = (TRt, TRv)
                    for c in range(8):
                        cx.op(PE, lambda: nc.tensor.transpose(TRv[:, c * 128:(c + 1) * 128], htile[:, c * 128:(c + 1) * 128], ident), [ht, cstT], [TRt])

                def f_mod():
                    TRt, TRv = stt_["tr"]
                    for c in range(8):
                        cx.op(DVE, lambda: nc.vector.tensor_scalar(out=uT[:, c, col0:col0 + 128], in0=TRv[:, c * 128:(c + 1) * 128],
                                                                   scalar1=modT[:, 8 + c, j:j + 1], scalar2=modT[:, c, j:j + 1],
                                                                   op0=ALU.mult, op1=ALU.add), [TRt, modTt], [uTt])
                def f_trmod():
                    f_tr()
                    f_mod()
                return [f_stats, None, f_hat, None, f_trmod, None, None, None]

            def ln_modulate(key, uTt, uT, col0, j):
                for f in ln_steps(key, uTt, uT, col0, j):
                    if f is not None:
                        f()

            def proj_fm(uTt, uT, n, off):
                pt, pb = proj.next()
                for kc in range(8):
                    cx.op(PE, lambda: nc.tensor.matmul(pb[:, 0:n], lhsT=Win[:, kc, off:off + 128], rhs=uT[:, kc, 0:n],
                                                       start=(kc == 0), stop=(kc == 7)), [WinT, uTt], [pt])
                return pt, pb

            def rope_front(uTt, uT, n, off, rtt, rt):
                pt, pb = proj_fm(uTt, uT, n, off)
                zt, zb = zbr.next()
                cx.op(ACT, lambda: nc.scalar.copy(out=zb[:, 0:n], in_=pb[:, 0:n]), [pt], [zt])
                t1t, t1 = t1r.next()
                cx.op(DVE, lambda: nc.vector.tensor_tensor(out=t1[:, 0:n], in0=pb[:, 0:n], in1=rt[:, 0, 0:n], op=ALU.mult), [pt, rtt], [t1t])
                return (zt, zb, t1t, t1)

            def rope_back(st_, n, rtt, rt, dsts):
                zt, zb, t1t, t1 = st_
                p2t, p2b = proj.next()
                cx.op(PE, lambda: nc.tensor.matmul(p2b[:, 0:n], lhsT=perm, rhs=zb[:, 0:n], start=True, stop=True), [zt, cstT], [p2t])
                t2t, t2 = t2r.next()
                cx.op(DVE, lambda: nc.vector.tensor_tensor(out=t2[:, 0:n], in0=p2b[:, 0:n], in1=rt[:, 1, 0:n], op=ALU.mult), [p2t, rtt], [t2t])
                for (dt_, dap, c0, w, p0, p1) in dsts:
                    cx.op(POOL, lambda: nc.gpsimd.tensor_tensor(out=dap, in0=t1[p0:p1, c0:c0 + w], in1=t2[p0:p1, c0:c0 + w], op=ALU.add), [t1t, t2t], [dt_])

            def rope_many(uTt, uT, n, rtt, rt, jobs):
                prev = None
                for (off, dsts) in jobs:
                    cur = (rope_front(uTt, uT, n, off, rtt, rt), dsts)
                    if prev is not None:
                        rope_back(prev[0], n, rtt, rt, prev[1])
                    prev = cur
                if prev is not None:
                    rope_back(prev[0], n, rtt, rt, prev[1])

            def v_block(uTt, uT, col0, vt, vap):
                pt, pb = proj.next()
                for kc in range(8):
                    cx.op(PE, lambda: nc.tensor.matmul(pb[:, 0:256], lhsT=uT[:, kc, col0:col0 + 128], rhs=Win[:, kc, V_OFF:V_OFF + 256],
                                                       start=(kc == 0), stop=(kc == 7)), [WinT, uTt], [pt])
                cx.op(ACT, lambda: nc.scalar.copy(out=vap, in_=pb[:, 0:256].rearrange("p (g d) -> p g d", g=4)), [pt], [vt])

            def p_block(uTt, uT, col0, ptile, pap):
                pt, pb = proj.next()
                for kc in range(8):
                    cx.op(PE, lambda: nc.tensor.matmul(pb[:, 0:512], lhsT=uT[:, kc, col0:col0 + 128], rhs=Win[:, kc, P_OFF:P_OFF + 512],
                                                       start=(kc == 0), stop=(kc == 7)), [WinT, uTt], [pt])
                cx.op(DVE, lambda: nc.vector.tensor_copy(out=pap, in_=pb[:, 0:512]), [pt], [ptile])

            uct, uc = uTl[1]
            for i in range(2):
                issue_x(("c", i), ctx_in[i * 128:(i + 1) * 128, :])
            stopat("ms")
            def run_zip(lists):
                k = 0
                while any(k < len(l_) for l_ in lists):
                    for l_ in lists:
                        if k < len(l_) and l_[k] is not None:
                            l_[k]()
                    k += 1

            run_zip([ln_steps(("c", i), uct, uc, i * 128, 1) for i in range(2)])
            stopat("ctxln")

            def ctx_jobs():
                jobs = []
                for g in range(4):
                    def fk(g=g):
                        pt, pb = proj_fm(uct, uc, CTX, K_OFF + g * 128)
                        cx.op(ACT, lambda: nc.scalar.copy(out=KcT[0:64, 0, g, :], in_=pb[0:64, 0:CTX]), [pt], [KcTt])
                        cx.op(ACT, lambda: nc.scalar.copy(out=KcT[64:128, 1, g, :], in_=pb[64:128, 0:CTX]), [pt], [KcTt])
                    jobs.append(fk)
                for i in range(2):
                    jobs.append(lambda i=i: v_block(uct, uc, i * 128, Vct[i], Vcb[:, i, :, 0:64]))
                return jobs

            def sb_blocks(s):
                if s == -1:
                    return [0]
                if s == NSB:
                    return [NEXT - 1]
                return [1 + NB * s + i for i in range(NB)]

            def issue_loads(s):
                bl = sb_blocks(s)
                for e in bl:
                    issue_x(("x", e), x_rows(e))
                rtt, rt = rtl[(s + 3) % 3]
                n = len(bl) * 128
                cx.dma(SP, rt[:, :, 0:n], rope_tab[:, :, bl[0] * 128:bl[0] * 128 + n], rtt, load=True)

            def lnA_steps(s):
                bl = sb_blocks(s)
                uTt, uT = uTl[(s + 3) % 3]
                steps = [lambda: issue_loads(s)]
                for i, e in enumerate(bl):
                    steps += ln_steps(("x", e), uTt, uT, i * 128, 0)
                return steps

            def projA_jobs(s):
                bl = sb_blocks(s)
                n = len(bl) * 128
                main = 0 <= s < NSB
                uTt, uT = uTl[(s + 3) % 3]
                rtt, rt = rtl[(s + 3) % 3]
                hold = {"prev": None}
                jobs = []

                def rope_job(off, dsts):
                    def f():
                        cur = (rope_front(uTt, uT, n, off, rtt, rt), dsts)
                        if hold["prev"] is not None:
                            rope_back(hold["prev"][0], n, rtt, rt, hold["prev"][1])
                        hold["prev"] = cur
                    return f

                def rope_flush():
                    if hold["prev"] is not None:
                        rope_back(hold["prev"][0], n, rtt, rt, hold["prev"][1])
                        hold["prev"] = None

                for g in range(4):
                    dsts = []
                    for i, e in enumerate(bl):
                        dsts.append((KTt[e % RING], KTb[0:64, e % RING, 0, g, :], i * 128, 128, 0, 64))
                        dsts.append((KTt[e % RING], KTb[64:128, e % RING, 1, g, :], i * 128, 128, 64, 128))
                    jobs.append(rope_job(K_OFF + g * 128, dsts))
                jobs.append(rope_flush)
                for i, e in enumerate(bl):
                    jobs.append(lambda i=i, e=e: v_block(uTt, uT, i * 128, Vt[e % RING], Vb[:, e % RING, :, 0:64]))
                for i, e in enumerate(bl):
                    jobs.append(lambda i=i, e=e: p_block(uTt, uT, i * 128, Pt[e % RING], Pb[:, e % RING, :]))
                if main:
                    qt, q = QTl[s % 2]
                    for c in range(8):
                        jobs.append(rope_job(Q_OFF + c * 128, [(qt, q[:, c, 0:n], 0, n, 0, 128)]))
                    jobs.append(rope_flush)
                return jobs

            def projA(s):
                for f in projA_jobs(s):
                    f()

            def pooling(s):
                for i, e in enumerate(sb_blocks(s)):
                    j = e - 1
                    var = 0 if j == 0 else (2 if j == NBLK - 1 else 1)
                    pt, pb = proj.next()
                    for g in range(4):
                        for rel in range(3):
                            ee = e - 1 + rel
                            cx.op(PE, lambda: nc.tensor.matmul(pb[:, g * 128:(g + 1) * 128], lhsT=Pb[:, ee % RING, g * 128:(g + 1) * 128],
                                                               rhs=amat_ap(var, rel, g), start=(rel == 0), stop=(rel == 2)),
                                  [Pt[ee % RING], cstT], [pt])
                    cx.op(ACT, lambda: nc.scalar.copy(out=PL[:, :, i * 128:(i + 1) * 128], in_=pb[:, 0:512].rearrange("p (g t) -> p g t", g=4)), [pt], [PLt])

            def att_S(s, i, g, between=None):
                qt, q = QTl[s % 2]
                e = sb_blocks(s)[i]
                j = e - 1
                c0 = i * 128
                srcs = [
                    (KTt[(e - 1) % RING], KTb[:, (e - 1) % RING, :, g, :], Vt[(e - 1) % RING], Vb[:, (e - 1) % RING, g, :], mask_ap(0 if j == 0 else 1)),
                    (KTt[e % RING], KTb[:, e % RING, :, g, :], Vt[e % RING], Vb[:, e % RING, g, :], None),
                    (KTt[(e + 1) % RING], KTb[:, (e + 1) % RING, :, g, :], Vt[(e + 1) % RING], Vb[:, (e + 1) % RING, g, :], mask_ap(3 if j == NBLK - 1 else 2)),
                    (KcTt, KcT[:, :, g, 0:128], Vct[0], Vcb[:, 0, g, :], None),
                    (KcTt, KcT[:, :, g, 128:256], Vct[1], Vcb[:, 1, g, :], None),
                ]
                pts = []
                for kbi, (kt, kap, vt, vap, mk) in enumerate(srcs):
                    if between is not None and kbi in (2, 4):
                        between()
                    stt, stb = STr.next()
                    cx.op(PE, lambda: nc.tensor.matmul(stb[:, 0:256], lhsT=kap[:, 0, :], rhs=q[:, 2 * g:2 * g + 2, c0:c0 + 128],
                                                       start=True, stop=True, skip_group_check=True), [kt, qt], [stt])
                    cx.op(PE, lambda: nc.tensor.matmul(stb[:, 256:512], lhsT=kap[:, 1, :], rhs=q[:, 2 * g:2 * g + 2, c0:c0 + 128],
                                                       start=True, stop=True, skip_group_check=True), [kt, qt], [stt])
                    ptt, ptb = PTr.next()
                    cx.op(ACT, lambda: nc.scalar.activation(out=ptb, in_=stb[:, 0:512], func=AF.Exp, scale=0.125), [stt], [ptt])
                    if mk is not None:
                        cx.op(POOL, lambda: nc.gpsimd.tensor_tensor(out=ptb.rearrange("p (c q) -> p c q", c=4), in0=ptb.rearrange("p (c q) -> p c q", c=4),
                                                                    in1=mk.unsqueeze(1).to_broadcast([128, 4, 128]), op=ALU.mult), [ptt, cstT], [ptt])
                    pts.append((ptt, ptb, vt, vap))
                return pts

            def att_PV(s, i, g, pts):
                obt, ob = Obl[i % 2]
                ot, obk = Or.next()
                ov = obk[:, 0:260].rearrange("p (c d) -> p c d", d=65)
                for cb in range(4):
                    for kb, (ptt, ptb, vt, vap) in enumerate(pts):
                        cx.op(PE, lambda: nc.tensor.matmul(ov[:, cb, :], lhsT=ptb[:, cb * 128:(cb + 1) * 128], rhs=vap,
                                                           start=(kb == 0), stop=(kb == 4)), [ptt, vt], [ot])
                dt_, den = denr.next()
                cx.op(DVE, lambda: nc.vector.tensor_tensor(out=den[:, 0:4], in0=ov[:, :, 64], in1=expsink[:, 4 * g:4 * g + 4], op=ALU.add), [ot, miscT], [dt_])
                cx.op(DVE, lambda: nc.vector.reciprocal(out=den[:, 4:8], in_=den[:, 0:4]), [dt_], [dt_])
                cx.op(DVE, lambda: nc.vector.tensor_tensor(out=ob[:, 4 * g:4 * g + 4, :], in0=ov[:, :, 0:64],
                                                           in1=den[:, 4:8].unsqueeze(2).to_broadcast([128, 4, 64]), op=ALU.mult), [ot, dt_], [obt])

            def att_T(s, i):
                obt, ob = Obl[i % 2]
                c0 = i * 128
                obf = ob[:, :, :].rearrange("p h d -> p (h d)")
                TRt, TRb = proj.next()
                TRv = TRb[:, :].bitcast(BF16)
                for c in range(8):
                    cx.op(PE, lambda: nc.tensor.transpose(TRv[:, c * 128:(c + 1) * 128], obf[:, c * 128:(c + 1) * 128], ident), [obt, cstT], [TRt])
                cx.op(ACT, lambda: nc.scalar.copy(out=OT[:, :, c0:c0 + 128], in_=TRv[:, :].rearrange("p (c t) -> p c t", c=8)), [TRt], [OTt])

            def attention(s, pend, fill):
                units = [(i, g) for i in range(NB) for g in range(4)]
                prev = None
                nfill = len(fill)
                def one_fill():
                    if fill:
                        fill.pop(0)()

                pend_T = None
                for ui, (i, g) in enumerate(units):
                    if ui == 4:
                        while len(fill) > nfill - 7 and fill:
                            fill.pop(0)()
                    pts = att_S(s, i, g, between=one_fill)
                    if pend_T is not None:
                        pend_T[1] -= 1
                        if pend_T[1] == 0:
                            att_T(s, pend_T[0])
                            pend_T = None
                    if prev is not None:
                        pi, pg, ppts = prev
                        att_PV(s, pi, pg, ppts)
                        if pg == 3:
                            pend_T = [pi, 2]
                    prev = (i, g, pts)
                    if ui < 4:
                        one_fill()
                    run_steps_a(pend, 2)
                pi, pg, ppts = prev
                att_PV(s, pi, pg, ppts)
                for _ in range(3):
                    one_fill()
                att_T(s, pi)

            def run_steps_a(q_, n_):
                for _ in range(n_):
                    if q_:
                        f = q_.pop(0)
                        if f is not None:
                            f()

            mt_pending = []

            def merge(s):
                uTt, uT = uTl[(s + 3) % 3]
                mt, mT = mTl[s % 2]
                for m in range(8):
                    tgs = []
                    for a, off in enumerate((GA_OFF, GP_OFF)):
                        tgt, tg = tgr.next()
                        pt, pb = proj_fm(uTt, uT, SBW, off + m * 128)
                        cx.op(ACT, lambda: nc.scalar.activation(out=tg[:, :], in_=pb[:, 0:SBW], func=AF.Tanh, scale=0.5), [pt], [tgt])
                        tgs.append((tgt, tg))
                    u1t, u1 = u1r.next()
                    u2t_, u2_ = u1r.next()
                    pt, pb = proj.next()
                    for c in range(8):
                        cx.op(PE, lambda: nc.tensor.matmul(pb[:, 0:SBW], lhsT=Wab[:, c, m * 128:(m + 1) * 128], rhs=OT[:, c, :],
                                                           start=(c == 0), stop=(c == 7)), [WabT, OTt], [pt])
                    cx.op(DVE, lambda: nc.vector.scalar_tensor_tensor(out=u1[:, :], in0=tgs[0][1][:, :], scalar=1.0, in1=pb[:, 0:SBW],
                                                                      op0=ALU.add, op1=ALU.mult), [tgs[0][0], pt], [u1t])
                    pt, pb = proj.next()
                    g = m // 2
                    cx.op(PE, lambda: nc.tensor.matmul(pb[:, 0:SBW], lhsT=Wpl[:, g, (m % 2) * 128:(m % 2) * 128 + 128], rhs=PL[:, g, :],
                                                       start=True, stop=True), [WplT, PLt], [pt])
                    cx.op(DVE, lambda: nc.vector.scalar_tensor_tensor(out=u2_[:, :], in0=tgs[1][1][:, :], scalar=1.0, in1=pb[:, 0:SBW],
                                                                      op0=ALU.add, op1=ALU.mult), [tgs[1][0], pt], [u2t_])
                    cx.op(POOL, lambda: nc.gpsimd.tensor_tensor(out=mT[:, m, :], in0=u1[:, :], in1=u2_[:, :], op=ALU.add), [u1t, u2t_], [mt])
                mt_pending.append(lambda: cx.dma(SP, mt_scr[s], mT[:, :, :].rearrange("p c t -> p (c t)"), mt, load=False))

            stopat("ctx")
            def jobs_with(jobs, pend_, k_):
                for jb in jobs:
                    jb()
                    run_steps_a(pend_, k_)

            pend0 = lnA_steps(-1) + lnA_steps(0)
            jobs_with(ctx_jobs(), pend0, 3)
            run_steps_a(pend0, len(pend0))
            pend1 = lnA_steps(1)
            jobs_with(projA_jobs(-1), pend1, 1)
            jobs_with(projA_jobs(0), pend1, 1)
            run_steps_a(pend1, len(pend1))
            if DEBUG:
                dump("uT0", uTl[0][0], uTl[0][1][:, :, :].rearrange("p c t -> p (c t)"))
                dump("QT0", QTl[0][0], QTl[0][1][:, :, :].rearrange("p c t -> p (c t)"))
                dump("KT1", KTt[1], KTb[:, 1, 0, :, :].rearrange("p g t -> p (g t)"))
                dump("V1", Vt[1], Vb[:, 1, :, :].rearrange("p g d -> p (g d)"))
            if level == 1 and nsb_run == 0:
                cx.barrier()
                raise _Stop()
            for s in range(nsb_run):
                pend = lnA_steps(s + 2) if s + 2 <= NSB else []
                fill = projA_jobs(s + 1)
                while mt_pending:
                    mt_pending.pop(0)()
                attention(s, pend, fill)
                while fill:
                    fill.pop(0)()
                pooling(s)
                merge(s)
                run_steps_a(pend, len(pend))
                if DEBUG and s == 0:
                    dump("OT0", OTt, OT[:, :, :].rearrange("p c t -> p (c t)"))
                    dump("PL0", PLt, PL[:, :, :].rearrange("p g t -> p (g t)"))
                    dump("mt", mTl[0][0], mTl[0][1][:, :, :].rearrange("p c t -> p (c t)"))
            while mt_pending:
                mt_pending.pop(0)()
            cx.barrier()
            if level == 1:
                raise _Stop()

        sw.__exit__(None, None, None)

        with ExitStack() as sbc:
            W1 = sbuf(sbc, "W1", [128, 8, DFF], BF16)
            W1T = Tile("W1")
            W2 = sbuf(sbc, "W2", [128, 32, D], BF16)
            W2T = Tile("W2")
            lng = sbuf(sbc, "lng", [128, D], F32)
            lnb = sbuf(sbc, "lnb", [128, D], F32)
            lnT = Tile("lnT")

            def bcast_row(ri, dst, dstT, scale):
                cx.dma(SP, dst[:, :], g_scr[ri:ri + 1, :].partition_broadcast(128), dstT, load=True)
                if scale != 1.0:
                    cx.op(ACT, lambda: nc.scalar.mul(out=dst[:, :], in_=dst[:, :], mul=scale), [dstT], [dstT])

            with ExitStack() as sb_:
                Wout = sbuf(sb_, "Wout", [128, 8, D], BF16)
                WoutTk = [Tile("Wout%d" % i) for i in range(8)]
                g1b = sbuf(sb_, "g1b", [128, D], F32)
                g1bT = Tile("g1b")
                stg = [(Tile("stg%d" % i), sbuf(sb_, "stg%d" % i, [128, D], F32)) for i in range(2)]
                mll = [(Tile("ml%d" % i), sbuf(sb_, "ml%d" % i, [128, 8, SBW], BF16)) for i in range(2)]
                NXB = 5
                xbl = [(Tile("xb%d" % i), sbuf(sb_, "xb%d" % i, [128, D], F32)) for i in range(NXB)]
                bcast_row(0, g1b, g1bT, 0.5)
                cx.dma(SP, lng[:, :], ln1_g[0:1, :].partition_broadcast(128), lnT, load=True)
                cx.dma(SP, lnb[:, :], ln1_b[0:1, :].partition_broadcast(128), lnT, load=True)
                w_out_v = w_out.rearrange("(kc p) n -> p kc n", p=128)
                cx.dma(SP, mll[0][1][:, :, :].rearrange("p c t -> p (c t)"), mt_scr[0], mll[0][0], load=True)
                for kc in range(8):
                    st_t, st_ = stg[kc % 2]
                    cx.dma(SP, st_[:, :], w_out_v[:, kc, :], st_t, load=True)
                    if kc % 2 == 0:
                        cx.op(DVE, lambda: nc.vector.tensor_tensor(out=Wout[:, kc, :], in0=st_[:, :], in1=g1b[:, :], op=ALU.mult), [st_t, g1bT], [WoutTk[kc]])
                    else:
                        cx.op(POOL, lambda: nc.gpsimd.tensor_tensor(out=Wout[:, kc, :], in0=st_[:, :], in1=g1b[:, :], op=ALU.mult), [st_t, g1bT], [WoutTk[kc]])
                w1_v = w_mlp_in.rearrange("(kc p) n -> p kc n", p=128)
                w2_v = w_mlp_out.rearrange("(kc p) n -> p kc n", p=128)
                w2_jobs = list(range(32))
                w1_jobs = [(kc, q) for kc in range(8) for q in range(4)]
                stg_ctr = [8]

                def w_job():
                    for _ in range(1):
                        if w1_jobs:
                            kc, q = w1_jobs.pop(0)
                            st_t, st_ = stg[stg_ctr[0] % 2]
                            stg_ctr[0] += 1
                            cx.dma(SP, st_[:, :], w1_v[:, kc, q * 1024:(q + 1) * 1024], st_t, load=True)
                            cx.op(ACT, lambda: nc.scalar.copy(out=W1[:, kc, q * 1024:(q + 1) * 1024], in_=st_[:, :]), [st_t], [W1T])
                        elif w2_jobs:
                            c = w2_jobs.pop(0)
                            st_t, st_ = stg[stg_ctr[0] % 2]
                            stg_ctr[0] += 1
                            cx.dma(SP, st_[:, :], w2_v[:, c, :], st_t, load=True)
                            cx.op(ACT, lambda: nc.scalar.copy(out=W2[:, c, :], in_=st_[:, :]), [st_t], [W2T])

                projB = Ring(banks[0:8])
                xb_ctr = [0]

                def blockB_steps(s_, i, mlt, ml):
                    r0 = (s_ * NB + i) * 128
                    xt, xb = xbl[xb_ctr[0] % NXB]
                    xb_ctr[0] += 1
                    st = {}

                    def head():
                        cx.dma(SP, xb[:, :], x_main[r0:r0 + 128, :], xt, load=True)
                        for hf in range(2):
                            pt, pb = projB.next()
                            for kc in range(8):
                                cx.op(PE, lambda: nc.tensor.matmul(pb[:, 0:512], lhsT=ml[:, kc, i * 128:(i + 1) * 128], rhs=Wout[:, kc, hf * 512:(hf + 1) * 512],
                                                                   start=(kc == 0), stop=(kc == 7)), [mlt, WoutTk[kc]], [pt])
                            cx.op(DVE, lambda: nc.vector.scalar_tensor_tensor(out=xb[:, hf * 512:(hf + 1) * 512], in0=xb[:, hf * 512:(hf + 1) * 512], scalar=ALPHA,
                                                                              in1=pb[:, 0:512], op0=ALU.mult, op1=ALU.add), [xt, pt], [xt])

                    def s_stats():
                        st["ln"] = ln_stats(xt, xb)

                    def s_norm():
                        t_, rstd, nmr = st["ln"]
                        cx.op(ACT, lambda: nc.scalar.activation(out=xb[:, :], in_=xb[:, :], func=AF.Identity, bias=nmr, scale=rstd), [xt, t_], [xt])

                    def s_mul():
                        cx.op(DVE, lambda: nc.vector.tensor_tensor(out=xb[:, 0:512], in0=xb[:, 0:512], in1=lng[:, 0:512], op=ALU.mult), [xt, lnT], [xt])
                        cx.op(POOL, lambda: nc.gpsimd.tensor_tensor(out=xb[:, 512:1024], in0=xb[:, 512:1024], in1=lng[:, 512:1024], op=ALU.mult), [xt, lnT], [xt])

                    def s_add():
                        cx.op(POOL, lambda: nc.gpsimd.tensor_tensor(out=xb[:, :], in0=xb[:, :], in1=lnb[:, :], op=ALU.add), [xt, lnT], [xt])

                    def s_store():
                        cx.dma(SP, x1_scr[r0:r0 + 128, :], xb[:, :], xt, load=False)
                        if DEBUG and s_ == 0:
                            cx.dma(SP, dbg["x1"][i * 128:(i + 1) * 128, :], xb[:, :], xt, load=False)

                    return head, [s_stats, None, None, s_norm, None, s_mul, None, s_add, None, None, None, s_store]

                active = []
                for s in range(NSB):
                    mlt, ml = mll[s % 2]
                    if s > 0:
                        cx.dma(SP, ml[:, :, :].rearrange("p c t -> p (c t)"), mt_scr[s], mlt, load=True)
                    for i in range(NB):
                        head, steps = blockB_steps(s, i, mlt, ml)
                        head()
                        for st_ in active:
                            for _ in range(3):
                                if st_:
                                    f_ = st_.pop(0)
                                    if f_ is not None:
                                        f_()
                        active = [a for a in active if a]
                        active.append(steps)
                        w_job()
                        w_job()
                while active:
                    for st_ in active:
                        if st_:
                            f_ = st_.pop(0)
                            if f_ is not None:
                                f_()
                    active = [a for a in active if a]
                while w2_jobs or w1_jobs:
                    w_job()
                cx.barrier()
                if level == 2:
                    raise _Stop()

            with ExitStack() as sc:
                cx.dma(SP, lng[:, :], ln2_g[0:1, :].partition_broadcast(128), lnT, load=True)
                cx.dma(SP, lnb[:, :], ln2_b[0:1, :].partition_broadcast(128), lnT, load=True)
                g2b = sbuf(sc, "g2b", [128, D], F32)
                g2bT = Tile("g2b")
                bcast_row(1, g2b, g2bT, 1.0)
                zr = Ring([(Tile("zt%d" % i), sbuf(sc, "zt%d" % i, [128, 512], F32)) for i in range(4)])
                NXC = 6
                xcl = [(Tile("xc%d" % i), sbuf(sc, "xc%d" % i, [128, D], F32)) for i in range(NXC)]
                xhc = [(Tile("xhc%d" % i), sbuf(sc, "xhc%d" % i, [128, D], BF16)) for i in range(2)]
                u2l = [(Tile("u2%d" % i), sbuf(sc, "u2%d" % i, [128, 8, SBW], BF16)) for i in range(2)]
                rlr = Ring([(Tile("rl%d" % i), sbuf(sc, "rl%d" % i, [128, SBW], F32)) for i in range(4)])
                hr = Ring([(Tile("h%d" % i), sbuf(sc, "h%d" % i, [128, SBW], BF16)) for i in range(6)])
                projC = Ring(banks[0:4])
                acc = banks[4:8]
                xc_ctr = [0]
                loaded = {}

                def load_sb(s):
                    for i in range(NB):
                        r0 = (s * NB + i) * 128
                        xt, xc = xcl[xc_ctr[0] % NXC]
                        xc_ctr[0] += 1
                        cx.dma(SP, xc[:, :], x1_scr[r0:r0 + 128, :], xt, load=True)
                        loaded[(s, i)] = (xt, xc)

                load_sb(0)
                xh_c = [0]
                from collections import deque
                DSK = 3

                def interleave(lists):
                    out_ = []
                    k = 0
                    while any(k < len(l_) for l_ in lists):
                        for l_ in lists:
                            if k < len(l_):
                                out_.append(l_[k])
                        k += 1
                    return out_

                def lnmod_steps(s_, stride=14):
                    u2t, u2 = u2l[s_ % 2]
                    steps = []
                    for i in range(NB):
                        xt, xc = loaded[(s_, i)]
                        stt_ = {}

                        def f_stats(xt=xt, xc=xc, stt_=stt_):
                            stt_["ln"] = ln_stats(xt, xc)

                        def f_hat(xt=xt, xc=xc, stt_=stt_):
                            t_, rstd, nmr = stt_["ln"]
                            ht, htile = xhc[xh_c[0] % 2]
                            xh_c[0] += 1
                            stt_["h"] = (ht, htile)
                            cx.op(ACT, lambda: nc.scalar.activation(out=htile[:, :], in_=xc[:, :], func=AF.Identity, bias=nmr, scale=rstd), [xt, t_], [ht])

                        def f_tr(stt_=stt_):
                            ht, htile = stt_["h"]
                            TRt, TRb = projC.next()
                            TRv = TRb[:, :].bitcast(BF16)
                            stt_["tr"] = (TRt, TRv)
                            for c in range(8):
                                cx.op(PE, lambda: nc.tensor.transpose(TRv[:, c * 128:(c + 1) * 128], htile[:, c * 128:(c + 1) * 128], ident), [ht, cstT], [TRt])

                        def f_mod(i=i, stt_=stt_):
                            TRt, TRv = stt_["tr"]
                            for c in range(8):
                                cx.op(DVE, lambda: nc.vector.tensor_scalar(out=u2[:, c, i * 128:(i + 1) * 128], in0=TRv[:, c * 128:(c + 1) * 128],
                                                                           scalar1=modT[:, 32 + c, 0:1], scalar2=modT[:, 24 + c, 0:1],
                                                                           op0=ALU.mult, op1=ALU.add), [TRt, modTt], [u2t])
                        steps.append((f_stats, f_hat, f_tr, f_mod))
                    slots = [None] * (stride * len(steps) + 20)
                    for b_, (a_, h_, t_, m_) in enumerate(steps):
                        o_ = stride * b_
                        slots[o_], slots[o_ + 11], slots[o_ + 15], slots[o_ + 17] = a_, h_, t_, m_
                    return slots

                def fin_steps(s_):
                    z_steps, steps_all = [], []
                    for i in range(NB):
                        steps = []
                        xt, xc = loaded.pop((s_, i))
                        r0 = (s_ * NB + i) * 128
                        stt_ = {}
                        for hf in range(2):
                            zst = {}

                            def f_z(i=i, hf=hf, zst=zst):
                                at, ab = acc[i * 2 + hf]
                                zt_, zz = zr.next()
                                zst["z"] = (zt_, zz)
                                cx.op(DVE, lambda: nc.vector.tensor_tensor(out=zz[:, :], in0=ab[:, 0:512], in1=g2b[:, hf * 512:(hf + 1) * 512], op=ALU.mult), [at, g2bT], [zt_])

                            def f_z2(hf=hf, xt=xt, xc=xc, zst=zst):
                                zt_, zz = zst["z"]
                                cx.op(DVE, lambda: nc.vector.scalar_tensor_tensor(out=xc[:, hf * 512:(hf + 1) * 512], in0=xc[:, hf * 512:(hf + 1) * 512], scalar=ALPHA,
                                                                                  in1=zz[:, :], op0=ALU.mult, op1=ALU.add), [xt, zt_], [xt])
                            z_steps.append(f_z)
                            steps.append(f_z2)
                            steps.append(None)

                        def f_stats(xt=xt, xc=xc, stt_=stt_):
                            stt_["ln"] = ln_stats(xt, xc)

                        def f_norm(xt=xt, xc=xc, stt_=stt_):
                            t_, rstd, nmr = stt_["ln"]
                            cx.op(ACT, lambda: nc.scalar.activation(out=xc[:, :], in_=xc[:, :], func=AF.Identity, bias=nmr, scale=rstd), [xt, t_], [xt])

                        def f_mul(xt=xt, xc=xc):
                            cx.op(DVE, lambda: nc.vector.tensor_tensor(out=xc[:, 0:512], in0=xc[:, 0:512], in1=lng[:, 0:512], op=ALU.mult), [xt, lnT], [xt])
                            cx.op(POOL, lambda: nc.gpsimd.tensor_tensor(out=xc[:, 512:1024], in0=xc[:, 512:1024], in1=lng[:, 512:1024], op=ALU.mult), [xt, lnT], [xt])

                        def f_add(xt=xt, xc=xc):
                            cx.op(POOL, lambda: nc.gpsimd.tensor_tensor(out=xc[:, :], in0=xc[:, :], in1=lnb[:, :], op=ALU.add), [xt, lnT], [xt])

                        def f_store(xt=xt, xc=xc, r0=r0):
                            cx.dma(SP, out[r0:r0 + 128, :], xc[:, :], xt, load=False)
                        steps += [f_stats] + [None] * 5 + [f_norm] + [None] * 2 + [f_mul] + [None] * 2 + [f_add] + [None] * 4 + [f_store]
                        steps_all.append(steps)
                    return z_steps, interleave(steps_all)

                def run_steps(q, n):
                    for _ in range(n):
                        if q:
                            f = q.popleft()
                            if f is not None:
                                f()

                q_pre = deque(lnmod_steps(0, stride=1))
                run_steps(q_pre, len(q_pre))
                pend = deque()
                for s in range(NSB):
                    if s + 1 < NSB:
                        load_sb(s + 1)
                    u2t, u2 = u2l[s % 2]
                    if s + 1 < NSB:
                        pend.extend(lnmod_steps(s + 1))
                    hs = {}
                    for c in range(32 + DSK):
                        if c < 32:
                            pt, pb = projC.next()
                            for kc in range(8):
                                cx.op(PE, lambda: nc.tensor.matmul(pb[:, 0:SBW], lhsT=W1[:, kc, c * 128:(c + 1) * 128], rhs=u2[:, kc, :],
                                                                   start=(kc == 0), stop=(kc == 7)), [W1T, u2t], [pt])
                            rt_, rl = rlr.next()
                            cx.op(ACT, lambda: nc.scalar.activation(out=rl[:, :], in_=pb[:, 0:SBW], func=AF.Relu), [pt], [rt_])
                            ht_, hh = hr.next()
                            cx.op(DVE, lambda: nc.vector.tensor_tensor(out=hh[:, :], in0=pb[:, 0:SBW], in1=rl[:, :], op=ALU.mult), [pt, rt_], [ht_])
                            hs[c] = (ht_, hh)
                        cc = c - DSK
                        if cc >= 0:
                            ht_, hh = hs.pop(cc)
                            for i in range(NB):
                                for hf in range(2):
                                    at, ab = acc[i * 2 + hf]
                                    cx.op(PE, lambda: nc.tensor.matmul(ab[:, 0:512], lhsT=hh[:, i * 128:(i + 1) * 128], rhs=W2[:, cc, hf * 512:(hf + 1) * 512],
                                                                       start=(cc == 0), stop=(cc == 31)), [ht_, W2T], [at])
                        run_steps(pend, 3)
                    z_steps, f_steps = fin_steps(s)
                    if s + 1 < NSB:
                        for f in z_steps:
                            f()
                        rest = deque(f_steps)
                        run_steps(pend, len(pend))
                        pend = rest
                    else:
                        for f in z_steps:
                            f()
                        run_steps(pend, len(pend))
                        pend = deque(f_steps)
                        run_steps(pend, len(pend))
                cx.barrier([SP])


def _const_pack(core):
    half = core % 2
    cst = np.zeros((128, C_COLS), np.float32)
    cst[:, C_ID:C_ID + 128] = np.eye(128, dtype=np.float32)
    cst[:, C_ID4:C_ID4 + 512] = np.tile(np.eye(128, dtype=np.float32), (1, 4))
    qi = np.arange(128)[:, None]
    ki = np.arange(128)[None, :]
    kk = np.arange(128)[:, None]
    qq = np.arange(128)[None, :]
    prev_mid = np.where(kk >= qq, 1.0, 0.0).astype(np.float32)
    next_mid = np.where(kk <= qq, 1.0, 0.0).astype(np.float32)
    allneg = np.zeros((128, 128), np.float32)
    masks = [allneg if half == 0 else prev_mid, prev_mid, next_mid, allneg if half == 1 else next_mid]
    for i, m in enumerate(masks):
        cst[:, C_MASK + i * 128:C_MASK + (i + 1) * 128] = m
    pm = np.zeros((128, 128), np.float32)
    for m_ in range(128):
        d = m_ % 64
        partner = d + 16 if (d % 32) < 16 else d - 16
        pm[(m_ // 64) * 64 + partner, m_] = 1.0
    cst[:, C_PERM:C_PERM + 128] = pm
    start = half * TOK
    for var in range(3):
        j = 0 if var == 0 else (NBLK - 1 if var == 2 else 5)
        base = start + j * 128
        for g, w in enumerate((2, 4, 8, 16)):
            T = base + np.arange(128)
            lo = np.clip(T - w // 2, 0, SEQ)
            hi = np.clip(T + w // 2, 0, SEQ)
            cnt = (hi - lo).astype(np.float32)
            for rel in range(3):
                Tp = base + (rel - 1) * 128 + np.arange(128)
                inwin = (Tp[:, None] >= lo[None, :]) & (Tp[:, None] < hi[None, :])
                A = np.where(inwin, 1.0 / cnt[None, :], 0.0).astype(np.float32)
                if rel == 1:
                    A = A - np.eye(128, dtype=np.float32)
                o = C_AMAT + ((var * 3 + rel) * 4 + g) * 128
                cst[:, o:o + 128] = A
    return cst


def _rope_tab(core):
    half = core % 2
    t = half * TOK - 128 + np.arange(NEXT * 128)
    t = np.clip(t, 0, SEQ - 1)
    rows = (t // 64).astype(np.float64)
    cols = (t % 64).astype(np.float64)
    inv = 1.0 / (10000.0 ** (np.arange(16, dtype=np.float64) / 16.0))
    tab = np.zeros((128, 2, NEXT * 128), np.float32)
    for p in range(128):
        d = p % 64
        pos = rows if d < 32 else cols
        i = d % 16
        sign = -1.0 if (d % 32) < 16 else 1.0
        ang = pos * inv[i]
        tab[p, 0] = np.cos(ang).astype(np.float32)
        tab[p, 1] = (sign * np.sin(ang)).astype(np.float32)
    return tab


_NC_CACHE = {}


def make_in_maps(x, c, ctx, c_ctx, w_ada, b_ada, w_in, w_attn_branch, w_pool, pool_scale, attn_sink, w_out,
                 ln1_g, ln1_b, w_mlp_in, w_mlp_out, ln2_g, ln2_b, cores=None):
    f = lambda a: np.ascontiguousarray(np.asarray(a, dtype=np.float32))
    x, c, ctx, c_ctx = f(x), f(c), f(ctx), f(c_ctx)
    w_ada, b_ada, w_in = f(w_ada)[0], f(b_ada), f(w_in)[0]
    cols = []
    cols.append(w_in[:, 2048:4096])
    for cc in range(8):
        g, r = cc // 2, cc % 2
        h0, h1 = 4 * g + r, 4 * g + 2 + r
        cols.append(w_in[:, h0 * 64:(h0 + 1) * 64])
        cols.append(w_in[:, h1 * 64:(h1 + 1) * 64])
    for g in range(4):
        kg = w_in[:, 1024 + g * 64:1024 + (g + 1) * 64]
        cols.append(kg)
        cols.append(kg)
    cols.append(w_in[:, 1280:1536])
    cols.append(w_in[:, 1536:2048])
    w_in_p = np.ascontiguousarray(np.concatenate(cols, axis=1))
    assert w_in_p.shape == (D, WIN_COLS)
    b_adaT = np.ascontiguousarray(b_ada[0].reshape(48, 128).T)
    shared = {
        "w_ada": w_ada, "b_adaT": b_adaT, "b_ada": b_ada, "w_in_p": w_in_p,
        "w_ab": f(w_attn_branch)[0], "w_pool": f(w_pool)[0], "pool_scale": f(pool_scale),
        "attn_sink": f(attn_sink), "w_out": f(w_out)[0], "ln1_g": f(ln1_g), "ln1_b": f(ln1_b),
        "w_mlp_in": f(w_mlp_in)[0], "w_mlp_out": f(w_mlp_out)[0], "ln2_g": f(ln2_g), "ln2_b": f(ln2_b),
    }
    in_maps = []
    for core in (range(NCORES) if cores is None else cores):
        b, half = core // 2, core % 2
        t0 = half * TOK
        halo = np.zeros((256, D), np.float32)
        if half == 1:
            halo[0:128] = x[b, t0 - 128:t0]
        else:
            halo[128:256] = x[b, t0 + TOK:t0 + TOK + 128]
        cc2 = np.stack([c[b], c_ctx], axis=1)
        ccT = np.ascontiguousarray(cc2.reshape(8, 128, 2).transpose(1, 0, 2))
        m = dict(shared)
        m.update({
            "x_main": np.ascontiguousarray(x[b, t0:t0 + TOK]), "x_halo": halo, "ctx_in": np.ascontiguousarray(ctx[b]),
            "ccT": ccT, "rope_tab": _rope_tab(core), "cst": _const_pack(core),
        })
        in_maps.append(m)
    return in_maps


def kernel(x, c, ctx, c_ctx, w_ada, b_ada, w_in, w_attn_branch, w_pool, pool_scale, attn_sink, w_out,
           ln1_g, ln1_b, w_mlp_in, w_mlp_out, ln2_g, ln2_b):
    in_maps = make_in_maps(x, c, ctx, c_ctx, w_ada, b_ada, w_in, w_attn_branch, w_pool, pool_scale, attn_sink, w_out,
                           ln1_g, ln1_b, w_mlp_in, w_mlp_out, ln2_g, ln2_b)
    if "nc" not in _NC_CACHE:
        _NC_CACHE["nc"] = build_program()
    res = run_bass_kernel_spmd(_NC_CACHE["nc"], in_maps, core_ids=list(range(NCORES)))
    _NC_CACHE["last"] = res
    outp = np.empty((4, SEQ, D), np.float32)
    for core in range(NCORES):
        b, half = core // 2, core % 2
        outp[b, half * TOK:(half + 1) * TOK] = res.results[core]["out"]
    return outp
```

```python
import math
from contextlib import ExitStack

import numpy as np
import concourse.bass as bass
import concourse.mybir as mybir
from concourse.bass_utils import run_bass_kernel_spmd

F32 = mybir.dt.float32
BF16 = mybir.dt.bfloat16
AF = mybir.ActivationFunctionType
ALU = mybir.AluOpType

D = 1024
SEQ = 8192
NCORES = 8
TOK = 4096
NBLK = 32
NB = 2
SBW = NB * 128
NSB = NBLK // NB
NEXT = NBLK + 2
HEADS = 16
HD = 64
CTX = 256
DFF = 4096
ALPHA = 2.0 ** 0.25
LN_EPS = 1e-6
NEG = -30000.0
GA_OFF, GP_OFF, Q_OFF, K_OFF, V_OFF, P_OFF, WIN_COLS = 0, 1024, 2048, 3072, 3584, 3840, 4352
C_ID, C_ID4, C_MASK, C_PERM, C_AMAT = 0, 128, 640, 1152, 1280
C_COLS = C_AMAT + 36 * 128
RING = 6
EPOCH = 20000
DEBUG = False


class Eng:
    def __init__(self, ctx, name, h, is_pe=False):
        self.ctx, self.name, self.h, self.is_pe = ctx, name, h, is_pe
        self.sems = []
        self.count = 0
        self.known = {}

    def sem_for(self, seq):
        ep = (seq - 1) // EPOCH
        while len(self.sems) <= ep:
            self.sems.append(self.ctx.es.enter_context(self.ctx.nc.semaphore("s_%s_%d" % (self.name, len(self.sems)))))
        return self.sems[ep], seq - ep * EPOCH, ep


class Tile:
    def __init__(self, name, psum=False, multi=False):
        self.name = name
        self.psum = psum
        self.multi = multi
        self.writers = {}
        self.readers = {}
        self.dsem = None
        self.dcnt = 0
        self.dw = 0
        self.da = 0


class Ctx:
    def __init__(self, nc, es):
        self.nc, self.es = nc, es
        self.pe = Eng(self, "pe", nc.tensor, True)
        self.act = Eng(self, "act", nc.scalar)
        self.dve = Eng(self, "dve", nc.vector)
        self.pool = Eng(self, "pool", nc.gpsimd)
        self.sp = Eng(self, "sp", nc.sync)
        self.engs = [self.pe, self.act, self.dve, self.pool, self.sp]
        self.dma_tiles = []

    def wait_eng(self, eng, e, n):
        if n <= 0:
            return
        if e is eng and eng.is_pe:
            return
        sem, val, ep = e.sem_for(n)
        key = (e.name, ep)
        if eng.known.get(key, 0) >= val:
            return
        eng.h.wait_ge(sem, val)
        eng.known[key] = val

    def wait_dma(self, eng, t, val):
        if val <= 0:
            return
        key = ("d", id(t))
        if eng.known.get(key, 0) >= val:
            return
        eng.h.wait_ge(t.dsem, val)
        eng.known[key] = val

    def _pre(self, eng, reads, writes):
        for t in reads:
            for e, n in t.writers.items():
                self.wait_eng(eng, e, n)
            if t.psum:
                for e, n in t.readers.items():
                    if e is not eng:
                        self.wait_eng(eng, e, n)
            self.wait_dma(eng, t, t.dw)
        for t in writes:
            if not t.multi:
                for e, n in t.writers.items():
                    self.wait_eng(eng, e, n)
            for e, n in t.readers.items():
                self.wait_eng(eng, e, n)
            self.wait_dma(eng, t, t.da)

    def op(self, eng, fn, reads=(), writes=()):
        self._pre(eng, reads, writes)
        ins = fn()
        eng.count += 1
        seq = eng.count
        sem, _, _ = eng.sem_for(seq)
        ins.then_inc(sem, 1)
        for t in reads:
            if t not in writes:
                t.readers[eng] = seq
        for t in writes:
            if t.multi:
                t.writers[eng] = seq
            else:
                t.writers = {eng: seq}
                t.readers = {}
        return ins

    def dma(self, q, out_ap, in_ap, tile, load, extra_reads=(), **kw):
        if tile.dsem is None:
            tile.dsem = self.es.enter_context(self.nc.semaphore("d_%s" % tile.name))
            self.dma_tiles.append(tile)
        if load:
            self._pre(q, extra_reads, [tile])
        else:
            self._pre(q, [tile] + list(extra_reads), [])
        ins = q.h.dma_start(out=out_ap, in_=in_ap, **kw)
        tile.dcnt += 16
        ins.then_inc(tile.dsem, 16)
        if load:
            tile.dw = tile.da = tile.dcnt
            tile.writers = {}
            tile.readers = {}
        else:
            tile.da = tile.dcnt
        return ins

    def barrier(self, engs=None):
        engs = engs or self.engs
        for e in engs:
            for f in self.engs:
                if f is not self.sp and f is not e:
                    self.wait_eng(e, f, f.count)
            for t in self.dma_tiles:
                self.wait_dma(e, t, t.dcnt)


class Ring:
    def __init__(self, items):
        self.items, self.i = items, 0

    def next(self):
        it = self.items[self.i % len(self.items)]
        self.i += 1
        return it


class _Stop(Exception):
    pass


def build_program(level=3, nsb_run=NSB):
    nc = bass.Bass("TRN2", target_bir_lowering=False)
    try:
        _build(nc, level, nsb_run)
    except _Stop:
        pass
    return nc


STOPAT = None
VARIANT = None
ROPE_ADD_DVE = False


def _build(nc, level, nsb_run):

    def din(name, shape, dt=F32):
        return nc.dram_tensor(name, list(shape), dt, kind="ExternalInput").ap()

    x_main = din("x_main", [TOK, D])
    x_halo = din("x_halo", [256, D])
    ctx_in = din("ctx_in", [CTX, D])
    ccT_in = din("ccT", [128, 8, 2])
    w_ada = din("w_ada", [D, 6 * D])
    b_adaT_in = din("b_adaT", [128, 48])
    b_ada = din("b_ada", [1, 6 * D])
    w_in_p = din("w_in_p", [D, WIN_COLS])
    w_ab = din("w_ab", [D, D])
    w_pool = din("w_pool", [4, 128, 256])
    pool_scale = din("pool_scale", [1, D])
    attn_sink = din("attn_sink", [1, HEADS])
    w_out = din("w_out", [D, D])
    ln1_g = din("ln1_g", [1, D])
    ln1_b = din("ln1_b", [1, D])
    w_mlp_in = din("w_mlp_in", [D, DFF])
    w_mlp_out = din("w_mlp_out", [DFF, D])
    ln2_g = din("ln2_g", [1, D])
    ln2_b = din("ln2_b", [1, D])
    rope_tab = din("rope_tab", [128, 2, NEXT * 128])
    cst_in = din("cst", [128, C_COLS])
    out = nc.dram_tensor("out", [TOK, D], F32, kind="ExternalOutput").ap()
    mt_scr = nc.dram_tensor("mt_scr", [NSB, 128, 8 * SBW], BF16, kind="Internal").ap()
    x1_scr = nc.dram_tensor("x1_scr", [TOK, D], F32, kind="Internal").ap()
    dbg = {}
    if DEBUG:
        dbg["modT"] = nc.dram_tensor("dbg_modT", [128, 96], F32, kind="ExternalOutput").ap()
        dbg["uT0"] = nc.dram_tensor("dbg_uT0", [128, 8 * SBW], BF16, kind="ExternalOutput").ap()
        dbg["QT0"] = nc.dram_tensor("dbg_QT0", [128, 8 * SBW], BF16, kind="ExternalOutput").ap()
        dbg["KT1"] = nc.dram_tensor("dbg_KT1", [128, 512], BF16, kind="ExternalOutput").ap()
        dbg["V1"] = nc.dram_tensor("dbg_V1", [128, 4 * 65], BF16, kind="ExternalOutput").ap()
        dbg["OT0"] = nc.dram_tensor("dbg_OT0", [128, 8 * SBW], BF16, kind="ExternalOutput").ap()
        dbg["PL0"] = nc.dram_tensor("dbg_PL0", [128, 4 * SBW], BF16, kind="ExternalOutput").ap()
        dbg["mt"] = nc.dram_tensor("dbg_mt", [128, 8 * SBW], BF16, kind="ExternalOutput").ap()
        dbg["x1"] = nc.dram_tensor("dbg_x1", [256, D], F32, kind="ExternalOutput").ap()

    with ExitStack() as es:
        cx = Ctx(nc, es)
        PE, ACT, DVE, POOL, SP = cx.pe, cx.act, cx.dve, cx.pool, cx.sp

        def sbuf(stack, name, shape, dt):
            return stack.enter_context(nc.sbuf_tensor(name, list(shape), dt))

        banks = []
        for i in range(8):
            t = es.enter_context(nc.psum_tensor("bank%d" % i, [128, 512], F32))
            banks.append((Tile("bank%d" % i, psum=True), t))

        cst = sbuf(es, "cst_sb", [128, C_COLS], BF16)
        cstT = Tile("cst")
        ident = cst[:, C_ID:C_ID + 128]
        ident4 = cst[:, C_ID4:C_ID4 + 512]
        perm = cst[:, C_PERM:C_PERM + 128]

        def mask_ap(i):
            return cst[:, C_MASK + i * 128:C_MASK + (i + 1) * 128]

        def amat_ap(var, rel, g):
            o = C_AMAT + ((var * 3 + rel) * 4 + g) * 128
            return cst[:, o:o + 128]

        modT = sbuf(es, "modT", [128, 48, 2], F32)
        modTt = Tile("modT")
        g_scr = nc.dram_tensor("g_scr", [2, D], F32, kind="Internal").ap()
        mhalf = sbuf(es, "mhalf", [128, 1], F32)
        expsink = sbuf(es, "expsink", [128, HEADS], F32)
        miscT = Tile("misc")
        dbg_stage = None

        def dump(name, tile, ap):
            if DEBUG and name in dbg:
                cx.dma(SP, dbg[name], ap, tile, load=False)

        ln_sm = []
        for i in range(8):
            ln_sm.append((Tile("lnsm%d" % i), sbuf(es, "lnst%d" % i, [128, 2, 6], F32), sbuf(es, "lnmv%d" % i, [128, 4], F32)))
        ln_ring = Ring(ln_sm)

        def ln_stats(xt, xap):
            t, st, mv = ln_ring.next()
            cx.op(DVE, lambda: nc.vector.bn_stats(out=st[:, 0, :], in_=xap[:, 0:512]), [xt], [t])
            cx.op(DVE, lambda: nc.vector.bn_stats(out=st[:, 1, :], in_=xap[:, 512:1024]), [xt, t], [t])
            cx.op(DVE, lambda: nc.vector.bn_aggr(out=mv[:, 0:2], in_=st[:, :, :].rearrange("p a b -> p (a b)")), [t], [t])
            cx.op(POOL, lambda: nc.gpsimd.tensor_scalar(out=mv[:, 2:3], in0=mv[:, 1:2], scalar1=LN_EPS, scalar2=None, op0=ALU.add), [t], [t])
            cx.op(POOL, lambda: nc.gpsimd.tensor_tensor(out=mv[:, 2:3], in0=mv[:, 2:3], in1=mhalf[:, :], op=ALU.pow), [t, miscT], [t])
            cx.op(POOL, lambda: nc.gpsimd.tensor_tensor(out=mv[:, 3:4], in0=mv[:, 0:1], in1=mv[:, 2:3], op=ALU.mult), [t], [t])
            cx.op(POOL, lambda: nc.gpsimd.tensor_scalar(out=mv[:, 3:4], in0=mv[:, 3:4], scalar1=-1.0, scalar2=None, op0=ALU.mult), [t], [t])
            return t, mv[:, 2:3], mv[:, 3:4]

        def cast_load(dst_fn, src_fn, ncols, tile):
            c0 = 0
            while c0 < ncols:
                w = min(2048, ncols - c0)
                cx.dma(POOL, dst_fn(c0, w), src_fn(c0, w), tile, load=True)
                c0 += w

        cast_load(lambda c0, w: cst[:, c0:c0 + w], lambda c0, w: cst_in[:, c0:c0 + w], C_COLS, cstT)
        cx.op(POOL, lambda: nc.gpsimd.memset(mhalf[:, :], -0.5), [], [miscT])
        cx.dma(SP, expsink[:, :], attn_sink[0:1, :].partition_broadcast(128), miscT, load=True)
        cx.op(ACT, lambda: nc.scalar.activation(out=expsink[:, :], in_=expsink[:, :], func=AF.Exp), [miscT], [miscT])

        sw = ExitStack()
        sw.__enter__()
        Win = sbuf(sw, "Win", [128, 8, WIN_COLS], BF16)
        WinT = Tile("Win", multi=True)
        Wab = sbuf(sw, "Wab", [128, 8, D], BF16)
        WabT = Tile("Wab", multi=True)
        Wpl = sbuf(sw, "Wpl", [128, 4, 256], BF16)
        WplT = Tile("Wpl")
        with ExitStack() as s0:
            ccT = sbuf(s0, "ccT_sb", [128, 8, 2], F32)
            cth = sbuf(s0, "cth", [128, 8, 2], F32)
            siluT = sbuf(s0, "siluT", [128, 8, 2], BF16)
            badaT = sbuf(s0, "badaT", [128, 48], F32)
            brow = sbuf(s0, "brow", [1, 2, D], F32)
            grow = sbuf(s0, "grow", [1, 2, D], F32)
            growT = Tile("grow")
            s0T = Tile("s0")
            wsl = [(Tile("wada%d" % i, multi=True), sbuf(s0, "wada%d" % i, [128, 8, 512], BF16)) for i in range(2)]
            stq = Ring([(Tile("stq%d" % i), sbuf(s0, "stq%d" % i, [128, 2048], F32)) for i in range(5)])
            cast_engs = Ring([DVE, ACT])

            def stage_cast(dst_ap, src_ap, dtile, shape3=None):
                st_t, st = stq.next()
                w = 1
                for d_ in dst_ap.shape[1:]:
                    w *= d_
                sv = st[:, 0:w]
                if shape3 is not None:
                    sv = sv.rearrange("p (k n) -> p k n", k=shape3)
                cx.dma(SP, sv, src_ap, st_t, load=True)
                e = cast_engs.next()
                if e is ACT:
                    cx.op(ACT, lambda: nc.scalar.copy(out=dst_ap, in_=sv), [st_t], [dtile])
                elif e is DVE:
                    cx.op(DVE, lambda: nc.vector.tensor_copy(out=dst_ap, in_=sv), [st_t], [dtile])
                else:
                    cx.op(POOL, lambda: nc.gpsimd.tensor_copy(out=dst_ap, in_=sv), [st_t], [dtile])

            cx.dma(SP, ccT[:, :, :], ccT_in[:, :, :], s0T, load=True)
            cx.dma(SP, badaT[:, :], b_adaT_in[:, :], s0T, load=True)
            cx.dma(SP, brow[:, 0, :], b_ada[0:1, 2 * D:3 * D], s0T, load=True)
            cx.dma(SP, brow[:, 1, :], b_ada[0:1, 5 * D:6 * D], s0T, load=True)
            cx.op(ACT, lambda: nc.scalar.activation(out=cth[:, :, :], in_=ccT[:, :, :], func=AF.Tanh, scale=0.5), [s0T], [s0T])
            cx.op(DVE, lambda: nc.vector.scalar_tensor_tensor(out=cth[:, :, :], in0=cth[:, :, :], scalar=1.0, in1=ccT[:, :, :], op0=ALU.add, op1=ALU.mult), [s0T], [s0T])
            cx.op(DVE, lambda: nc.vector.tensor_scalar(out=siluT[:, :, :], in0=cth[:, :, :], scalar1=0.5, scalar2=None, op0=ALU.mult), [s0T], [s0T])
            w_ada_v = w_ada.rearrange("(kc p) n -> p kc n", p=128)
            w_in_v = w_in_p.rearrange("(kc p) n -> p kc n", p=128)
            w_ab_v = w_ab.rearrange("(kc p) n -> p kc n", p=128)
            bmod_t, bmod = banks[0]
            brw_t, brw = banks[1]

            def load_win():
                for kc in range(8):
                    c0 = 0
                    while c0 < WIN_COLS:
                        w = min(2048, WIN_COLS - c0)
                        stage_cast(Win[:, kc, c0:c0 + w], w_in_v[:, kc, c0:c0 + w], WinT)
                        c0 += w

            def load_wab():
                for kc in range(0, 8, 2):
                    stage_cast(Wab[:, kc:kc + 2, :], w_ab_v[:, kc:kc + 2, :], WabT, shape3=2)

            for pc in range(12):
                if pc == 4:
                    load_win()
                wt, wtile = wsl[pc % 2]
                for h in range(2):
                    stage_cast(wtile[:, :, h * 256:(h + 1) * 256], w_ada_v[:, :, pc * 512 + h * 256:pc * 512 + (h + 1) * 256], wt, shape3=8)
                for mm in range(4):
                    m = pc * 4 + mm
                    for kc in range(8):
                        cx.op(PE, lambda: nc.tensor.matmul(bmod[:, 2 * m:2 * m + 2], lhsT=wtile[:, kc, mm * 128:(mm + 1) * 128],
                                                           rhs=siluT[:, kc, :], start=(kc == 0), stop=(kc == 7)),
                              [wt, s0T], [bmod_t])
                if pc in (4, 5, 10, 11):
                    for kc in range(8):
                        cx.op(PE, lambda: nc.tensor.matmul(brw[0:2, :], lhsT=siluT[:, kc, :], rhs=wtile[:, kc, :],
                                                           start=(kc == 0), stop=(kc == 7)), [wt, s0T], [brw_t])
                    bi = 0 if pc < 6 else 1
                    hf = pc % 2
                    cx.op(DVE, lambda: nc.vector.tensor_tensor(out=grow[0:1, bi, hf * 512:(hf + 1) * 512], in0=brw[0:1, :],
                                                               in1=brow[0:1, bi, hf * 512:(hf + 1) * 512], op=ALU.add),
                          [brw_t, s0T], [growT])
            load_wab()
            cx.op(DVE, lambda: nc.vector.tensor_tensor(out=modT[:, :, :], in0=bmod[:, 0:96].rearrange("p (m j) -> p m j", j=2),
                                                       in1=badaT[:, :].unsqueeze(2).to_broadcast([128, 48, 2]), op=ALU.add),
                  [bmod_t, s0T], [modTt])
            cx.op(DVE, lambda: nc.vector.tensor_scalar(out=modT[:, 8:16, :], in0=modT[:, 8:16, :], scalar1=1.0, scalar2=None, op0=ALU.add), [modTt], [modTt])
            cx.op(DVE, lambda: nc.vector.tensor_scalar(out=modT[:, 32:40, :], in0=modT[:, 32:40, :], scalar1=1.0, scalar2=None, op0=ALU.add), [modTt], [modTt])
            dump("modT", modTt, modT[:, :, :].rearrange("p m j -> p (m j)"))
            cx.dma(SP, g_scr[0:1, :], grow[0:1, 0, :], growT, load=False)
            cx.dma(SP, g_scr[1:2, :], grow[0:1, 1, :], growT, load=False)
            wpf = sbuf(s0, "wpf", [128, 4, 256], F32)
            psb = sbuf(s0, "psb", [128, D], F32)
            s1T = Tile("s1")
            cx.dma(SP, wpf[:, :, :], w_pool.rearrange("g c n -> c g n"), s1T, load=True)
            cx.dma(SP, psb[:, :], pool_scale[0:1, :].partition_broadcast(128), s1T, load=True)
            cx.op(DVE, lambda: nc.vector.tensor_tensor(out=Wpl[:, :, :], in0=wpf[:, :, :],
                                                       in1=psb[:, :].rearrange("p (g n) -> p g n", g=4), op=ALU.mult), [s1T], [WplT])
            cx.barrier()
            if level == 0:
                sw.__exit__(None, None, None)
                raise _Stop()

        with ExitStack() as sa:
            def stopat(label):
                if STOPAT == label:
                    cx.barrier()
                    raise _Stop()

            stopat("w")
            NXS = 2
            xsl = [(Tile("xs%d" % i), sbuf(sa, "xs%d" % i, [128, D], F32)) for i in range(NXS)]
            xhl = [(Tile("xh%d" % i), sbuf(sa, "xh%d" % i, [128, D], BF16)) for i in range(2)]
            uTl = [(Tile("uT%d" % i), sbuf(sa, "uT%d" % i, [128, 8, SBW], BF16)) for i in range(3)]
            QTl = [(Tile("QT%d" % i), sbuf(sa, "QT%d" % i, [128, 8, SBW], BF16)) for i in range(2)]
            KTb = sbuf(sa, "KTb", [128, RING, 2, 4, 128], BF16)
            KTt = [Tile("KT%d" % i) for i in range(RING)]
            Vb = sbuf(sa, "Vb", [128, RING, 4, 65], BF16)
            Vt = [Tile("V%d" % i) for i in range(RING)]
            Pb = sbuf(sa, "Pb", [128, RING, 512], BF16)
            Pt = [Tile("P%d" % i) for i in range(RING)]
            KcT = sbuf(sa, "KcT", [128, 2, 4, CTX], BF16)
            KcTt = Tile("KcT")
            Vcb = sbuf(sa, "Vcb", [128, 2, 4, 65], BF16)
            Vct = [Tile("Vc%d" % i) for i in range(2)]
            NPT = 10
            PTb = sbuf(sa, "PTb", [128, NPT, 512], BF16)
            PTr = Ring([(Tile("PT%d" % i), PTb[:, i, :]) for i in range(NPT)])
            Obl = [(Tile("Ob%d" % i), sbuf(sa, "Ob%d" % i, [128, HEADS, HD], BF16)) for i in range(2)]
            OTt, OT = Tile("OT"), sbuf(sa, "OT", [128, 8, SBW], BF16)
            PLt, PL = Tile("PL"), sbuf(sa, "PL", [128, 4, SBW], BF16)
            mTl = [(Tile("mT%d" % i), sbuf(sa, "mT%d" % i, [128, 8, SBW], BF16)) for i in range(2)]
            zbr = Ring([(Tile("zb%d" % i), sbuf(sa, "zb%d" % i, [128, SBW], BF16)) for i in range(3)])
            t1r = Ring([(Tile("t1%d" % i), sbuf(sa, "t1%d" % i, [128, SBW], F32)) for i in range(3)])
            t2r = Ring([(Tile("t2%d" % i), sbuf(sa, "t2%d" % i, [128, SBW], F32)) for i in range(2)])
            rtl = [(Tile("rt%d" % i), sbuf(sa, "rt%d" % i, [128, 2, SBW], F32)) for i in range(3)]
            tgr = Ring([(Tile("tg%d" % i), sbuf(sa, "tg%d" % i, [128, SBW], F32)) for i in range(4)])
            u1r = Ring([(Tile("u1%d" % i), sbuf(sa, "u1%d" % i, [128, SBW], F32)) for i in range(4)])
            denr = Ring([(Tile("den%d" % i), sbuf(sa, "den%d" % i, [128, 8], F32)) for i in range(2)])
            proj = Ring(banks[0:4])
            STr = Ring(banks[4:6])
            Or = Ring(banks[6:8])

            for i in range(RING):
                cx.op(POOL, lambda: nc.gpsimd.memset(Vb[:, i, :, 64:65], 1.0), [], [Vt[i]])
                cx.op(POOL, lambda: nc.gpsimd.memset(KTb[:, i, :, :, :], 0.0), [], [KTt[i]])
            cx.op(POOL, lambda: nc.gpsimd.memset(KcT[:, :, :, :], 0.0), [], [KcTt])
            for i in range(2):
                cx.op(POOL, lambda: nc.gpsimd.memset(Vcb[:, i, :, 64:65], 1.0), [], [Vct[i]])

            def x_rows(e):
                if e == 0:
                    return x_halo[0:128, :]
                if e == NEXT - 1:
                    return x_halo[128:256, :]
                return x_main[(e - 1) * 128:e * 128, :]

            xs_ctr = [0]
            pending_x = {}

            def issue_x(key, src_ap):
                i = xs_ctr[0] % NXS
                xs_ctr[0] += 1
                t, tl = xsl[i]
                cx.dma(SP, tl[:, :], src_ap, t, load=True)
                pending_x[key] = (t, tl)

            xh_ctr = [0]

            def ln_steps(key, uTt, uT, col0, j):
                stt_ = {}

                def f_stats():
                    xt, xtile = pending_x.pop(key)
                    stt_["x"] = (xt, xtile)
                    stt_["ln"] = ln_stats(xt, xtile)

                def f_hat():
                    xt, xtile = stt_["x"]
                    st, rstd, nmr = stt_["ln"]
                    ht, htile = xhl[xh_ctr[0] % 2]
                    xh_ctr[0] += 1
                    stt_["h"] = (ht, htile)
                    cx.op(ACT, lambda: nc.scalar.activation(out=htile[:, :], in_=xtile[:, :], func=AF.Identity, bias=nmr, scale=rstd), [xt, st], [ht])

                def f_tr():
                    ht, htile = stt_["h"]
                    TRt, TRb = proj.next()
                    TRv = TRb[:, :].bitcast(BF16)
                    stt_["tr"] = (TRt, TRv)
                    for c in range(8):
                        cx.op(PE, lambda: nc.tensor.transpose(TRv[:, c * 128:(c + 1) * 128], htile[:, c * 128:(c + 1) * 128], ident), [ht, cstT], [TRt])

                def f_mod():
                    TRt, TRv = stt_["tr"]
                    for c in range(8):
                        cx.op(DVE, lambda: nc.vector.tensor_scalar(out=uT[:, c, col0:col0 + 128], in0=TRv[:, c * 128:(c + 1) * 128],
                                                                   scalar1=modT[:, 8 + c, j:j + 1], scalar2=modT[:, c, j:j + 1],
                                                                   op0=ALU.mult, op1=ALU.add), [TRt, modTt], [uTt])
                def f_trmod():
                    f_tr()
                    f_mod()
                return [f_stats, None, f_hat, None, f_trmod, None, None, None]

            def ln_modulate(key, uTt, uT, col0, j):
                for f in ln_steps(key, uTt, uT, col0, j):
                    if f is not None:
                        f()

            def proj_fm(uTt, uT, n, off):
                pt, pb = proj.next()
                for kc in range(8):
                    cx.op(PE, lambda: nc.tensor.matmul(pb[:, 0:n], lhsT=Win[:, kc, off:off + 128], rhs=uT[:, kc, 0:n],
                                                       start=(kc == 0), stop=(kc == 7)), [WinT, uTt], [pt])
                return pt, pb

            def rope_front(uTt, uT, n, off, rtt, rt):
                pt, pb = proj_fm(uTt, uT, n, off)
                zt, zb = zbr.next()
                cx.op(ACT, lambda: nc.scalar.copy(out=zb[:, 0:n], in_=pb[:, 0:n]), [pt], [zt])
                t1t, t1 = t1r.next()
                cx.op(DVE, lambda: nc.vector.tensor_tensor(out=t1[:, 0:n], in0=pb[:, 0:n], in1=rt[:, 0, 0:n], op=ALU.mult), [pt, rtt], [t1t])
                return (zt, zb, t1t, t1)

            def rope_back(st_, n, rtt, rt, dsts):
                zt, zb, t1t, t1 = st_
                p2t, p2b = proj.next()
                cx.op(PE, lambda: nc.tensor.matmul(p2b[:, 0:n], lhsT=perm, rhs=zb[:, 0:n], start=True, stop=True), [zt, cstT], [p2t])
                t2t, t2 = t2r.next()
                cx.op(DVE, lambda: nc.vector.tensor_tensor(out=t2[:, 0:n], in0=p2b[:, 0:n], in1=rt[:, 1, 0:n], op=ALU.mult), [p2t, rtt], [t2t])
                for (dt_, dap, c0, w, p0, p1) in dsts:
                    cx.op(POOL, lambda: nc.gpsimd.tensor_tensor(out=dap, in0=t1[p0:p1, c0:c0 + w], in1=t2[p0:p1, c0:c0 + w], op=ALU.add), [t1t, t2t], [dt_])

            def rope_many(uTt, uT, n, rtt, rt, jobs):
                prev = None
                for (off, dsts) in jobs:
                    cur = (rope_front(uTt, uT, n, off, rtt, rt), dsts)
                    if prev is not None:
                        rope_back(prev[0], n, rtt, rt, prev[1])
                    prev = cur
                if prev is not None:
                    rope_back(prev[0], n, rtt, rt, prev[1])

            def v_block(uTt, uT, col0, vt, vap):
                pt, pb = proj.next()
                for kc in range(8):
                    cx.op(PE, lambda: nc.tensor.matmul(pb[:, 0:256], lhsT=uT[:, kc, col0:col0 + 128], rhs=Win[:, kc, V_OFF:V_OFF + 256],
                                                       start=(kc == 0), stop=(kc == 7)), [WinT, uTt], [pt])
                cx.op(ACT, lambda: nc.scalar.copy(out=vap, in_=pb[:, 0:256].rearrange("p (g d) -> p g d", g=4)), [pt], [vt])

            def p_block(uTt, uT, col0, ptile, pap):
                pt, pb = proj.next()
                for kc in range(8):
                    cx.op(PE, lambda: nc.tensor.matmul(pb[:, 0:512], lhsT=uT[:, kc, col0:col0 + 128], rhs=Win[:, kc, P_OFF:P_OFF + 512],
                                                       start=(kc == 0), stop=(kc == 7)), [WinT, uTt], [pt])
                cx.op(DVE, lambda: nc.vector.tensor_copy(out=pap, in_=pb[:, 0:512]), [pt], [ptile])

            uct, uc = uTl[1]
            for i in range(2):
                issue_x(("c", i), ctx_in[i * 128:(i + 1) * 128, :])
            stopat("ms")
            def run_zip(lists):
                k = 0
                while any(k < len(l_) for l_ in lists):
                    for l_ in lists:
                        if k < len(l_) and l_[k] is not None:
                            l_[k]()
                    k += 1

            run_zip([ln_steps(("c", i), uct, uc, i * 128, 1) for i in range(2)])
            stopat("ctxln")

            def ctx_jobs():
                jobs = []
                for g in range(4):
                    def fk(g=g):
                        pt, pb = proj_fm(uct, uc, CTX, K_OFF + g * 128)
                        cx.op(ACT, lambda: nc.scalar.copy(out=KcT[0:64, 0, g, :], in_=pb[0:64, 0:CTX]), [pt], [KcTt])
                        cx.op(ACT, lambda: nc.scalar.copy(out=KcT[64:128, 1, g, :], in_=pb[64:128, 0:CTX]), [pt], [KcTt])
                    jobs.append(fk)
                for i in range(2):
                    jobs.append(lambda i=i: v_block(uct, uc, i * 128, Vct[i], Vcb[:, i, :, 0:64]))
                return jobs

            def sb_blocks(s):
                if s == -1:
                    return [0]
                if s == NSB:
                    return [NEXT - 1]
                return [1 + NB * s + i for i in range(NB)]

            def issue_loads(s):
                bl = sb_blocks(s)
                for e in bl:
                    issue_x(("x", e), x_rows(e))
                rtt, rt = rtl[(s + 3) % 3]
                n = len(bl) * 128
                cx.dma(SP, rt[:, :, 0:n], rope_tab[:, :, bl[0] * 128:bl[0] * 128 + n], rtt, load=True)

            def lnA_steps(s):
                bl = sb_blocks(s)
                uTt, uT = uTl[(s + 3) % 3]
                steps = [lambda: issue_loads(s)]
                for i, e in enumerate(bl):
                    steps += ln_steps(("x", e), uTt, uT, i * 128, 0)
                return steps

            def projA_jobs(s):
                bl = sb_blocks(s)
                n = len(bl) * 128
                main = 0 <= s < NSB
                uTt, uT = uTl[(s + 3) % 3]
                rtt, rt = rtl[(s + 3) % 3]
                hold = {"prev": None}
                jobs = []

                def rope_job(off, dsts):
                    def f():
                        cur = (rope_front(uTt, uT, n, off, rtt, rt), dsts)
                        if hold["prev"] is not None:
                            rope_back(hold["prev"][0], n, rtt, rt, hold["prev"][1])
                        hold["prev"] = cur
                    return f

                def rope_flush():
                    if hold["prev"] is not None:
                        rope_back(hold["prev"][0], n, rtt, rt, hold["prev"][1])
                        hold["prev"] = None

                for g in range(4):
                    dsts = []
                    for i, e in enumerate(bl):
                        dsts.append((KTt[e % RING], KTb[0:64, e % RING, 0, g, :], i * 128, 128, 0, 64))
                        dsts.append((KTt[e % RING], KTb[64:128, e % RING, 1, g, :], i * 128, 128, 64, 128))
                    jobs.append(rope_job(K_OFF + g * 128, dsts))
                jobs.append(rope_flush)
                for i, e in enumerate(bl):
                    jobs.append(lambda i=i, e=e: v_block(uTt, uT, i * 128, Vt[e % RING], Vb[:, e % RING, :, 0:64]))
                for i, e in enumerate(bl):
                    jobs.append(lambda i=i, e=e: p_block(uTt, uT, i * 128, Pt[e % RING], Pb[:, e % RING, :]))
                if main:
                    qt, q = QTl[s % 2]
                    for c in range(8):
                        jobs.append(rope_job(Q_OFF + c * 128, [(qt, q[:, c, 0:n], 0, n, 0, 128)]))
                    jobs.append(rope_flush)
                return jobs

            def projA(s):
                for f in projA_jobs(s):
                    f()

            def pooling(s):
                for i, e in enumerate(sb_blocks(s)):
                    j = e - 1
                    var = 0 if j == 0 else (2 if j == NBLK - 1 else 1)
                    pt, pb = proj.next()
                    for g in range(4):
                        for rel in range(3):
                            ee = e - 1 + rel
                            cx.op(PE, lambda: nc.tensor.matmul(pb[:, g * 128:(g + 1) * 128], lhsT=Pb[:, ee % RING, g * 128:(g + 1) * 128],
                                                               rhs=amat_ap(var, rel, g), start=(rel == 0), stop=(rel == 2)),
                                  [Pt[ee % RING], cstT], [pt])
                    cx.op(ACT, lambda: nc.scalar.copy(out=PL[:, :, i * 128:(i + 1) * 128], in_=pb[:, 0:512].rearrange("p (g t) -> p g t", g=4)), [pt], [PLt])

            def att_S(s, i, g, between=None):
                qt, q = QTl[s % 2]
                e = sb_blocks(s)[i]
                j = e - 1
                c0 = i * 128
                srcs = [
                    (KTt[(e - 1) % RING], KTb[:, (e - 1) % RING, :, g, :], Vt[(e - 1) % RING], Vb[:, (e - 1) % RING, g, :], mask_ap(0 if j == 0 else 1)),
                    (KTt[e % RING], KTb[:, e % RING, :, g, :], Vt[e % RING], Vb[:, e % RING, g, :], None),
                    (KTt[(e + 1) % RING], KTb[:, (e + 1) % RING, :, g, :], Vt[(e + 1) % RING], Vb[:, (e + 1) % RING, g, :], mask_ap(3 if j == NBLK - 1 else 2)),
                    (KcTt, KcT[:, :, g, 0:128], Vct[0], Vcb[:, 0, g, :], None),
                    (KcTt, KcT[:, :, g, 128:256], Vct[1], Vcb[:, 1, g, :], None),
                ]
                pts = []
                for kbi, (kt, kap, vt, vap, mk) in enumerate(srcs):
                    if between is not None and kbi in (2, 4):
                        between()
                    stt, stb = STr.next()
                    cx.op(PE, lambda: nc.tensor.matmul(stb[:, 0:256], lhsT=kap[:, 0, :], rhs=q[:, 2 * g:2 * g + 2, c0:c0 + 128],
                                                       start=True, stop=True, skip_group_check=True), [kt, qt], [stt])
                    cx.op(PE, lambda: nc.tensor.matmul(stb[:, 256:512], lhsT=kap[:, 1, :], rhs=q[:, 2 * g:2 * g + 2, c0:c0 + 128],
                                                       start=True, stop=True, skip_group_check=True), [kt, qt], [stt])
                    ptt, ptb = PTr.next()
                    cx.op(ACT, lambda: nc.scalar.activation(out=ptb, in_=stb[:, 0:512], func=AF.Exp, scale=0.125), [stt], [ptt])
                    if mk is not None:
                        cx.op(POOL, lambda: nc.gpsimd.tensor_tensor(out=ptb.rearrange("p (c q) -> p c q", c=4), in0=ptb.rearrange("p (c q) -> p c q", c=4),
                                                                    in1=mk.unsqueeze(1).to_broadcast([128, 4, 128]), op=ALU.mult), [ptt, cstT], [ptt])
                    pts.append((ptt, ptb, vt, vap))
                return pts

            def att_PV(s, i, g, pts):
                obt, ob = Obl[i % 2]
                ot, obk = Or.next()
                ov = obk[:, 0:260].rearrange("p (c d) -> p c d", d=65)
                for cb in range(4):
                    for kb, (ptt, ptb, vt, vap) in enumerate(pts):
                        cx.op(PE, lambda: nc.tensor.matmul(ov[:, cb, :], lhsT=ptb[:, cb * 128:(cb + 1) * 128], rhs=vap,
                                                           start=(kb == 0), stop=(kb == 4)), [ptt, vt], [ot])
                dt_, den = denr.next()
                cx.op(DVE, lambda: nc.vector.tensor_tensor(out=den[:, 0:4], in0=ov[:, :, 64], in1=expsink[:, 4 * g:4 * g + 4], op=ALU.add), [ot, miscT], [dt_])
                cx.op(DVE, lambda: nc.vector.reciprocal(out=den[:, 4:8], in_=den[:, 0:4]), [dt_], [dt_])
                cx.op(DVE, lambda: nc.vector.tensor_tensor(out=ob[:, 4 * g:4 * g + 4, :], in0=ov[:, :, 0:64],
                                                           in1=den[:, 4:8].unsqueeze(2).to_broadcast([128, 4, 64]), op=ALU.mult), [ot, dt_], [obt])

            def att_T(s, i):
                obt, ob = Obl[i % 2]
                c0 = i * 128
                obf = ob[:, :, :].rearrange("p h d -> p (h d)")
                TRt, TRb = proj.next()
                TRv = TRb[:, :].bitcast(BF16)
                for c in range(8):
                    cx.op(PE, lambda: nc.tensor.transpose(TRv[:, c * 128:(c + 1) * 128], obf[:, c * 128:(c + 1) * 128], ident), [obt, cstT], [TRt])
                cx.op(ACT, lambda: nc.scalar.copy(out=OT[:, :, c0:c0 + 128], in_=TRv[:, :].rearrange("p (c t) -> p c t", c=8)), [TRt], [OTt])

            def attention(s, pend, fill):
                units = [(i, g) for i in range(NB) for g in range(4)]
                prev = None
                nfill = len(fill)
                def one_fill():
                    if fill:
                        fill.pop(0)()

                pend_T = None
                for ui, (i, g) in enumerate(units):
                    if ui == 4:
                        while len(fill) > nfill - 7 and fill:
                            fill.pop(0)()
                    pts = att_S(s, i, g, between=one_fill)
                    if pend_T is not None:
                        pend_T[1] -= 1
                        if pend_T[1] == 0:
                            att_T(s, pend_T[0])
                            pend_T = None
                    if prev is not None:
                        pi, pg, ppts = prev
                        att_PV(s, pi, pg, ppts)
                        if pg == 3:
                            pend_T = [pi, 2]
                    prev = (i, g, pts)
                    if ui < 4:
                        one_fill()
                    run_steps_a(pend, 2)
                pi, pg, ppts = prev
                att_PV(s, pi, pg, ppts)
                for _ in range(3):
                    one_fill()
                att_T(s, pi)

            def run_steps_a(q_, n_):
                for _ in range(n_):
                    if q_:
                        f = q_.pop(0)
                        if f is not None:
                            f()

            mt_pending = []

            def merge(s):
                uTt, uT = uTl[(s + 3) % 3]
                mt, mT = mTl[s % 2]
                for m in range(8):
                    tgs = []
                    for a, off in enumerate((GA_OFF, GP_OFF)):
                        tgt, tg = tgr.next()
                        pt, pb = proj_fm(uTt, uT, SBW, off + m * 128)
                        cx.op(ACT, lambda: nc.scalar.activation(out=tg[:, :], in_=pb[:, 0:SBW], func=AF.Tanh, scale=0.5), [pt], [tgt])
                        tgs.append((tgt, tg))
                    u1t, u1 = u1r.next()
                    u2t_, u2_ = u1r.next()
                    pt, pb = proj.next()
                    for c in range(8):
                        cx.op(PE, lambda: nc.tensor.matmul(pb[:, 0:SBW], lhsT=Wab[:, c, m * 128:(m + 1) * 128], rhs=OT[:, c, :],
                                                           start=(c == 0), stop=(c == 7)), [WabT, OTt], [pt])
                    cx.op(DVE, lambda: nc.vector.scalar_tensor_tensor(out=u1[:, :], in0=tgs[0][1][:, :], scalar=1.0, in1=pb[:, 0:SBW],
                                                                      op0=ALU.add, op1=ALU.mult), [tgs[0][0], pt], [u1t])
                    pt, pb = proj.next()
                    g = m // 2
                    cx.op(PE, lambda: nc.tensor.matmul(pb[:, 0:SBW], lhsT=Wpl[:, g, (m % 2) * 128:(m % 2) * 128 + 128], rhs=PL[:, g, :],
                                                       start=True, stop=True), [WplT, PLt], [pt])
                    cx.op(DVE, lambda: nc.vector.scalar_tensor_tensor(out=u2_[:, :], in0=tgs[1][1][:, :], scalar=1.0, in1=pb[:, 0:SBW],
                                                                      op0=ALU.add, op1=ALU.mult), [tgs[1][0], pt], [u2t_])
                    cx.op(POOL, lambda: nc.gpsimd.tensor_tensor(out=mT[:, m, :], in0=u1[:, :], in1=u2_[:, :], op=ALU.add), [u1t, u2t_], [mt])
                mt_pending.append(lambda: cx.dma(SP, mt_scr[s], mT[:, :, :].rearrange("p c t -> p (c t)"), mt, load=False))

            stopat("ctx")
            def jobs_with(jobs, pend_, k_):
                for jb in jobs:
                    jb()
                    run_steps_a(pend_, k_)

            pend0 = lnA_steps(-1) + lnA_steps(0)
            jobs_with(ctx_jobs(), pend0, 3)
            run_steps_a(pend0, len(pend0))
            pend1 = lnA_steps(1)
            jobs_with(projA_jobs(-1), pend1, 1)
            jobs_with(projA_jobs(0), pend1, 1)
            run_steps_a(pend1, len(pend1))
            if DEBUG:
                dump("uT0", uTl[0][0], uTl[0][1][:, :, :].rearrange("p c t -> p (c t)"))
                dump("QT0", QTl[0][0], QTl[0][1][:, :, :].rearrange("p c t -> p (c t)"))
                dump("KT1", KTt[1], KTb[:, 1, 0, :, :].rearrange("p g t -> p (g t)"))
                dump("V1", Vt[1], Vb[:, 1, :, :].rearrange("p g d -> p (g d)"))
            if level == 1 and nsb_run == 0:
                cx.barrier()
                raise _Stop()
            for s in range(nsb_run):
                pend = lnA_steps(s + 2) if s + 2 <= NSB else []
                fill = projA_jobs(s + 1)
                while mt_pending:
                    mt_pending.pop(0)()
                attention(s, pend, fill)
                while fill:
                    fill.pop(0)()
                pooling(s)
                merge(s)
                run_steps_a(pend, len(pend))
                if DEBUG and s == 0:
                    dump("OT0", OTt, OT[:, :, :].rearrange("p c t -> p (c t)"))
                    dump("PL0", PLt, PL[:, :, :].rearrange("p g t -> p (g t)"))
                    dump("mt", mTl[0][0], mTl[0][1][:, :, :].rearrange("p c t -> p (c t)"))
            while mt_pending:
                mt_pending.pop(0)()
            cx.barrier()
            if level == 1:
                raise _Stop()

        sw.__exit__(None, None, None)

        with ExitStack() as sbc:
            W1 = sbuf(sbc, "W1", [128, 8, DFF], BF16)
            W1T = Tile("W1")
            W2 = sbuf(sbc, "W2", [128, 32, D], BF16)
            W2T = Tile("W2")
            lng = sbuf(sbc, "lng", [128, D], F32)
            lnb = sbuf(sbc, "lnb", [128, D], F32)
            lnT = Tile("lnT")

            def bcast_row(ri, dst, dstT, scale):
                cx.dma(SP, dst[:, :], g_scr[ri:ri + 1, :].partition_broadcast(128), dstT, load=True)
                if scale != 1.0:
                    cx.op(ACT, lambda: nc.scalar.mul(out=dst[:, :], in_=dst[:, :], mul=scale), [dstT], [dstT])

            with ExitStack() as sb_:
                Wout = sbuf(sb_, "Wout", [128, 8, D], BF16)
                WoutTk = [Tile("Wout%d" % i) for i in range(8)]
                g1b = sbuf(sb_, "g1b", [128, D], F32)
                g1bT = Tile("g1b")
                stg = [(Tile("stg%d" % i), sbuf(sb_, "stg%d" % i, [128, D], F32)) for i in range(2)]
                mll = [(Tile("ml%d" % i), sbuf(sb_, "ml%d" % i, [128, 8, SBW], BF16)) for i in range(2)]
                NXB = 5
                xbl = [(Tile("xb%d" % i), sbuf(sb_, "xb%d" % i, [128, D], F32)) for i in range(NXB)]
                bcast_row(0, g1b, g1bT, 0.5)
                cx.dma(SP, lng[:, :], ln1_g[0:1, :].partition_broadcast(128), lnT, load=True)
                cx.dma(SP, lnb[:, :], ln1_b[0:1, :].partition_broadcast(128), lnT, load=True)
                w_out_v = w_out.rearrange("(kc p) n -> p kc n", p=128)
                for kc in range(8):
                    st_t, st_ = stg[kc % 2]
                    cx.dma(SP, st_[:, :], w_out_v[:, kc, :], st_t, load=True)
                    if kc % 2 == 0:
                        cx.op(DVE, lambda: nc.vector.tensor_tensor(out=Wout[:, kc, :], in0=st_[:, :], in1=g1b[:, :], op=ALU.mult), [st_t, g1bT], [WoutTk[kc]])
                    else:
                        cx.op(POOL, lambda: nc.gpsimd.tensor_tensor(out=Wout[:, kc, :], in0=st_[:, :], in1=g1b[:, :], op=ALU.mult), [st_t, g1bT], [WoutTk[kc]])
                w1_v = w_mlp_in.rearrange("(kc p) n -> p kc n", p=128)
                w2_v = w_mlp_out.rearrange("(kc p) n -> p kc n", p=128)
                w2_jobs = list(range(32))
                w1_jobs = [(kc, q) for kc in range(8) for q in range(4)]
                stg_ctr = [8]

                w_pending = []

                def w_job():
                    nxt = None
                    if w1_jobs:
                        kc, q = w1_jobs.pop(0)
                        nxt = (w1_v[:, kc, q * 1024:(q + 1) * 1024], W1[:, kc, q * 1024:(q + 1) * 1024], W1T)
                    elif w2_jobs:
                        c = w2_jobs.pop(0)
                        nxt = (w2_v[:, c, :], W2[:, c, :], W2T)
                    if nxt is not None:
                        st_t, st_ = stg[stg_ctr[0] % 2]
                        stg_ctr[0] += 1
                        cx.dma(ACT, st_[:, :], nxt[0], st_t, load=True)
                        w_pending.append((st_t, st_, nxt[1], nxt[2]))
                    if w_pending and (len(w_pending) > 1 or nxt is None):
                        st_t, st_, dst, dt_ = w_pending.pop(0)
                        cx.op(ACT, lambda: nc.scalar.copy(out=dst, in_=st_[:, :]), [st_t], [dt_])

                projB = Ring(banks[0:8])
                xb_ctr = [0]

                def blockB_steps(s_, i, mlt, ml):
                    r0 = (s_ * NB + i) * 128
                    xt, xb = xbl[xb_ctr[0] % NXB]
                    xb_ctr[0] += 1
                    st = {}

                    def head():
                        cx.dma(SP, xb[:, :], x_main[r0:r0 + 128, :], xt, load=True)
                        for hf in range(2):
                            pt, pb = projB.next()
                            for kc in range(8):
                                cx.op(PE, lambda: nc.tensor.matmul(pb[:, 0:512], lhsT=ml[:, kc, i * 128:(i + 1) * 128], rhs=Wout[:, kc, hf * 512:(hf + 1) * 512],
                                                                   start=(kc == 0), stop=(kc == 7)), [mlt, WoutTk[kc]], [pt])
                            cx.op(DVE, lambda: nc.vector.scalar_tensor_tensor(out=xb[:, hf * 512:(hf + 1) * 512], in0=xb[:, hf * 512:(hf + 1) * 512], scalar=ALPHA,
                                                                              in1=pb[:, 0:512], op0=ALU.mult, op1=ALU.add), [xt, pt], [xt])

                    def s_stats():
                        st["ln"] = ln_stats(xt, xb)

                    def s_norm():
                        t_, rstd, nmr = st["ln"]
                        cx.op(ACT, lambda: nc.scalar.activation(out=xb[:, :], in_=xb[:, :], func=AF.Identity, bias=nmr, scale=rstd), [xt, t_], [xt])

                    def s_mul():
                        cx.op(DVE, lambda: nc.vector.tensor_tensor(out=xb[:, 0:512], in0=xb[:, 0:512], in1=lng[:, 0:512], op=ALU.mult), [xt, lnT], [xt])
                        cx.op(POOL, lambda: nc.gpsimd.tensor_tensor(out=xb[:, 512:1024], in0=xb[:, 512:1024], in1=lng[:, 512:1024], op=ALU.mult), [xt, lnT], [xt])

                    def s_add():
                        cx.op(POOL, lambda: nc.gpsimd.tensor_tensor(out=xb[:, :], in0=xb[:, :], in1=lnb[:, :], op=ALU.add), [xt, lnT], [xt])

                    def s_store():
                        cx.dma(SP, x1_scr[r0:r0 + 128, :], xb[:, :], xt, load=False)
                        if DEBUG and s_ == 0:
                            cx.dma(SP, dbg["x1"][i * 128:(i + 1) * 128, :], xb[:, :], xt, load=False)

                    return head, [s_stats, None, None, s_norm, None, s_mul, None, s_add, None, None, None, s_store]

                active = []
                for s in range(NSB):
                    mlt, ml = mll[s % 2]
                    cx.dma(SP, ml[:, :, :].rearrange("p c t -> p (c t)"), mt_scr[s], mlt, load=True)
                    for i in range(NB):
                        head, steps = blockB_steps(s, i, mlt, ml)
                        head()
                        for st_ in active:
                            for _ in range(3):
                                if st_:
                                    f_ = st_.pop(0)
                                    if f_ is not None:
                                        f_()
                        active = [a for a in active if a]
                        active.append(steps)
                        w_job()
                        w_job()
                while active:
                    for st_ in active:
                        if st_:
                            f_ = st_.pop(0)
                            if f_ is not None:
                                f_()
                    active = [a for a in active if a]
                while w2_jobs or w1_jobs or w_pending:
                    w_job()
                cx.barrier()
                if level == 2:
                    raise _Stop()

            with ExitStack() as sc:
                cx.dma(SP, lng[:, :], ln2_g[0:1, :].partition_broadcast(128), lnT, load=True)
                cx.dma(SP, lnb[:, :], ln2_b[0:1, :].partition_broadcast(128), lnT, load=True)
                g2b = sbuf(sc, "g2b", [128, D], F32)
                g2bT = Tile("g2b")
                bcast_row(1, g2b, g2bT, 1.0)
                zr = Ring([(Tile("zt%d" % i), sbuf(sc, "zt%d" % i, [128, 512], F32)) for i in range(4)])
                NXC = 6
                xcl = [(Tile("xc%d" % i), sbuf(sc, "xc%d" % i, [128, D], F32)) for i in range(NXC)]
                xhc = [(Tile("xhc%d" % i), sbuf(sc, "xhc%d" % i, [128, D], BF16)) for i in range(2)]
                u2l = [(Tile("u2%d" % i), sbuf(sc, "u2%d" % i, [128, 8, SBW], BF16)) for i in range(2)]
                rlr = Ring([(Tile("rl%d" % i), sbuf(sc, "rl%d" % i, [128, SBW], F32)) for i in range(4)])
                hr = Ring([(Tile("h%d" % i), sbuf(sc, "h%d" % i, [128, SBW], BF16)) for i in range(6)])
                projC = Ring(banks[0:4])
                acc = banks[4:8]
                xc_ctr = [0]
                loaded = {}

                def load_sb(s):
                    for i in range(NB):
                        r0 = (s * NB + i) * 128
                        xt, xc = xcl[xc_ctr[0] % NXC]
                        xc_ctr[0] += 1
                        cx.dma(SP, xc[:, :], x1_scr[r0:r0 + 128, :], xt, load=True)
                        loaded[(s, i)] = (xt, xc)

                load_sb(0)
                xh_c = [0]
                from collections import deque
                DSK = 3

                def interleave(lists):
                    out_ = []
                    k = 0
                    while any(k < len(l_) for l_ in lists):
                        for l_ in lists:
                            if k < len(l_):
                                out_.append(l_[k])
                        k += 1
                    return out_

                def lnmod_steps(s_):
                    u2t, u2 = u2l[s_ % 2]
                    steps = []
                    for i in range(NB):
                        xt, xc = loaded[(s_, i)]
                        stt_ = {}

                        def f_stats(xt=xt, xc=xc, stt_=stt_):
                            stt_["ln"] = ln_stats(xt, xc)

                        def f_hat(xt=xt, xc=xc, stt_=stt_):
                            t_, rstd, nmr = stt_["ln"]
                            ht, htile = xhc[xh_c[0] % 2]
                            xh_c[0] += 1
                            stt_["h"] = (ht, htile)
                            cx.op(ACT, lambda: nc.scalar.activation(out=htile[:, :], in_=xc[:, :], func=AF.Identity, bias=nmr, scale=rstd), [xt, t_], [ht])

                        def f_tr(stt_=stt_):
                            ht, htile = stt_["h"]
                            TRt, TRb = projC.next()
                            TRv = TRb[:, :].bitcast(BF16)
                            stt_["tr"] = (TRt, TRv)
                            for c in range(8):
                                cx.op(PE, lambda: nc.tensor.transpose(TRv[:, c * 128:(c + 1) * 128], htile[:, c * 128:(c + 1) * 128], ident), [ht, cstT], [TRt])

                        def f_mod(i=i, stt_=stt_):
                            TRt, TRv = stt_["tr"]
                            for c in range(8):
                                cx.op(DVE, lambda: nc.vector.tensor_scalar(out=u2[:, c, i * 128:(i + 1) * 128], in0=TRv[:, c * 128:(c + 1) * 128],
                                                                           scalar1=modT[:, 32 + c, 0:1], scalar2=modT[:, 24 + c, 0:1],
                                                                           op0=ALU.mult, op1=ALU.add), [TRt, modTt], [u2t])
                        steps.append((f_stats, f_hat, f_tr, f_mod))
                    slots = [None] * (14 * len(steps) + 8)
                    for b_, (a_, h_, t_, m_) in enumerate(steps):
                        o_ = 14 * b_
                        slots[o_], slots[o_ + 11], slots[o_ + 15], slots[o_ + 17] = a_, h_, t_, m_
                    return slots

                def fin_steps(s_):
                    z_steps, steps_all = [], []
                    for i in range(NB):
                        steps = []
                        xt, xc = loaded.pop((s_, i))
                        r0 = (s_ * NB + i) * 128
                        stt_ = {}
                        for hf in range(2):
                            zst = {}

                            def f_z(i=i, hf=hf, zst=zst):
                                at, ab = acc[i * 2 + hf]
                                zt_, zz = zr.next()
                                zst["z"] = (zt_, zz)
                                cx.op(DVE, lambda: nc.vector.tensor_tensor(out=zz[:, :], in0=ab[:, 0:512], in1=g2b[:, hf * 512:(hf + 1) * 512], op=ALU.mult), [at, g2bT], [zt_])

                            def f_z2(hf=hf, xt=xt, xc=xc, zst=zst):
                                zt_, zz = zst["z"]
                                cx.op(DVE, lambda: nc.vector.scalar_tensor_tensor(out=xc[:, hf * 512:(hf + 1) * 512], in0=xc[:, hf * 512:(hf + 1) * 512], scalar=ALPHA,
                                                                                  in1=zz[:, :], op0=ALU.mult, op1=ALU.add), [xt, zt_], [xt])
                            z_steps.append(f_z)
                            steps.append(f_z2)
                            steps.append(None)

                        def f_stats(xt=xt, xc=xc, stt_=stt_):
                            stt_["ln"] = ln_stats(xt, xc)

                        def f_norm(xt=xt, xc=xc, stt_=stt_):
                            t_, rstd, nmr = stt_["ln"]
                            cx.op(ACT, lambda: nc.scalar.activation(out=xc[:, :], in_=xc[:, :], func=AF.Identity, bias=nmr, scale=rstd), [xt, t_], [xt])

                        def f_mul(xt=xt, xc=xc):
                            cx.op(DVE, lambda: nc.vector.tensor_tensor(out=xc[:, 0:512], in0=xc[:, 0:512], in1=lng[:, 0:512], op=ALU.mult), [xt, lnT], [xt])
                            cx.op(POOL, lambda: nc.gpsimd.tensor_tensor(out=xc[:, 512:1024], in0=xc[:, 512:1024], in1=lng[:, 512:1024], op=ALU.mult), [xt, lnT], [xt])

                        def f_add(xt=xt, xc=xc):
                            cx.op(POOL, lambda: nc.gpsimd.tensor_tensor(out=xc[:, :], in0=xc[:, :], in1=lnb[:, :], op=ALU.add), [xt, lnT], [xt])

                        def f_store(xt=xt, xc=xc, r0=r0):
                            cx.dma(SP, out[r0:r0 + 128, :], xc[:, :], xt, load=False)
                        steps += [f_stats] + [None] * 5 + [f_norm] + [None] * 2 + [f_mul] + [None] * 2 + [f_add] + [None] * 4 + [f_store]
                        steps_all.append(steps)
                    return z_steps, interleave(steps_all)

                def run_steps(q, n):
                    for _ in range(n):
                        if q:
                            f = q.popleft()
                            if f is not None:
                                f()

                q_pre = deque(lnmod_steps(0))
                run_steps(q_pre, len(q_pre))
                pend = deque()
                for s in range(NSB):
                    if s + 1 < NSB:
                        load_sb(s + 1)
                    u2t, u2 = u2l[s % 2]
                    if s + 1 < NSB:
                        pend.extend(lnmod_steps(s + 1))
                    hs = {}
                    for c in range(32 + DSK):
                        if c < 32:
                            pt, pb = projC.next()
                            for kc in range(8):
                                cx.op(PE, lambda: nc.tensor.matmul(pb[:, 0:SBW], lhsT=W1[:, kc, c * 128:(c + 1) * 128], rhs=u2[:, kc, :],
                                                                   start=(kc == 0), stop=(kc == 7)), [W1T, u2t], [pt])
                            rt_, rl = rlr.next()
                            cx.op(ACT, lambda: nc.scalar.activation(out=rl[:, :], in_=pb[:, 0:SBW], func=AF.Relu), [pt], [rt_])
                            ht_, hh = hr.next()
                            cx.op(DVE, lambda: nc.vector.tensor_tensor(out=hh[:, :], in0=pb[:, 0:SBW], in1=rl[:, :], op=ALU.mult), [pt, rt_], [ht_])
                            hs[c] = (ht_, hh)
                        cc = c - DSK
                        if cc >= 0:
                            ht_, hh = hs.pop(cc)
                            for i in range(NB):
                                for hf in range(2):
                                    at, ab = acc[i * 2 + hf]
                                    cx.op(PE, lambda: nc.tensor.matmul(ab[:, 0:512], lhsT=hh[:, i * 128:(i + 1) * 128], rhs=W2[:, cc, hf * 512:(hf + 1) * 512],
                                                                       start=(cc == 0), stop=(cc == 31)), [ht_, W2T], [at])
                        run_steps(pend, 3)
                    z_steps, f_steps = fin_steps(s)
                    if s + 1 < NSB:
                        for f in z_steps:
                            f()
                        rest = deque(f_steps)
                        run_steps(pend, len(pend))
                        pend = rest
                    else:
                        for f in z_steps:
                            f()
                        run_steps(pend, len(pend))
                        pend = deque(f_steps)
                        run_steps(pend, len(pend))
                cx.barrier([SP])


def _const_pack(core):
    half = core % 2
    cst = np.zeros((128, C_COLS), np.float32)
    cst[:, C_ID:C_ID + 128] = np.eye(128, dtype=np.float32)
    cst[:, C_ID4:C_ID4 + 512] = np.tile(np.eye(128, dtype=np.float32), (1, 4))
    qi = np.arange(128)[:, None]
    ki = np.arange(128)[None, :]
    kk = np.arange(128)[:, None]
    qq = np.arange(128)[None, :]
    prev_mid = np.where(kk >= qq, 1.0, 0.0).astype(np.float32)
    next_mid = np.where(kk <= qq, 1.0, 0.0).astype(np.float32)
    allneg = np.zeros((128, 128), np.float32)
    masks = [allneg if half == 0 else prev_mid, prev_mid, next_mid, allneg if half == 1 else next_mid]
    for i, m in enumerate(masks):
        cst[:, C_MASK + i * 128:C_MASK + (i + 1) * 128] = m
    pm = np.zeros((128, 128), np.float32)
    for m_ in range(128):
        d = m_ % 64
        partner = d + 16 if (d % 32) < 16 else d - 16
        pm[(m_ // 64) * 64 + partner, m_] = 1.0
    cst[:, C_PERM:C_PERM + 128] = pm
    start = half * TOK
    for var in range(3):
        j = 0 if var == 0 else (NBLK - 1 if var == 2 else 5)
        base = start + j * 128
        for g, w in enumerate((2, 4, 8, 16)):
            T = base + np.arange(128)
            lo = np.clip(T - w // 2, 0, SEQ)
            hi = np.clip(T + w // 2, 0, SEQ)
            cnt = (hi - lo).astype(np.float32)
            for rel in range(3):
                Tp = base + (rel - 1) * 128 + np.arange(128)
                inwin = (Tp[:, None] >= lo[None, :]) & (Tp[:, None] < hi[None, :])
                A = np.where(inwin, 1.0 / cnt[None, :], 0.0).astype(np.float32)
                if rel == 1:
                    A = A - np.eye(128, dtype=np.float32)
                o = C_AMAT + ((var * 3 + rel) * 4 + g) * 128
                cst[:, o:o + 128] = A
    return cst


def _rope_tab(core):
    half = core % 2
    t = half * TOK - 128 + np.arange(NEXT * 128)
    t = np.clip(t, 0, SEQ - 1)
    rows = (t // 64).astype(np.float64)
    cols = (t % 64).astype(np.float64)
    inv = 1.0 / (10000.0 ** (np.arange(16, dtype=np.float64) / 16.0))
    tab = np.zeros((128, 2, NEXT * 128), np.float32)
    for p in range(128):
        d = p % 64
        pos = rows if d < 32 else cols
        i = d % 16
        sign = -1.0 if (d % 32) < 16 else 1.0
        ang = pos * inv[i]
        tab[p, 0] = np.cos(ang).astype(np.float32)
        tab[p, 1] = (sign * np.sin(ang)).astype(np.float32)
    return tab


_NC_CACHE = {}


def make_in_maps(x, c, ctx, c_ctx, w_ada, b_ada, w_in, w_attn_branch, w_pool, pool_scale, attn_sink, w_out,
                 ln1_g, ln1_b, w_mlp_in, w_mlp_out, ln2_g, ln2_b, cores=None):
    f = lambda a: np.ascontiguousarray(np.asarray(a, dtype=np.float32))
    x, c, ctx, c_ctx = f(x), f(c), f(ctx), f(c_ctx)
    w_ada, b_ada, w_in = f(w_ada)[0], f(b_ada), f(w_in)[0]
    cols = []
    cols.append(w_in[:, 2048:4096])
    for cc in range(8):
        g, r = cc // 2, cc % 2
        h0, h1 = 4 * g + r, 4 * g + 2 + r
        cols.append(w_in[:, h0 * 64:(h0 + 1) * 64])
        cols.append(w_in[:, h1 * 64:(h1 + 1) * 64])
    for g in range(4):
        kg = w_in[:, 1024 + g * 64:1024 + (g + 1) * 64]
        cols.append(kg)
        cols.append(kg)
    cols.append(w_in[:, 1280:1536])
    cols.append(w_in[:, 1536:2048])
    w_in_p = np.ascontiguousarray(np.concatenate(cols, axis=1))
    assert w_in_p.shape == (D, WIN_COLS)
    b_adaT = np.ascontiguousarray(b_ada[0].reshape(48, 128).T)
    shared = {
        "w_ada": w_ada, "b_adaT": b_adaT, "b_ada": b_ada, "w_in_p": w_in_p,
        "w_ab": f(w_attn_branch)[0], "w_pool": f(w_pool)[0], "pool_scale": f(pool_scale),
        "attn_sink": f(attn_sink), "w_out": f(w_out)[0], "ln1_g": f(ln1_g), "ln1_b": f(ln1_b),
        "w_mlp_in": f(w_mlp_in)[0], "w_mlp_out": f(w_mlp_out)[0], "ln2_g": f(ln2_g), "ln2_b": f(ln2_b),
    }
    in_maps = []
    for core in (range(NCORES) if cores is None else cores):
        b, half = core // 2, core % 2
        t0 = half * TOK
        halo = np.zeros((256, D), np.float32)
        if half == 1:
            halo[0:128] = x[b, t0 - 128:t0]
        else:
            halo[128:256] = x[b, t0 + TOK:t0 + TOK + 128]
        cc2 = np.stack([c[b], c_ctx], axis=1)
        ccT = np.ascontiguousarray(cc2.reshape(8, 128, 2).transpose(1, 0, 2))
        m = dict(shared)
        m.update({
            "x_main": np.ascontiguousarray(x[b, t0:t0 + TOK]), "x_halo": halo, "ctx_in": np.ascontiguousarray(ctx[b]),
            "ccT": ccT, "rope_tab": _rope_tab(core), "cst": _const_pack(core),
        })
        in_maps.append(m)
    return in_maps


def kernel(x, c, ctx, c_ctx, w_ada, b_ada, w_in, w_attn_branch, w_pool, pool_scale, attn_sink, w_out,
           ln1_g, ln1_b, w_mlp_in, w_mlp_out, ln2_g, ln2_b):
    in_maps = make_in_maps(x, c, ctx, c_ctx, w_ada, b_ada, w_in, w_attn_branch, w_pool, pool_scale, attn_sink, w_out,
                           ln1_g, ln1_b, w_mlp_in, w_mlp_out, ln2_g, ln2_b)
    if "nc" not in _NC_CACHE:
        _NC_CACHE["nc"] = build_program()
    res = run_bass_kernel_spmd(_NC_CACHE["nc"], in_maps, core_ids=list(range(NCORES)))
    _NC_CACHE["last"] = res
    outp = np.empty((4, SEQ, D), np.float32)
    for core in range(NCORES):
        b, half = core // 2, core % 2
        outp[b, half * TOK:(half + 1) * TOK] = res.results[core]["out"]
    return outp
```

```python
import math
from contextlib import ExitStack

import numpy as np
import concourse.bass as bass
import concourse.mybir as mybir
from concourse.bass_utils import run_bass_kernel_spmd

F32 = mybir.dt.float32
BF16 = mybir.dt.bfloat16
AF = mybir.ActivationFunctionType
ALU = mybir.AluOpType

D = 1024
SEQ = 8192
NCORES = 8
TOK = 4096
NBLK = 32
NB = 2
SBW = NB * 128
NSB = NBLK // NB
NEXT = NBLK + 2
HEADS = 16
HD = 64
CTX = 256
DFF = 4096
ALPHA = 2.0 ** 0.25
LN_EPS = 1e-6
NEG = -30000.0
GA_OFF, GP_OFF, Q_OFF, K_OFF, V_OFF, P_OFF, WIN_COLS = 0, 1024, 2048, 3072, 3584, 3840, 4352
C_ID, C_ID4, C_MASK, C_PERM, C_AMAT = 0, 128, 640, 1152, 1280
C_COLS = C_AMAT + 36 * 128
RING = 6
EPOCH = 20000
DEBUG = False


class Eng:
    def __init__(self, ctx, name, h, is_pe=False):
        self.ctx, self.name, self.h, self.is_pe = ctx, name, h, is_pe
        self.sems = []
        self.count = 0
        self.known = {}

    def sem_for(self, seq):
        ep = (seq - 1) // EPOCH
        while len(self.sems) <= ep:
            self.sems.append(self.ctx.es.enter_context(self.ctx.nc.semaphore("s_%s_%d" % (self.name, len(self.sems)))))
        return self.sems[ep], seq - ep * EPOCH, ep


class Tile:
    def __init__(self, name, psum=False, multi=False):
        self.name = name
        self.psum = psum
        self.multi = multi
        self.writers = {}
        self.readers = {}
        self.dsem = None
        self.dcnt = 0
        self.dw = 0
        self.da = 0


class Ctx:
    def __init__(self, nc, es):
        self.nc, self.es = nc, es
        self.pe = Eng(self, "pe", nc.tensor, True)
        self.act = Eng(self, "act", nc.scalar)
        self.dve = Eng(self, "dve", nc.vector)
        self.pool = Eng(self, "pool", nc.gpsimd)
        self.sp = Eng(self, "sp", nc.sync)
        self.engs = [self.pe, self.act, self.dve, self.pool, self.sp]
        self.dma_tiles = []

    def wait_eng(self, eng, e, n):
        if n <= 0:
            return
        if e is eng and eng.is_pe:
            return
        sem, val, ep = e.sem_for(n)
        key = (e.name, ep)
        if eng.known.get(key, 0) >= val:
            return
        eng.h.wait_ge(sem, val)
        eng.known[key] = val

    def wait_dma(self, eng, t, val):
        if val <= 0:
            return
        key = ("d", id(t))
        if eng.known.get(key, 0) >= val:
            return
        eng.h.wait_ge(t.dsem, val)
        eng.known[key] = val

    def _pre(self, eng, reads, writes):
        for t in reads:
            for e, n in t.writers.items():
                self.wait_eng(eng, e, n)
            if t.psum:
                for e, n in t.readers.items():
                    if e is not eng:
                        self.wait_eng(eng, e, n)
            self.wait_dma(eng, t, t.dw)
        for t in writes:
            if not t.multi:
                for e, n in t.writers.items():
                    self.wait_eng(eng, e, n)
            for e, n in t.readers.items():
                self.wait_eng(eng, e, n)
            self.wait_dma(eng, t, t.da)

    def op(self, eng, fn, reads=(), writes=()):
        self._pre(eng, reads, writes)
        ins = fn()
        eng.count += 1
        seq = eng.count
        sem, _, _ = eng.sem_for(seq)
        ins.then_inc(sem, 1)
        for t in reads:
            if t not in writes:
                t.readers[eng] = seq
        for t in writes:
            if t.multi:
                t.writers[eng] = seq
            else:
                t.writers = {eng: seq}
                t.readers = {}
        return ins

    def dma(self, q, out_ap, in_ap, tile, load, extra_reads=(), **kw):
        if tile.dsem is None:
            tile.dsem = self.es.enter_context(self.nc.semaphore("d_%s" % tile.name))
            self.dma_tiles.append(tile)
        if load:
            self._pre(q, extra_reads, [tile])
        else:
            self._pre(q, [tile] + list(extra_reads), [])
        ins = q.h.dma_start(out=out_ap, in_=in_ap, **kw)
        tile.dcnt += 16
        ins.then_inc(tile.dsem, 16)
        if load:
            tile.dw = tile.da = tile.dcnt
            tile.writers = {}
            tile.readers = {}
        else:
            tile.da = tile.dcnt
        return ins

    def barrier(self, engs=None):
        engs = engs or self.engs
        for e in engs:
            for f in self.engs:
                if f is not self.sp and f is not e:
                    self.wait_eng(e, f, f.count)
            for t in self.dma_tiles:
                self.wait_dma(e, t, t.dcnt)


class Ring:
    def __init__(self, items):
        self.items, self.i = items, 0

    def next(self):
        it = self.items[self.i % len(self.items)]
        self.i += 1
        return it


class _Stop(Exception):
    pass


def build_program(level=3, nsb_run=NSB):
    nc = bass.Bass("TRN2", target_bir_lowering=False)
    try:
        _build(nc, level, nsb_run)
    except _Stop:
        pass
    return nc


STOPAT = None
VARIANT = None
ROPE_ADD_DVE = False


def _build(nc, level, nsb_run):

    def din(name, shape, dt=F32):
        return nc.dram_tensor(name, list(shape), dt, kind="ExternalInput").ap()

    x_main = din("x_main", [TOK, D])
    x_halo = din("x_halo", [256, D])
    ctx_in = din("ctx_in", [CTX, D])
    ccT_in = din("ccT", [128, 8, 2])
    w_ada = din("w_ada", [D, 6 * D])
    b_adaT_in = din("b_adaT", [128, 48])
    b_ada = din("b_ada", [1, 6 * D])
    w_in_p = din("w_in_p", [D, WIN_COLS])
    w_ab = din("w_ab", [D, D])
    w_pool = din("w_pool", [4, 128, 256])
    pool_scale = din("pool_scale", [1, D])
    attn_sink = din("attn_sink", [1, HEADS])
    w_out = din("w_out", [D, D])
    ln1_g = din("ln1_g", [1, D])
    ln1_b = din("ln1_b", [1, D])
    w_mlp_in = din("w_mlp_in", [D, DFF])
    w_mlp_out = din("w_mlp_out", [DFF, D])
    ln2_g = din("ln2_g", [1, D])
    ln2_b = din("ln2_b", [1, D])
    rope_tab = din("rope_tab", [128, 2, NEXT * 128])
    cst_in = din("cst", [128, C_COLS])
    out = nc.dram_tensor("out", [TOK, D], F32, kind="ExternalOutput").ap()
    mt_scr = nc.dram_tensor("mt_scr", [NSB, 128, 8 * SBW], BF16, kind="Internal").ap()
    x1_scr = nc.dram_tensor("x1_scr", [TOK, D], F32, kind="Internal").ap()
    dbg = {}
    if DEBUG:
        dbg["modT"] = nc.dram_tensor("dbg_modT", [128, 96], F32, kind="ExternalOutput").ap()
        dbg["uT0"] = nc.dram_tensor("dbg_uT0", [128, 8 * SBW], BF16, kind="ExternalOutput").ap()
        dbg["QT0"] = nc.dram_tensor("dbg_QT0", [128, 8 * SBW], BF16, kind="ExternalOutput").ap()
        dbg["KT1"] = nc.dram_tensor("dbg_KT1", [128, 512], BF16, kind="ExternalOutput").ap()
        dbg["V1"] = nc.dram_tensor("dbg_V1", [128, 4 * 65], BF16, kind="ExternalOutput").ap()
        dbg["OT0"] = nc.dram_tensor("dbg_OT0", [128, 8 * SBW], BF16, kind="ExternalOutput").ap()
        dbg["PL0"] = nc.dram_tensor("dbg_PL0", [128, 4 * SBW], BF16, kind="ExternalOutput").ap()
        dbg["mt"] = nc.dram_tensor("dbg_mt", [128, 8 * SBW], BF16, kind="ExternalOutput").ap()
        dbg["x1"] = nc.dram_tensor("dbg_x1", [256, D], F32, kind="ExternalOutput").ap()

    with ExitStack() as es:
        cx = Ctx(nc, es)
        PE, ACT, DVE, POOL, SP = cx.pe, cx.act, cx.dve, cx.pool, cx.sp

        def sbuf(stack, name, shape, dt):
            return stack.enter_context(nc.sbuf_tensor(name, list(shape), dt))

        banks = []
        for i in range(8):
            t = es.enter_context(nc.psum_tensor("bank%d" % i, [128, 512], F32))
            banks.append((Tile("bank%d" % i, psum=True), t))

        cst = sbuf(es, "cst_sb", [128, C_COLS], BF16)
        cstT = Tile("cst")
        ident = cst[:, C_ID:C_ID + 128]
        ident4 = cst[:, C_ID4:C_ID4 + 512]
        perm = cst[:, C_PERM:C_PERM + 128]

        def mask_ap(i):
            return cst[:, C_MASK + i * 128:C_MASK + (i + 1) * 128]

        def amat_ap(var, rel, g):
            o = C_AMAT + ((var * 3 + rel) * 4 + g) * 128
            return cst[:, o:o + 128]

        modT = sbuf(es, "modT", [128, 48, 2], F32)
        modTt = Tile("modT")
        g_scr = nc.dram_tensor("g_scr", [2, D], F32, kind="Internal").ap()
        mhalf = sbuf(es, "mhalf", [128, 1], F32)
        expsink = sbuf(es, "expsink", [128, HEADS], F32)
        miscT = Tile("misc")
        dbg_stage = None

        def dump(name, tile, ap):
            if DEBUG and name in dbg:
                cx.dma(SP, dbg[name], ap, tile, load=False)

        ln_sm = []
        for i in range(8):
            ln_sm.append((Tile("lnsm%d" % i), sbuf(es, "lnst%d" % i, [128, 2, 6], F32), sbuf(es, "lnmv%d" % i, [128, 4], F32)))
        ln_ring = Ring(ln_sm)

        def ln_stats(xt, xap):
            t, st, mv = ln_ring.next()
            cx.op(DVE, lambda: nc.vector.bn_stats(out=st[:, 0, :], in_=xap[:, 0:512]), [xt], [t])
            cx.op(DVE, lambda: nc.vector.bn_stats(out=st[:, 1, :], in_=xap[:, 512:1024]), [xt, t], [t])
            cx.op(DVE, lambda: nc.vector.bn_aggr(out=mv[:, 0:2], in_=st[:, :, :].rearrange("p a b -> p (a b)")), [t], [t])
            cx.op(POOL, lambda: nc.gpsimd.tensor_scalar(out=mv[:, 2:3], in0=mv[:, 1:2], scalar1=LN_EPS, scalar2=None, op0=ALU.add), [t], [t])
            cx.op(POOL, lambda: nc.gpsimd.tensor_tensor(out=mv[:, 2:3], in0=mv[:, 2:3], in1=mhalf[:, :], op=ALU.pow), [t, miscT], [t])
            cx.op(POOL, lambda: nc.gpsimd.tensor_tensor(out=mv[:, 3:4], in0=mv[:, 0:1], in1=mv[:, 2:3], op=ALU.mult), [t], [t])
            cx.op(POOL, lambda: nc.gpsimd.tensor_scalar(out=mv[:, 3:4], in0=mv[:, 3:4], scalar1=-1.0, scalar2=None, op0=ALU.mult), [t], [t])
            return t, mv[:, 2:3], mv[:, 3:4]

        def cast_load(dst_fn, src_fn, ncols, tile):
            c0 = 0
            while c0 < ncols:
                w = min(2048, ncols - c0)
                cx.dma(POOL, dst_fn(c0, w), src_fn(c0, w), tile, load=True)
                c0 += w

        cast_load(lambda c0, w: cst[:, c0:c0 + w], lambda c0, w: cst_in[:, c0:c0 + w], C_COLS, cstT)
        cx.op(POOL, lambda: nc.gpsimd.memset(mhalf[:, :], -0.5), [], [miscT])
        cx.dma(SP, expsink[:, :], attn_sink[0:1, :].partition_broadcast(128), miscT, load=True)
        cx.op(ACT, lambda: nc.scalar.activation(out=expsink[:, :], in_=expsink[:, :], func=AF.Exp), [miscT], [miscT])

        sw = ExitStack()
        sw.__enter__()
        Win = sbuf(sw, "Win", [128, 8, WIN_COLS], BF16)
        WinT = Tile("Win", multi=True)
        Wab = sbuf(sw, "Wab", [128, 8, D], BF16)
        WabT = Tile("Wab", multi=True)
        Wpl = sbuf(sw, "Wpl", [128, 4, 256], BF16)
        WplT = Tile("Wpl")
        with ExitStack() as s0:
            ccT = sbuf(s0, "ccT_sb", [128, 8, 2], F32)
            cth = sbuf(s0, "cth", [128, 8, 2], F32)
            siluT = sbuf(s0, "siluT", [128, 8, 2], BF16)
            badaT = sbuf(s0, "badaT", [128, 48], F32)
            brow = sbuf(s0, "brow", [1, 2, D], F32)
            grow = sbuf(s0, "grow", [1, 2, D], F32)
            growT = Tile("grow")
            s0T = Tile("s0")
            wsl = [(Tile("wada%d" % i, multi=True), sbuf(s0, "wada%d" % i, [128, 8, 512], BF16)) for i in range(2)]
            stq = Ring([(Tile("stq%d" % i), sbuf(s0, "stq%d" % i, [128, 2048], F32)) for i in range(5)])
            cast_engs = Ring([DVE, ACT])

            def stage_cast(dst_ap, src_ap, dtile, shape3=None):
                st_t, st = stq.next()
                w = 1
                for d_ in dst_ap.shape[1:]:
                    w *= d_
                sv = st[:, 0:w]
                if shape3 is not None:
                    sv = sv.rearrange("p (k n) -> p k n", k=shape3)
                cx.dma(SP, sv, src_ap, st_t, load=True)
                e = cast_engs.next()
                if e is ACT:
                    cx.op(ACT, lambda: nc.scalar.copy(out=dst_ap, in_=sv), [st_t], [dtile])
                elif e is DVE:
                    cx.op(DVE, lambda: nc.vector.tensor_copy(out=dst_ap, in_=sv), [st_t], [dtile])
                else:
                    cx.op(POOL, lambda: nc.gpsimd.tensor_copy(out=dst_ap, in_=sv), [st_t], [dtile])

            cx.dma(SP, ccT[:, :, :], ccT_in[:, :, :], s0T, load=True)
            cx.dma(SP, badaT[:, :], b_adaT_in[:, :], s0T, load=True)
            cx.dma(SP, brow[:, 0, :], b_ada[0:1, 2 * D:3 * D], s0T, load=True)
            cx.dma(SP, brow[:, 1, :], b_ada[0:1, 5 * D:6 * D], s0T, load=True)
            cx.op(ACT, lambda: nc.scalar.activation(out=cth[:, :, :], in_=ccT[:, :, :], func=AF.Tanh, scale=0.5), [s0T], [s0T])
            cx.op(DVE, lambda: nc.vector.scalar_tensor_tensor(out=cth[:, :, :], in0=cth[:, :, :], scalar=1.0, in1=ccT[:, :, :], op0=ALU.add, op1=ALU.mult), [s0T], [s0T])
            cx.op(DVE, lambda: nc.vector.tensor_scalar(out=siluT[:, :, :], in0=cth[:, :, :], scalar1=0.5, scalar2=None, op0=ALU.mult), [s0T], [s0T])
            w_ada_v = w_ada.rearrange("(kc p) n -> p kc n", p=128)
            w_in_v = w_in_p.rearrange("(kc p) n -> p kc n", p=128)
            w_ab_v = w_ab.rearrange("(kc p) n -> p kc n", p=128)
            bmod_t, bmod = banks[0]
            brw_t, brw = banks[1]

            def load_win():
                for kc in range(8):
                    c0 = 0
                    while c0 < WIN_COLS:
                        w = min(2048, WIN_COLS - c0)
                        stage_cast(Win[:, kc, c0:c0 + w], w_in_v[:, kc, c0:c0 + w], WinT)
                        c0 += w

            def load_wab():
                for kc in range(0, 8, 2):
                    stage_cast(Wab[:, kc:kc + 2, :], w_ab_v[:, kc:kc + 2, :], WabT, shape3=2)

            for pc in range(12):
                if pc == 4:
                    load_win()
                wt, wtile = wsl[pc % 2]
                for h in range(2):
                    stage_cast(wtile[:, :, h * 256:(h + 1) * 256], w_ada_v[:, :, pc * 512 + h * 256:pc * 512 + (h + 1) * 256], wt, shape3=8)
                for mm in range(4):
                    m = pc * 4 + mm
                    for kc in range(8):
                        cx.op(PE, lambda: nc.tensor.matmul(bmod[:, 2 * m:2 * m + 2], lhsT=wtile[:, kc, mm * 128:(mm + 1) * 128],
                                                           rhs=siluT[:, kc, :], start=(kc == 0), stop=(kc == 7)),
                              [wt, s0T], [bmod_t])
                if pc in (4, 5, 10, 11):
                    for kc in range(8):
                        cx.op(PE, lambda: nc.tensor.matmul(brw[0:2, :], lhsT=siluT[:, kc, :], rhs=wtile[:, kc, :],
                                                           start=(kc == 0), stop=(kc == 7)), [wt, s0T], [brw_t])
                    bi = 0 if pc < 6 else 1
                    hf = pc % 2
                    cx.op(DVE, lambda: nc.vector.tensor_tensor(out=grow[0:1, bi, hf * 512:(hf + 1) * 512], in0=brw[0:1, :],
                                                               in1=brow[0:1, bi, hf * 512:(hf + 1) * 512], op=ALU.add),
                          [brw_t, s0T], [growT])
            load_wab()
            cx.op(DVE, lambda: nc.vector.tensor_tensor(out=modT[:, :, :], in0=bmod[:, 0:96].rearrange("p (m j) -> p m j", j=2),
                                                       in1=badaT[:, :].unsqueeze(2).to_broadcast([128, 48, 2]), op=ALU.add),
                  [bmod_t, s0T], [modTt])
            cx.op(DVE, lambda: nc.vector.tensor_scalar(out=modT[:, 8:16, :], in0=modT[:, 8:16, :], scalar1=1.0, scalar2=None, op0=ALU.add), [modTt], [modTt])
            cx.op(DVE, lambda: nc.vector.tensor_scalar(out=modT[:, 32:40, :], in0=modT[:, 32:40, :], scalar1=1.0, scalar2=None, op0=ALU.add), [modTt], [modTt])
            dump("modT", modTt, modT[:, :, :].rearrange("p m j -> p (m j)"))
            cx.dma(SP, g_scr[0:1, :], grow[0:1, 0, :], growT, load=False)
            cx.dma(SP, g_scr[1:2, :], grow[0:1, 1, :], growT, load=False)
            wpf = sbuf(s0, "wpf", [128, 4, 256], F32)
            psb = sbuf(s0, "psb", [128, D], F32)
            s1T = Tile("s1")
            cx.dma(SP, wpf[:, :, :], w_pool.rearrange("g c n -> c g n"), s1T, load=True)
            cx.dma(SP, psb[:, :], pool_scale[0:1, :].partition_broadcast(128), s1T, load=True)
            cx.op(DVE, lambda: nc.vector.tensor_tensor(out=Wpl[:, :, :], in0=wpf[:, :, :],
                                                       in1=psb[:, :].rearrange("p (g n) -> p g n", g=4), op=ALU.mult), [s1T], [WplT])
            cx.barrier()
            if level == 0:
                sw.__exit__(None, None, None)
                raise _Stop()

        with ExitStack() as sa:
            def stopat(label):
                if STOPAT == label:
                    cx.barrier()
                    raise _Stop()

            stopat("w")
            NXS = 2
            xsl = [(Tile("xs%d" % i), sbuf(sa, "xs%d" % i, [128, D], F32)) for i in range(NXS)]
            xhl = [(Tile("xh%d" % i), sbuf(sa, "xh%d" % i, [128, D], BF16)) for i in range(2)]
            uTl = [(Tile("uT%d" % i), sbuf(sa, "uT%d" % i, [128, 8, SBW], BF16)) for i in range(3)]
            QTl = [(Tile("QT%d" % i), sbuf(sa, "QT%d" % i, [128, 8, SBW], BF16)) for i in range(2)]
            KTb = sbuf(sa, "KTb", [128, RING, 2, 4, 128], BF16)
            KTt = [Tile("KT%d" % i) for i in range(RING)]
            Vb = sbuf(sa, "Vb", [128, RING, 4, 65], BF16)
            Vt = [Tile("V%d" % i) for i in range(RING)]
            Pb = sbuf(sa, "Pb", [128, RING, 512], BF16)
            Pt = [Tile("P%d" % i) for i in range(RING)]
            KcT = sbuf(sa, "KcT", [128, 2, 4, CTX], BF16)
            KcTt = Tile("KcT")
            Vcb = sbuf(sa, "Vcb", [128, 2, 4, 65], BF16)
            Vct = [Tile("Vc%d" % i) for i in range(2)]
            NPT = 10
            PTb = sbuf(sa, "PTb", [128, NPT, 512], BF16)
            PTr = Ring([(Tile("PT%d" % i), PTb[:, i, :]) for i in range(NPT)])
            Obl = [(Tile("Ob%d" % i), sbuf(sa, "Ob%d" % i, [128, HEADS, HD], BF16)) for i in range(2)]
            OTt, OT = Tile("OT"), sbuf(sa, "OT", [128, 8, SBW], BF16)
            PLt, PL = Tile("PL"), sbuf(sa, "PL", [128, 4, SBW], BF16)
            mTl = [(Tile("mT%d" % i), sbuf(sa, "mT%d" % i, [128, 8, SBW], BF16)) for i in range(2)]
            zbr = Ring([(Tile("zb%d" % i), sbuf(sa, "zb%d" % i, [128, SBW], BF16)) for i in range(3)])
            t1r = Ring([(Tile("t1%d" % i), sbuf(sa, "t1%d" % i, [128, SBW], F32)) for i in range(3)])
            t2r = Ring([(Tile("t2%d" % i), sbuf(sa, "t2%d" % i, [128, SBW], F32)) for i in range(2)])
            rtl = [(Tile("rt%d" % i), sbuf(sa, "rt%d" % i, [128, 2, SBW], F32)) for i in range(3)]
            tgr = Ring([(Tile("tg%d" % i), sbuf(sa, "tg%d" % i, [128, SBW], F32)) for i in range(4)])
            u1r = Ring([(Tile("u1%d" % i), sbuf(sa, "u1%d" % i, [128, SBW], F32)) for i in range(4)])
            denr = Ring([(Tile("den%d" % i), sbuf(sa, "den%d" % i, [128, 8], F32)) for i in range(2)])
            proj = Ring(banks[0:4])
            STr = Ring(banks[4:6])
            Or = Ring(banks[6:8])

            for i in range(RING):
                cx.op(POOL, lambda: nc.gpsimd.memset(Vb[:, i, :, 64:65], 1.0), [], [Vt[i]])
                cx.op(POOL, lambda: nc.gpsimd.memset(KTb[:, i, :, :, :], 0.0), [], [KTt[i]])
            cx.op(POOL, lambda: nc.gpsimd.memset(KcT[:, :, :, :], 0.0), [], [KcTt])
            for i in range(2):
                cx.op(POOL, lambda: nc.gpsimd.memset(Vcb[:, i, :, 64:65], 1.0), [], [Vct[i]])

            def x_rows(e):
                if e == 0:
                    return x_halo[0:128, :]
                if e == NEXT - 1:
                    return x_halo[128:256, :]
                return x_main[(e - 1) * 128:e * 128, :]

            xs_ctr = [0]
            pending_x = {}

            def issue_x(key, src_ap):
                i = xs_ctr[0] % NXS
                xs_ctr[0] += 1
                t, tl = xsl[i]
                cx.dma(SP, tl[:, :], src_ap, t, load=True)
                pending_x[key] = (t, tl)

            xh_ctr = [0]

            def ln_steps(key, uTt, uT, col0, j):
                stt_ = {}

                def f_stats():
                    xt, xtile = pending_x.pop(key)
                    stt_["x"] = (xt, xtile)
                    stt_["ln"] = ln_stats(xt, xtile)

                def f_hat():
                    xt, xtile = stt_["x"]
                    st, rstd, nmr = stt_["ln"]
                    ht, htile = xhl[xh_ctr[0] % 2]
                    xh_ctr[0] += 1
                    stt_["h"] = (ht, htile)
                    cx.op(ACT, lambda: nc.scalar.activation(out=htile[:, :], in_=xtile[:, :], func=AF.Identity, bias=nmr, scale=rstd), [xt, st], [ht])

                def f_tr():
                    ht, htile = stt_["h"]
                    TRt, TRb = proj.next()
                    TRv = TRb[:, :].bitcast(BF16)
                    stt_["tr"] = (TRt, TRv)
                    for c in range(8):
                        cx.op(PE, lambda: nc.tensor.transpose(TRv[:, c * 128:(c + 1) * 128], htile[:, c * 128:(c + 1) * 128], ident), [ht, cstT], [TRt])

                def f_mod():
                    TRt, TRv = stt_["tr"]
                    for c in range(8):
                        cx.op(DVE, lambda: nc.vector.tensor_scalar(out=uT[:, c, col0:col0 + 128], in0=TRv[:, c * 128:(c + 1) * 128],
                                                                   scalar1=modT[:, 8 + c, j:j + 1], scalar2=modT[:, c, j:j + 1],
                                                                   op0=ALU.mult, op1=ALU.add), [TRt, modTt], [uTt])
                def f_trmod():
                    f_tr()
                    f_mod()
                return [f_stats, None, f_hat, None, f_trmod, None, None, None]

            def ln_modulate(key, uTt, uT, col0, j):
                for f in ln_steps(key, uTt, uT, col0, j):
                    if f is not None:
                        f()

            def proj_fm(uTt, uT, n, off):
                pt, pb = proj.next()
                for kc in range(8):
                    cx.op(PE, lambda: nc.tensor.matmul(pb[:, 0:n], lhsT=Win[:, kc, off:off + 128], rhs=uT[:, kc, 0:n],
                                                       start=(kc == 0), stop=(kc == 7)), [WinT, uTt], [pt])
                return pt, pb

            def rope_front(uTt, uT, n, off, rtt, rt):
                pt, pb = proj_fm(uTt, uT, n, off)
                zt, zb = zbr.next()
                cx.op(ACT, lambda: nc.scalar.copy(out=zb[:, 0:n], in_=pb[:, 0:n]), [pt], [zt])
                t1t, t1 = t1r.next()
                cx.op(DVE, lambda: nc.vector.tensor_tensor(out=t1[:, 0:n], in0=pb[:, 0:n], in1=rt[:, 0, 0:n], op=ALU.mult), [pt, rtt], [t1t])
                return (zt, zb, t1t, t1)

            def rope_back(st_, n, rtt, rt, dsts):
                zt, zb, t1t, t1 = st_
                p2t, p2b = proj.next()
                cx.op(PE, lambda: nc.tensor.matmul(p2b[:, 0:n], lhsT=perm, rhs=zb[:, 0:n], start=True, stop=True), [zt, cstT], [p2t])
                t2t, t2 = t2r.next()
                cx.op(DVE, lambda: nc.vector.tensor_tensor(out=t2[:, 0:n], in0=p2b[:, 0:n], in1=rt[:, 1, 0:n], op=ALU.mult), [p2t, rtt], [t2t])
                for (dt_, dap, c0, w, p0, p1) in dsts:
                    cx.op(POOL, lambda: nc.gpsimd.tensor_tensor(out=dap, in0=t1[p0:p1, c0:c0 + w], in1=t2[p0:p1, c0:c0 + w], op=ALU.add), [t1t, t2t], [dt_])

            def rope_many(uTt, uT, n, rtt, rt, jobs):
                prev = None
                for (off, dsts) in jobs:
                    cur = (rope_front(uTt, uT, n, off, rtt, rt), dsts)
                    if prev is not None:
                        rope_back(prev[0], n, rtt, rt, prev[1])
                    prev = cur
                if prev is not None:
                    rope_back(prev[0], n, rtt, rt, prev[1])

            def v_block(uTt, uT, col0, vt, vap):
                pt, pb = proj.next()
                for kc in range(8):
                    cx.op(PE, lambda: nc.tensor.matmul(pb[:, 0:256], lhsT=uT[:, kc, col0:col0 + 128], rhs=Win[:, kc, V_OFF:V_OFF + 256],
                                                       start=(kc == 0), stop=(kc == 7)), [WinT, uTt], [pt])
                cx.op(ACT, lambda: nc.scalar.copy(out=vap, in_=pb[:, 0:256].rearrange("p (g d) -> p g d", g=4)), [pt], [vt])

            def p_block(uTt, uT, col0, ptile, pap):
                pt, pb = proj.next()
                for kc in range(8):
                    cx.op(PE, lambda: nc.tensor.matmul(pb[:, 0:512], lhsT=uT[:, kc, col0:col0 + 128], rhs=Win[:, kc, P_OFF:P_OFF + 512],
                                                       start=(kc == 0), stop=(kc == 7)), [WinT, uTt], [pt])
                cx.op(DVE, lambda: nc.vector.tensor_copy(out=pap, in_=pb[:, 0:512]), [pt], [ptile])

            uct, uc = uTl[1]
            for i in range(2):
                issue_x(("c", i), ctx_in[i * 128:(i + 1) * 128, :])
            stopat("ms")
            def run_zip(lists):
                k = 0
                while any(k < len(l_) for l_ in lists):
                    for l_ in lists:
                        if k < len(l_) and l_[k] is not None:
                            l_[k]()
                    k += 1

            run_zip([ln_steps(("c", i), uct, uc, i * 128, 1) for i in range(2)])
            stopat("ctxln")

            def ctx_jobs():
                jobs = []
                for g in range(4):
                    def fk(g=g):
                        pt, pb = proj_fm(uct, uc, CTX, K_OFF + g * 128)
                        cx.op(ACT, lambda: nc.scalar.copy(out=KcT[0:64, 0, g, :], in_=pb[0:64, 0:CTX]), [pt], [KcTt])
                        cx.op(ACT, lambda: nc.scalar.copy(out=KcT[64:128, 1, g, :], in_=pb[64:128, 0:CTX]), [pt], [KcTt])
                    jobs.append(fk)
                for i in range(2):
                    jobs.append(lambda i=i: v_block(uct, uc, i * 128, Vct[i], Vcb[:, i, :, 0:64]))
                return jobs

            def sb_blocks(s):
                if s == -1:
                    return [0]
                if s == NSB:
                    return [NEXT - 1]
                return [1 + NB * s + i for i in range(NB)]

            def issue_loads(s):
                bl = sb_blocks(s)
                for e in bl:
                    issue_x(("x", e), x_rows(e))
                rtt, rt = rtl[(s + 3) % 3]
                n = len(bl) * 128
                cx.dma(SP, rt[:, :, 0:n], rope_tab[:, :, bl[0] * 128:bl[0] * 128 + n], rtt, load=True)

            def lnA_steps(s):
                bl = sb_blocks(s)
                uTt, uT = uTl[(s + 3) % 3]
                steps = [lambda: issue_loads(s)]
                for i, e in enumerate(bl):
                    steps += ln_steps(("x", e), uTt, uT, i * 128, 0)
                return steps

            def projA_jobs(s):
                bl = sb_blocks(s)
                n = len(bl) * 128
                main = 0 <= s < NSB
                uTt, uT = uTl[(s + 3) % 3]
                rtt, rt = rtl[(s + 3) % 3]
                hold = {"prev": None}
                jobs = []

                def rope_job(off, dsts):
                    def f():
                        cur = (rope_front(uTt, uT, n, off, rtt, rt), dsts)
                        if hold["prev"] is not None:
                            rope_back(hold["prev"][0], n, rtt, rt, hold["prev"][1])
                        hold["prev"] = cur
                    return f

                def rope_flush():
                    if hold["prev"] is not None:
                        rope_back(hold["prev"][0], n, rtt, rt, hold["prev"][1])
                        hold["prev"] = None

                for g in range(4):
                    dsts = []
                    for i, e in enumerate(bl):
                        dsts.append((KTt[e % RING], KTb[0:64, e % RING, 0, g, :], i * 128, 128, 0, 64))
                        dsts.append((KTt[e % RING], KTb[64:128, e % RING, 1, g, :], i * 128, 128, 64, 128))
                    jobs.append(rope_job(K_OFF + g * 128, dsts))
                jobs.append(rope_flush)
                for i, e in enumerate(bl):
                    jobs.append(lambda i=i, e=e: v_block(uTt, uT, i * 128, Vt[e % RING], Vb[:, e % RING, :, 0:64]))
                for i, e in enumerate(bl):
                    jobs.append(lambda i=i, e=e: p_block(uTt, uT, i * 128, Pt[e % RING], Pb[:, e % RING, :]))
                if main:
                    qt, q = QTl[s % 2]
                    for c in range(8):
                        jobs.append(rope_job(Q_OFF + c * 128, [(qt, q[:, c, 0:n], 0, n, 0, 128)]))
                    jobs.append(rope_flush)
                return jobs

            def projA(s):
                for f in projA_jobs(s):
                    f()

            def pooling(s):
                for i, e in enumerate(sb_blocks(s)):
                    j = e - 1
                    var = 0 if j == 0 else (2 if j == NBLK - 1 else 1)
                    pt, pb = proj.next()
                    for g in range(4):
                        for rel in range(3):
                            ee = e - 1 + rel
                            cx.op(PE, lambda: nc.tensor.matmul(pb[:, g * 128:(g + 1) * 128], lhsT=Pb[:, ee % RING, g * 128:(g + 1) * 128],
                                                               rhs=amat_ap(var, rel, g), start=(rel == 0), stop=(rel == 2)),
                                  [Pt[ee % RING], cstT], [pt])
                    cx.op(ACT, lambda: nc.scalar.copy(out=PL[:, :, i * 128:(i + 1) * 128], in_=pb[:, 0:512].rearrange("p (g t) -> p g t", g=4)), [pt], [PLt])

            def att_S(s, i, g, between=None):
                qt, q = QTl[s % 2]
                e = sb_blocks(s)[i]
                j = e - 1
                c0 = i * 128
                srcs = [
                    (KTt[(e - 1) % RING], KTb[:, (e - 1) % RING, :, g, :], Vt[(e - 1) % RING], Vb[:, (e - 1) % RING, g, :], mask_ap(0 if j == 0 else 1)),
                    (KTt[e % RING], KTb[:, e % RING, :, g, :], Vt[e % RING], Vb[:, e % RING, g, :], None),
                    (KTt[(e + 1) % RING], KTb[:, (e + 1) % RING, :, g, :], Vt[(e + 1) % RING], Vb[:, (e + 1) % RING, g, :], mask_ap(3 if j == NBLK - 1 else 2)),
                    (KcTt, KcT[:, :, g, 0:128], Vct[0], Vcb[:, 0, g, :], None),
                    (KcTt, KcT[:, :, g, 128:256], Vct[1], Vcb[:, 1, g, :], None),
                ]
                pts = []
                for kbi, (kt, kap, vt, vap, mk) in enumerate(srcs):
                    if between is not None and kbi in (2, 4):
                        between()
                    stt, stb = STr.next()
                    cx.op(PE, lambda: nc.tensor.matmul(stb[:, 0:256], lhsT=kap[:, 0, :], rhs=q[:, 2 * g:2 * g + 2, c0:c0 + 128],
                                                       start=True, stop=True, skip_group_check=True), [kt, qt], [stt])
                    cx.op(PE, lambda: nc.tensor.matmul(stb[:, 256:512], lhsT=kap[:, 1, :], rhs=q[:, 2 * g:2 * g + 2, c0:c0 + 128],
                                                       start=True, stop=True, skip_group_check=True), [kt, qt], [stt])
                    ptt, ptb = PTr.next()
                    cx.op(ACT, lambda: nc.scalar.activation(out=ptb, in_=stb[:, 0:512], func=AF.Exp, scale=0.125), [stt], [ptt])
                    if mk is not None:
                        cx.op(POOL, lambda: nc.gpsimd.tensor_tensor(out=ptb.rearrange("p (c q) -> p c q", c=4), in0=ptb.rearrange("p (c q) -> p c q", c=4),
                                                                    in1=mk.unsqueeze(1).to_broadcast([128, 4, 128]), op=ALU.mult), [ptt, cstT], [ptt])
                    pts.append((ptt, ptb, vt, vap))
                return pts

            def att_PV(s, i, g, pts):
                obt, ob = Obl[i % 2]
                ot, obk = Or.next()
                ov = obk[:, 0:260].rearrange("p (c d) -> p c d", d=65)
                for cb in range(4):
                    for kb, (ptt, ptb, vt, vap) in enumerate(pts):
                        cx.op(PE, lambda: nc.tensor.matmul(ov[:, cb, :], lhsT=ptb[:, cb * 128:(cb + 1) * 128], rhs=vap,
                                                           start=(kb == 0), stop=(kb == 4)), [ptt, vt], [ot])
                dt_, den = denr.next()
                cx.op(DVE, lambda: nc.vector.tensor_tensor(out=den[:, 0:4], in0=ov[:, :, 64], in1=expsink[:, 4 * g:4 * g + 4], op=ALU.add), [ot, miscT], [dt_])
                cx.op(DVE, lambda: nc.vector.reciprocal(out=den[:, 4:8], in_=den[:, 0:4]), [dt_], [dt_])
                cx.op(DVE, lambda: nc.vector.tensor_tensor(out=ob[:, 4 * g:4 * g + 4, :], in0=ov[:, :, 0:64],
                                                           in1=den[:, 4:8].unsqueeze(2).to_broadcast([128, 4, 64]), op=ALU.mult), [ot, dt_], [obt])

            def att_T(s, i):
                obt, ob = Obl[i % 2]
                c0 = i * 128
                obf = ob[:, :, :].rearrange("p h d -> p (h d)")
                TRt, TRb = proj.next()
                TRv = TRb[:, :].bitcast(BF16)
                for c in range(8):
                    cx.op(PE, lambda: nc.tensor.transpose(TRv[:, c * 128:(c + 1) * 128], obf[:, c * 128:(c + 1) * 128], ident), [obt, cstT], [TRt])
                cx.op(ACT, lambda: nc.scalar.copy(out=OT[:, :, c0:c0 + 128], in_=TRv[:, :].rearrange("p (c t) -> p c t", c=8)), [TRt], [OTt])

            def attention(s, pend, fill):
                units = [(i, g) for i in range(NB) for g in range(4)]
                prev = None
                nfill = len(fill)
                def one_fill():
                    if fill:
                        fill.pop(0)()

                pend_T = None
                for ui, (i, g) in enumerate(units):
                    if ui == 4:
                        while len(fill) > nfill - 7 and fill:
                            fill.pop(0)()
                    pts = att_S(s, i, g, between=one_fill)
                    if pend_T is not None:
                        pend_T[1] -= 1
                        if pend_T[1] == 0:
                            att_T(s, pend_T[0])
                            pend_T = None
                    if prev is not None:
                        pi, pg, ppts = prev
                        att_PV(s, pi, pg, ppts)
                        if pg == 3:
                            pend_T = [pi, 2]
                    prev = (i, g, pts)
                    if ui < 4:
                        one_fill()
                    run_steps_a(pend, 2)
                pi, pg, ppts = prev
                att_PV(s, pi, pg, ppts)
                for _ in range(3):
                    one_fill()
                att_T(s, pi)

            def run_steps_a(q_, n_):
                for _ in range(n_):
                    if q_:
                        f = q_.pop(0)
                        if f is not None:
                            f()

            mt_pending = []

            def merge(s):
                uTt, uT = uTl[(s + 3) % 3]
                mt, mT = mTl[s % 2]
                for m in range(8):
                    tgs = []
                    for a, off in enumerate((GA_OFF, GP_OFF)):
                        tgt, tg = tgr.next()
                        pt, pb = proj_fm(uTt, uT, SBW, off + m * 128)
                        cx.op(ACT, lambda: nc.scalar.activation(out=tg[:, :], in_=pb[:, 0:SBW], func=AF.Tanh, scale=0.5), [pt], [tgt])
                        tgs.append((tgt, tg))
                    u1t, u1 = u1r.next()
                    u2t_, u2_ = u1r.next()
                    pt, pb = proj.next()
                    for c in range(8):
                        cx.op(PE, lambda: nc.tensor.matmul(pb[:, 0:SBW], lhsT=Wab[:, c, m * 128:(m + 1) * 128], rhs=OT[:, c, :],
                                                           start=(c == 0), stop=(c == 7)), [WabT, OTt], [pt])
                    cx.op(DVE, lambda: nc.vector.scalar_tensor_tensor(out=u1[:, :], in0=tgs[0][1][:, :], scalar=1.0, in1=pb[:, 0:SBW],
                                                                      op0=ALU.add, op1=ALU.mult), [tgs[0][0], pt], [u1t])
                    pt, pb = proj.next()
                    g = m // 2
                    cx.op(PE, lambda: nc.tensor.matmul(pb[:, 0:SBW], lhsT=Wpl[:, g, (m % 2) * 128:(m % 2) * 128 + 128], rhs=PL[:, g, :],
                                                       start=True, stop=True), [WplT, PLt], [pt])
                    cx.op(DVE, lambda: nc.vector.scalar_tensor_tensor(out=u2_[:, :], in0=tgs[1][1][:, :], scalar=1.0, in1=pb[:, 0:SBW],
                                                                      op0=ALU.add, op1=ALU.mult), [tgs[1][0], pt], [u2t_])
                    cx.op(POOL, lambda: nc.gpsimd.tensor_tensor(out=mT[:, m, :], in0=u1[:, :], in1=u2_[:, :], op=ALU.add), [u1t, u2t_], [mt])
                mt_pending.append(lambda: cx.dma(SP, mt_scr[s], mT[:, :, :].rearrange("p c t -> p (c t)"), mt, load=False))

            stopat("ctx")
            def jobs_with(jobs, pend_, k_):
                for jb in jobs:
                    jb()
                    run_steps_a(pend_, k_)

            pend0 = lnA_steps(-1) + lnA_steps(0)
            jobs_with(ctx_jobs(), pend0, 3)
            run_steps_a(pend0, len(pend0))
            pend1 = lnA_steps(1)
            jobs_with(projA_jobs(-1), pend1, 1)
            jobs_with(projA_jobs(0), pend1, 1)
            run_steps_a(pend1, len(pend1))
            if DEBUG:
                dump("uT0", uTl[0][0], uTl[0][1][:, :, :].rearrange("p c t -> p (c t)"))
                dump("QT0", QTl[0][0], QTl[0][1][:, :, :].rearrange("p c t -> p (c t)"))
                dump("KT1", KTt[1], KTb[:, 1, 0, :, :].rearrange("p g t -> p (g t)"))
                dump("V1", Vt[1], Vb[:, 1, :, :].rearrange("p g d -> p (g d)"))
            if level == 1 and nsb_run == 0:
                cx.barrier()
                raise _Stop()
            for s in range(nsb_run):
                pend = lnA_steps(s + 2) if s + 2 <= NSB else []
                fill = projA_jobs(s + 1)
                while mt_pending:
                    mt_pending.pop(0)()
                attention(s, pend, fill)
                while fill:
                    fill.pop(0)()
                pooling(s)
                merge(s)
                run_steps_a(pend, len(pend))
                if DEBUG and s == 0:
                    dump("OT0", OTt, OT[:, :, :].rearrange("p c t -> p (c t)"))
                    dump("PL0", PLt, PL[:, :, :].rearrange("p g t -> p (g t)"))
                    dump("mt", mTl[0][0], mTl[0][1][:, :, :].rearrange("p c t -> p (c t)"))
            while mt_pending:
                mt_pending.pop(0)()
            cx.barrier()
            if level == 1:
                raise _Stop()

        sw.__exit__(None, None, None)

        with ExitStack() as sbc:
            W1 = sbuf(sbc, "W1", [128, 8, DFF], BF16)
            W1T = Tile("W1")
            W2 = sbuf(sbc, "W2", [128, 32, D], BF16)
            W2T = Tile("W2")
            lng = sbuf(sbc, "lng", [128, D], F32)
            lnb = sbuf(sbc, "lnb", [128, D], F32)
            lnT = Tile("lnT")

            def bcast_row(ri, dst, dstT, scale):
                cx.dma(SP, dst[:, :], g_scr[ri:ri + 1, :].partition_broadcast(128), dstT, load=True)
                if scale != 1.0:
                    cx.op(ACT, lambda: nc.scalar.mul(out=dst[:, :], in_=dst[:, :], mul=scale), [dstT], [dstT])

            with ExitStack() as sb_:
                Wout = sbuf(sb_, "Wout", [128, 8, D], BF16)
                WoutTk = [Tile("Wout%d" % i) for i in range(8)]
                g1b = sbuf(sb_, "g1b", [128, D], F32)
                g1bT = Tile("g1b")
                stg = [(Tile("stg%d" % i), sbuf(sb_, "stg%d" % i, [128, D], F32)) for i in range(2)]
                mll = [(Tile("ml%d" % i), sbuf(sb_, "ml%d" % i, [128, 8, SBW], BF16)) for i in range(2)]
                NXB = 5
                xbl = [(Tile("xb%d" % i), sbuf(sb_, "xb%d" % i, [128, D], F32)) for i in range(NXB)]
                bcast_row(0, g1b, g1bT, 0.5)
                cx.dma(SP, lng[:, :], ln1_g[0:1, :].partition_broadcast(128), lnT, load=True)
                cx.dma(SP, lnb[:, :], ln1_b[0:1, :].partition_broadcast(128), lnT, load=True)
                w_out_v = w_out.rearrange("(kc p) n -> p kc n", p=128)
                for kc in range(8):
                    st_t, st_ = stg[kc % 2]
                    cx.dma(SP, st_[:, :], w_out_v[:, kc, :], st_t, load=True)
                    if kc % 2 == 0:
                        cx.op(DVE, lambda: nc.vector.tensor_tensor(out=Wout[:, kc, :], in0=st_[:, :], in1=g1b[:, :], op=ALU.mult), [st_t, g1bT], [WoutTk[kc]])
                    else:
                        cx.op(POOL, lambda: nc.gpsimd.tensor_tensor(out=Wout[:, kc, :], in0=st_[:, :], in1=g1b[:, :], op=ALU.mult), [st_t, g1bT], [WoutTk[kc]])
                w1_v = w_mlp_in.rearrange("(kc p) n -> p kc n", p=128)
                w2_v = w_mlp_out.rearrange("(kc p) n -> p kc n", p=128)
                w2_jobs = list(range(32))
                w1_jobs = [(kc, q) for kc in range(8) for q in range(4)]
                stg_ctr = [8]

                def w_job():
                    for _ in range(1):
                        if w1_jobs:
                            kc, q = w1_jobs.pop(0)
                            st_t, st_ = stg[stg_ctr[0] % 2]
                            stg_ctr[0] += 1
                            cx.dma(SP, st_[:, :], w1_v[:, kc, q * 1024:(q + 1) * 1024], st_t, load=True)
                            cx.op(ACT, lambda: nc.scalar.copy(out=W1[:, kc, q * 1024:(q + 1) * 1024], in_=st_[:, :]), [st_t], [W1T])
                        elif w2_jobs:
                            c = w2_jobs.pop(0)
                            st_t, st_ = stg[stg_ctr[0] % 2]
                            stg_ctr[0] += 1
                            cx.dma(SP, st_[:, :], w2_v[:, c, :], st_t, load=True)
                            cx.op(ACT, lambda: nc.scalar.copy(out=W2[:, c, :], in_=st_[:, :]), [st_t], [W2T])

                projB = Ring(banks[0:8])
                xb_ctr = [0]

                def blockB_steps(s_, i, mlt, ml):
                    r0 = (s_ * NB + i) * 128
                    xt, xb = xbl[xb_ctr[0] % NXB]
                    xb_ctr[0] += 1
                    st = {}

                    def head():
                        cx.dma(SP, xb[:, :], x_main[r0:r0 + 128, :], xt, load=True)
                        for hf in range(2):
                            pt, pb = projB.next()
                            for kc in range(8):
                                cx.op(PE, lambda: nc.tensor.matmul(pb[:, 0:512], lhsT=ml[:, kc, i * 128:(i + 1) * 128], rhs=Wout[:, kc, hf * 512:(hf + 1) * 512],
                                                                   start=(kc == 0), stop=(kc == 7)), [mlt, WoutTk[kc]], [pt])
                            cx.op(DVE, lambda: nc.vector.scalar_tensor_tensor(out=xb[:, hf * 512:(hf + 1) * 512], in0=xb[:, hf * 512:(hf + 1) * 512], scalar=ALPHA,
                                                                              in1=pb[:, 0:512], op0=ALU.mult, op1=ALU.add), [xt, pt], [xt])

                    def s_stats():
                        st["ln"] = ln_stats(xt, xb)

                    def s_norm():
                        t_, rstd, nmr = st["ln"]
                        cx.op(ACT, lambda: nc.scalar.activation(out=xb[:, :], in_=xb[:, :], func=AF.Identity, bias=nmr, scale=rstd), [xt, t_], [xt])

                    def s_mul():
                        cx.op(DVE, lambda: nc.vector.tensor_tensor(out=xb[:, 0:512], in0=xb[:, 0:512], in1=lng[:, 0:512], op=ALU.mult), [xt, lnT], [xt])
                        cx.op(POOL, lambda: nc.gpsimd.tensor_tensor(out=xb[:, 512:1024], in0=xb[:, 512:1024], in1=lng[:, 512:1024], op=ALU.mult), [xt, lnT], [xt])

                    def s_add():
                        cx.op(POOL, lambda: nc.gpsimd.tensor_tensor(out=xb[:, :], in0=xb[:, :], in1=lnb[:, :], op=ALU.add), [xt, lnT], [xt])

                    def s_store():
                        cx.dma(SP, x1_scr[r0:r0 + 128, :], xb[:, :], xt, load=False)
                        if DEBUG and s_ == 0:
                            cx.dma(SP, dbg["x1"][i * 128:(i + 1) * 128, :], xb[:, :], xt, load=False)

                    return head, [s_stats, None, None, s_norm, None, s_mul, None, s_add, None, None, None, s_store]

                active = []
                for s in range(NSB):
                    mlt, ml = mll[s % 2]
                    cx.dma(SP, ml[:, :, :].rearrange("p c t -> p (c t)"), mt_scr[s], mlt, load=True)
                    for i in range(NB):
                        head, steps = blockB_steps(s, i, mlt, ml)
                        head()
                        for st_ in active:
                            for _ in range(3):
                                if st_:
                                    f_ = st_.pop(0)
                                    if f_ is not None:
                                        f_()
                        active = [a for a in active if a]
                        active.append(steps)
                        w_job()
                        w_job()
                while active:
                    for st_ in active:
                        if st_:
                            f_ = st_.pop(0)
                            if f_ is not None:
                                f_()
                    active = [a for a in active if a]
                while w2_jobs or w1_jobs:
                    w_job()
                cx.barrier()
                if level == 2:
                    raise _Stop()

            with ExitStack() as sc:
                cx.dma(SP, lng[:, :], ln2_g[0:1, :].partition_broadcast(128), lnT, load=True)
                cx.dma(SP, lnb[:, :], ln2_b[0:1, :].partition_broadcast(128), lnT, load=True)
                g2b = sbuf(sc, "g2b", [128, D], F32)
                g2bT = Tile("g2b")
                bcast_row(1, g2b, g2bT, 1.0)
                zr = Ring([(Tile("zt%d" % i), sbuf(sc, "zt%d" % i, [128, 512], F32)) for i in range(4)])
                NXC = 6
                xcl = [(Tile("xc%d" % i), sbuf(sc, "xc%d" % i, [128, D], F32)) for i in range(NXC)]
                xhc = [(Tile("xhc%d" % i), sbuf(sc, "xhc%d" % i, [128, D], BF16)) for i in range(2)]
                u2l = [(Tile("u2%d" % i), sbuf(sc, "u2%d" % i, [128, 8, SBW], BF16)) for i in range(2)]
                rlr = Ring([(Tile("rl%d" % i), sbuf(sc, "rl%d" % i, [128, SBW], F32)) for i in range(4)])
                hr = Ring([(Tile("h%d" % i), sbuf(sc, "h%d" % i, [128, SBW], BF16)) for i in range(6)])
                projC = Ring(banks[0:4])
                acc = banks[4:8]
                xc_ctr = [0]
                loaded = {}

                def load_sb(s):
                    for i in range(NB):
                        r0 = (s * NB + i) * 128
                        xt, xc = xcl[xc_ctr[0] % NXC]
                        xc_ctr[0] += 1
                        cx.dma(SP, xc[:, :], x1_scr[r0:r0 + 128, :], xt, load=True)
                        loaded[(s, i)] = (xt, xc)

                load_sb(0)
                xh_c = [0]
                from collections import deque
                DSK = 3

                def interleave(lists):
                    out_ = []
                    k = 0
                    while any(k < len(l_) for l_ in lists):
                        for l_ in lists:
                            if k < len(l_):
                                out_.append(l_[k])
                        k += 1
                    return out_

                def lnmod_steps(s_):
                    u2t, u2 = u2l[s_ % 2]
                    steps = []
                    for i in range(NB):
                        xt, xc = loaded[(s_, i)]
                        stt_ = {}

                        def f_stats(xt=xt, xc=xc, stt_=stt_):
                            stt_["ln"] = ln_stats(xt, xc)

                        def f_hat(xt=xt, xc=xc, stt_=stt_):
                            t_, rstd, nmr = stt_["ln"]
                            ht, htile = xhc[xh_c[0] % 2]
                            xh_c[0] += 1
                            stt_["h"] = (ht, htile)
                            cx.op(ACT, lambda: nc.scalar.activation(out=htile[:, :], in_=xc[:, :], func=AF.Identity, bias=nmr, scale=rstd), [xt, t_], [ht])

                        def f_tr(stt_=stt_):
                            ht, htile = stt_["h"]
                            TRt, TRb = projC.next()
                            TRv = TRb[:, :].bitcast(BF16)
                            stt_["tr"] = (TRt, TRv)
                            for c in range(8):
                                cx.op(PE, lambda: nc.tensor.transpose(TRv[:, c * 128:(c + 1) * 128], htile[:, c * 128:(c + 1) * 128], ident), [ht, cstT], [TRt])

                        def f_mod(i=i, stt_=stt_):
                            TRt, TRv = stt_["tr"]
                            for c in range(8):
                                cx.op(DVE, lambda: nc.vector.tensor_scalar(out=u2[:, c, i * 128:(i + 1) * 128], in0=TRv[:, c * 128:(c + 1) * 128],
                                                                           scalar1=modT[:, 32 + c, 0:1], scalar2=modT[:, 24 + c, 0:1],
                                                                           op0=ALU.mult, op1=ALU.add), [TRt, modTt], [u2t])
                        steps.append((f_stats, f_hat, f_tr, f_mod))
                    slots = [None] * (14 * len(steps) + 8)
                    for b_, (a_, h_, t_, m_) in enumerate(steps):
                        o_ = 14 * b_
                        slots[o_], slots[o_ + 11], slots[o_ + 15], slots[o_ + 17] = a_, h_, t_, m_
                    return slots

                def fin_steps(s_):
                    z_steps, steps_all = [], []
                    for i in range(NB):
                        steps = []
                        xt, xc = loaded.pop((s_, i))
                        r0 = (s_ * NB + i) * 128
                        stt_ = {}
                        for hf in range(2):
                            zst = {}

                            def f_z(i=i, hf=hf, zst=zst):
                                at, ab = acc[i * 2 + hf]
                                zt_, zz = zr.next()
                                zst["z"] = (zt_, zz)
                                cx.op(DVE, lambda: nc.vector.tensor_tensor(out=zz[:, :], in0=ab[:, 0:512], in1=g2b[:, hf * 512:(hf + 1) * 512], op=ALU.mult), [at, g2bT], [zt_])

                            def f_z2(hf=hf, xt=xt, xc=xc, zst=zst):
                                zt_, zz = zst["z"]
                                cx.op(DVE, lambda: nc.vector.scalar_tensor_tensor(out=xc[:, hf * 512:(hf + 1) * 512], in0=xc[:, hf * 512:(hf + 1) * 512], scalar=ALPHA,
                                                                                  in1=zz[:, :], op0=ALU.mult, op1=ALU.add), [xt, zt_], [xt])
                            z_steps.append(f_z)
                            steps.append(f_z2)
                            steps.append(None)
                            steps.append(None)

                        def f_stats(xt=xt, xc=xc, stt_=stt_):
                            stt_["ln"] = ln_stats(xt, xc)

                        def f_norm(xt=xt, xc=xc, stt_=stt_):
                            t_, rstd, nmr = stt_["ln"]
                            cx.op(ACT, lambda: nc.scalar.activation(out=xc[:, :], in_=xc[:, :], func=AF.Identity, bias=nmr, scale=rstd), [xt, t_], [xt])

                        def f_mul(xt=xt, xc=xc):
                            cx.op(DVE, lambda: nc.vector.tensor_tensor(out=xc[:, 0:512], in0=xc[:, 0:512], in1=lng[:, 0:512], op=ALU.mult), [xt, lnT], [xt])
                            cx.op(POOL, lambda: nc.gpsimd.tensor_tensor(out=xc[:, 512:1024], in0=xc[:, 512:1024], in1=lng[:, 512:1024], op=ALU.mult), [xt, lnT], [xt])

                        def f_add(xt=xt, xc=xc):
                            cx.op(POOL, lambda: nc.gpsimd.tensor_tensor(out=xc[:, :], in0=xc[:, :], in1=lnb[:, :], op=ALU.add), [xt, lnT], [xt])

                        def f_store(xt=xt, xc=xc, r0=r0):
                            cx.dma(SP, out[r0:r0 + 128, :], xc[:, :], xt, load=False)
                        steps += [f_stats] + [None] * 5 + [f_norm] + [None] * 2 + [f_mul] + [None] * 2 + [f_add] + [None] * 4 + [f_store]
                        steps_all.append(steps)
                    return z_steps, interleave(steps_all)

                def run_steps(q, n):
                    for _ in range(n):
                        if q:
                            f = q.popleft()
                            if f is not None:
                                f()

                q_pre = deque(lnmod_steps(0))
                run_steps(q_pre, len(q_pre))
                pend = deque()
                for s in range(NSB):
                    if s + 1 < NSB:
                        load_sb(s + 1)
                    u2t, u2 = u2l[s % 2]
                    if s + 1 < NSB:
                        pend.extend(lnmod_steps(s + 1))
                    hs = {}
                    for c in range(32 + DSK):
                        if c < 32:
                            pt, pb = projC.next()
                            for kc in range(8):
                                cx.op(PE, lambda: nc.tensor.matmul(pb[:, 0:SBW], lhsT=W1[:, kc, c * 128:(c + 1) * 128], rhs=u2[:, kc, :],
                                                                   start=(kc == 0), stop=(kc == 7)), [W1T, u2t], [pt])
                            rt_, rl = rlr.next()
                            cx.op(ACT, lambda: nc.scalar.activation(out=rl[:, :], in_=pb[:, 0:SBW], func=AF.Relu), [pt], [rt_])
                            ht_, hh = hr.next()
                            cx.op(DVE, lambda: nc.vector.tensor_tensor(out=hh[:, :], in0=pb[:, 0:SBW], in1=rl[:, :], op=ALU.mult), [pt, rt_], [ht_])
                            hs[c] = (ht_, hh)
                        cc = c - DSK
                        if cc >= 0:
                            ht_, hh = hs.pop(cc)
                            for i in range(NB):
                                for hf in range(2):
                                    at, ab = acc[i * 2 + hf]
                                    cx.op(PE, lambda: nc.tensor.matmul(ab[:, 0:512], lhsT=hh[:, i * 128:(i + 1) * 128], rhs=W2[:, cc, hf * 512:(hf + 1) * 512],
                                                                       start=(cc == 0), stop=(cc == 31)), [ht_, W2T], [at])
                        run_steps(pend, 3)
                    z_steps, f_steps = fin_steps(s)
                    if s + 1 < NSB:
                        for f in z_steps:
                            f()
                        rest = deque(f_steps)
                        run_steps(pend, len(pend))
                        pend = rest
                    else:
                        for f in z_steps:
                            f()
                        run_steps(pend, len(pend))
                        pend = deque(f_steps)
                        run_steps(pend, len(pend))
                cx.barrier([SP])


def _const_pack(core):
    half = core % 2
    cst = np.zeros((128, C_COLS), np.float32)
    cst[:, C_ID:C_ID + 128] = np.eye(128, dtype=np.float32)
    cst[:, C_ID4:C_ID4 + 512] = np.tile(np.eye(128, dtype=np.float32), (1, 4))
    qi = np.arange(128)[:, None]
    ki = np.arange(128)[None, :]
    kk = np.arange(128)[:, None]
    qq = np.arange(128)[None, :]
    prev_mid = np.where(kk >= qq, 1.0, 0.0).astype(np.float32)
    next_mid = np.where(kk <= qq, 1.0, 0.0).astype(np.float32)
    allneg = np.zeros((128, 128), np.float32)
    masks = [allneg if half == 0 else prev_mid, prev_mid, next_mid, allneg if half == 1 else next_mid]
    for i, m in enumerate(masks):
        cst[:, C_MASK + i * 128:C_MASK + (i + 1) * 128] = m
    pm = np.zeros((128, 128), np.float32)
    for m_ in range(128):
        d = m_ % 64
        partner = d + 16 if (d % 32) < 16 else d - 16
        pm[(m_ // 64) * 64 + partner, m_] = 1.0
    cst[:, C_PERM:C_PERM + 128] = pm
    start = half * TOK
    for var in range(3):
        j = 0 if var == 0 else (NBLK - 1 if var == 2 else 5)
        base = start + j * 128
        for g, w in enumerate((2, 4, 8, 16)):
            T = base + np.arange(128)
            lo = np.clip(T - w // 2, 0, SEQ)
            hi = np.clip(T + w // 2, 0, SEQ)
            cnt = (hi - lo).astype(np.float32)
            for rel in range(3):
                Tp = base + (rel - 1) * 128 + np.arange(128)
                inwin = (Tp[:, None] >= lo[None, :]) & (Tp[:, None] < hi[None, :])
                A = np.where(inwin, 1.0 / cnt[None, :], 0.0).astype(np.float32)
                if rel == 1:
                    A = A - np.eye(128, dtype=np.float32)
                o = C_AMAT + ((var * 3 + rel) * 4 + g) * 128
                cst[:, o:o + 128] = A
    return cst


def _rope_tab(core):
    half = core % 2
    t = half * TOK - 128 + np.arange(NEXT * 128)
    t = np.clip(t, 0, SEQ - 1)
    rows = (t // 64).astype(np.float64)
    cols = (t % 64).astype(np.float64)
    inv = 1.0 / (10000.0 ** (np.arange(16, dtype=np.float64) / 16.0))
    tab = np.zeros((128, 2, NEXT * 128), np.float32)
    for p in range(128):
        d = p % 64
        pos = rows if d < 32 else cols
        i = d % 16
        sign = -1.0 if (d % 32) < 16 else 1.0
        ang = pos * inv[i]
        tab[p, 0] = np.cos(ang).astype(np.float32)
        tab[p, 1] = (sign * np.sin(ang)).astype(np.float32)
    return tab


_NC_CACHE = {}


def make_in_maps(x, c, ctx, c_ctx, w_ada, b_ada, w_in, w_attn_branch, w_pool, pool_scale, attn_sink, w_out,
                 ln1_g, ln1_b, w_mlp_in, w_mlp_out, ln2_g, ln2_b, cores=None):
    f = lambda a: np.ascontiguousarray(np.asarray(a, dtype=np.float32))
    x, c, ctx, c_ctx = f(x), f(c), f(ctx), f(c_ctx)
    w_ada, b_ada, w_in = f(w_ada)[0], f(b_ada), f(w_in)[0]
    cols = []
    cols.append(w_in[:, 2048:4096])
    for cc in range(8):
        g, r = cc // 2, cc % 2
        h0, h1 = 4 * g + r, 4 * g + 2 + r
        cols.append(w_in[:, h0 * 64:(h0 + 1) * 64])
        cols.append(w_in[:, h1 * 64:(h1 + 1) * 64])
    for g in range(4):
        kg = w_in[:, 1024 + g * 64:1024 + (g + 1) * 64]
        cols.append(kg)
        cols.append(kg)
    cols.append(w_in[:, 1280:1536])
    cols.append(w_in[:, 1536:2048])
    w_in_p = np.ascontiguousarray(np.concatenate(cols, axis=1))
    assert w_in_p.shape == (D, WIN_COLS)
    b_adaT = np.ascontiguousarray(b_ada[0].reshape(48, 128).T)
    shared = {
        "w_ada": w_ada, "b_adaT": b_adaT, "b_ada": b_ada, "w_in_p": w_in_p,
        "w_ab": f(w_attn_branch)[0], "w_pool": f(w_pool)[0], "pool_scale": f(pool_scale),
        "attn_sink": f(attn_sink), "w_out": f(w_out)[0], "ln1_g": f(ln1_g), "ln1_b": f(ln1_b),
        "w_mlp_in": f(w_mlp_in)[0], "w_mlp_out": f(w_mlp_out)[0], "ln2_g": f(ln2_g), "ln2_b": f(ln2_b),
    }
    in_maps = []
    for core in (range(NCORES) if cores is None else cores):
        b, half = core // 2, core % 2
        t0 = half * TOK
        halo = np.zeros((256, D), np.float32)
        if half == 1:
            halo[0:128] = x[b, t0 - 128:t0]
        else:
            halo[128:256] = x[b, t0 + TOK:t0 + TOK + 128]
        cc2 = np.stack([c[b], c_ctx], axis=1)
        ccT = np.ascontiguousarray(cc2.reshape(8, 128, 2).transpose(1, 0, 2))
        m = dict(shared)
        m.update({
            "x_main": np.ascontiguousarray(x[b, t0:t0 + TOK]), "x_halo": halo, "ctx_in": np.ascontiguousarray(ctx[b]),
            "ccT": ccT, "rope_tab": _rope_tab(core), "cst": _const_pack(core),
        })
        in_maps.append(m)
    return in_maps


def kernel(x, c, ctx, c_ctx, w_ada, b_ada, w_in, w_attn_branch, w_pool, pool_scale, attn_sink, w_out,
           ln1_g, ln1_b, w_mlp_in, w_mlp_out, ln2_g, ln2_b):
    in_maps = make_in_maps(x, c, ctx, c_ctx, w_ada, b_ada, w_in, w_attn_branch, w_pool, pool_scale, attn_sink, w_out,
                           ln1_g, ln1_b, w_mlp_in, w_mlp_out, ln2_g, ln2_b)
    if "nc" not in _NC_CACHE:
        _NC_CACHE["nc"] = build_program()
    res = run_bass_kernel_spmd(_NC_CACHE["nc"], in_maps, core_ids=list(range(NCORES)))
    _NC_CACHE["last"] = res
    outp = np.empty((4, SEQ, D), np.float32)
    for core in range(NCORES):
        b, half = core // 2, core % 2
        outp[b, half * TOK:(half + 1) * TOK] = res.results[core]["out"]
    return outp
```

```python
import math
from contextlib import ExitStack

import numpy as np
import concourse.bass as bass
import concourse.mybir as mybir
from concourse.bass_utils import run_bass_kernel_spmd

F32 = mybir.dt.float32
BF16 = mybir.dt.bfloat16
AF = mybir.ActivationFunctionType
ALU = mybir.AluOpType

D = 1024
SEQ = 8192
NCORES = 8
TOK = 4096
NBLK = 32
NB = 2
SBW = NB * 128
NSB = NBLK // NB
NEXT = NBLK + 2
HEADS = 16
HD = 64
CTX = 256
DFF = 4096
ALPHA = 2.0 ** 0.25
LN_EPS = 1e-6
NEG = -30000.0
GA_OFF, GP_OFF, Q_OFF, K_OFF, V_OFF, P_OFF, WIN_COLS = 0, 1024, 2048, 3072, 3584, 3840, 4352
C_ID, C_ID4, C_MASK, C_PERM, C_AMAT = 0, 128, 640, 1152, 1280
C_COLS = C_AMAT + 36 * 128
RING = 6
EPOCH = 20000
DEBUG = False


class Eng:
    def __init__(self, ctx, name, h, is_pe=False):
        self.ctx, self.name, self.h, self.is_pe = ctx, name, h, is_pe
        self.sems = []
        self.count = 0
        self.known = {}

    def sem_for(self, seq):
        ep = (seq - 1) // EPOCH
        while len(self.sems) <= ep:
            self.sems.append(self.ctx.es.enter_context(self.ctx.nc.semaphore("s_%s_%d" % (self.name, len(self.sems)))))
        return self.sems[ep], seq - ep * EPOCH, ep


class Tile:
    def __init__(self, name, psum=False, multi=False):
        self.name = name
        self.psum = psum
        self.multi = multi
        self.writers = {}
        self.readers = {}
        self.dsem = None
        self.dcnt = 0
        self.dw = 0
        self.da = 0


class Ctx:
    def __init__(self, nc, es):
        self.nc, self.es = nc, es
        self.pe = Eng(self, "pe", nc.tensor, True)
        self.act = Eng(self, "act", nc.scalar)
        self.dve = Eng(self, "dve", nc.vector)
        self.pool = Eng(self, "pool", nc.gpsimd)
        self.sp = Eng(self, "sp", nc.sync)
        self.engs = [self.pe, self.act, self.dve, self.pool, self.sp]
        self.dma_tiles = []

    def wait_eng(self, eng, e, n):
        if n <= 0:
            return
        if e is eng and eng.is_pe:
            return
        sem, val, ep = e.sem_for(n)
        key = (e.name, ep)
        if eng.known.get(key, 0) >= val:
            return
        eng.h.wait_ge(sem, val)
        eng.known[key] = val

    def wait_dma(self, eng, t, val):
        if val <= 0:
            return
        key = ("d", id(t))
        if eng.known.get(key, 0) >= val:
            return
        eng.h.wait_ge(t.dsem, val)
        eng.known[key] = val

    def _pre(self, eng, reads, writes):
        for t in reads:
            for e, n in t.writers.items():
                self.wait_eng(eng, e, n)
            if t.psum:
                for e, n in t.readers.items():
                    if e is not eng:
                        self.wait_eng(eng, e, n)
            self.wait_dma(eng, t, t.dw)
        for t in writes:
            if not t.multi:
                for e, n in t.writers.items():
                    self.wait_eng(eng, e, n)
            for e, n in t.readers.items():
                self.wait_eng(eng, e, n)
            self.wait_dma(eng, t, t.da)

    def op(self, eng, fn, reads=(), writes=()):
        self._pre(eng, reads, writes)
        ins = fn()
        eng.count += 1
        seq = eng.count
        sem, _, _ = eng.sem_for(seq)
        ins.then_inc(sem, 1)
        for t in reads:
            if t not in writes:
                t.readers[eng] = seq
        for t in writes:
            if t.multi:
                t.writers[eng] = seq
            else:
                t.writers = {eng: seq}
                t.readers = {}
        return ins

    def dma(self, q, out_ap, in_ap, tile, load, extra_reads=(), **kw):
        if tile.dsem is None:
            tile.dsem = self.es.enter_context(self.nc.semaphore("d_%s" % tile.name))
            self.dma_tiles.append(tile)
        if load:
            self._pre(q, extra_reads, [tile])
        else:
            self._pre(q, [tile] + list(extra_reads), [])
        ins = q.h.dma_start(out=out_ap, in_=in_ap, **kw)
        tile.dcnt += 16
        ins.then_inc(tile.dsem, 16)
        if load:
            tile.dw = tile.da = tile.dcnt
            tile.writers = {}
            tile.readers = {}
        else:
            tile.da = tile.dcnt
        return ins

    def barrier(self, engs=None):
        engs = engs or self.engs
        for e in engs:
            for f in self.engs:
                if f is not self.sp and f is not e:
                    self.wait_eng(e, f, f.count)
            for t in self.dma_tiles:
                self.wait_dma(e, t, t.dcnt)


class Ring:
    def __init__(self, items):
        self.items, self.i = items, 0

    def next(self):
        it = self.items[self.i % len(self.items)]
        self.i += 1
        return it


class _Stop(Exception):
    pass


def build_program(level=3, nsb_run=NSB):
    nc = bass.Bass("TRN2", target_bir_lowering=False)
    try:
        _build(nc, level, nsb_run)
    except _Stop:
        pass
    return nc


STOPAT = None
VARIANT = None
ROPE_ADD_DVE = False


def _build(nc, level, nsb_run):

    def din(name, shape, dt=F32):
        return nc.dram_tensor(name, list(shape), dt, kind="ExternalInput").ap()

    x_main = din("x_main", [TOK, D])
    x_halo = din("x_halo", [256, D])
    ctx_in = din("ctx_in", [CTX, D])
    ccT_in = din("ccT", [128, 8, 2])
    w_ada = din("w_ada", [D, 6 * D])
    b_adaT_in = din("b_adaT", [128, 48])
    b_ada = din("b_ada", [1, 6 * D])
    w_in_p = din("w_in_p", [D, WIN_COLS])
    w_ab = din("w_ab", [D, D])
    w_pool = din("w_pool", [4, 128, 256])
    pool_scale = din("pool_scale", [1, D])
    attn_sink = din("attn_sink", [1, HEADS])
    w_out = din("w_out", [D, D])
    ln1_g = din("ln1_g", [1, D])
    ln1_b = din("ln1_b", [1, D])
    w_mlp_in = din("w_mlp_in", [D, DFF])
    w_mlp_out = din("w_mlp_out", [DFF, D])
    ln2_g = din("ln2_g", [1, D])
    ln2_b = din("ln2_b", [1, D])
    rope_tab = din("rope_tab", [128, 2, NEXT * 128])
    cst_in = din("cst", [128, C_COLS])
    out = nc.dram_tensor("out", [TOK, D], F32, kind="ExternalOutput").ap()
    mt_scr = nc.dram_tensor("mt_scr", [NSB, 128, 8 * SBW], BF16, kind="Internal").ap()
    x1_scr = nc.dram_tensor("x1_scr", [TOK, D], F32, kind="Internal").ap()
    dbg = {}
    if DEBUG:
        dbg["modT"] = nc.dram_tensor("dbg_modT", [128, 96], F32, kind="ExternalOutput").ap()
        dbg["uT0"] = nc.dram_tensor("dbg_uT0", [128, 8 * SBW], BF16, kind="ExternalOutput").ap()
        dbg["QT0"] = nc.dram_tensor("dbg_QT0", [128, 8 * SBW], BF16, kind="ExternalOutput").ap()
        dbg["KT1"] = nc.dram_tensor("dbg_KT1", [128, 512], BF16, kind="ExternalOutput").ap()
        dbg["V1"] = nc.dram_tensor("dbg_V1", [128, 4 * 65], BF16, kind="ExternalOutput").ap()
        dbg["OT0"] = nc.dram_tensor("dbg_OT0", [128, 8 * SBW], BF16, kind="ExternalOutput").ap()
        dbg["PL0"] = nc.dram_tensor("dbg_PL0", [128, 4 * SBW], BF16, kind="ExternalOutput").ap()
        dbg["mt"] = nc.dram_tensor("dbg_mt", [128, 8 * SBW], BF16, kind="ExternalOutput").ap()
        dbg["x1"] = nc.dram_tensor("dbg_x1", [256, D], F32, kind="ExternalOutput").ap()

    with ExitStack() as es:
        cx = Ctx(nc, es)
        PE, ACT, DVE, POOL, SP = cx.pe, cx.act, cx.dve, cx.pool, cx.sp

        def sbuf(stack, name, shape, dt):
            return stack.enter_context(nc.sbuf_tensor(name, list(shape), dt))

        banks = []
        for i in range(8):
            t = es.enter_context(nc.psum_tensor("bank%d" % i, [128, 512], F32))
            banks.append((Tile("bank%d" % i, psum=True), t))

        cst = sbuf(es, "cst_sb", [128, C_COLS], BF16)
        cstT = Tile("cst")
        ident = cst[:, C_ID:C_ID + 128]
        ident4 = cst[:, C_ID4:C_ID4 + 512]
        perm = cst[:, C_PERM:C_PERM + 128]

        def mask_ap(i):
            return cst[:, C_MASK + i * 128:C_MASK + (i + 1) * 128]

        def amat_ap(var, rel, g):
            o = C_AMAT + ((var * 3 + rel) * 4 + g) * 128
            return cst[:, o:o + 128]

        modT = sbuf(es, "modT", [128, 48, 2], F32)
        modTt = Tile("modT")
        g_scr = nc.dram_tensor("g_scr", [2, D], F32, kind="Internal").ap()
        mhalf = sbuf(es, "mhalf", [128, 1], F32)
        expsink = sbuf(es, "expsink", [128, HEADS], F32)
        miscT = Tile("misc")
        dbg_stage = None

        def dump(name, tile, ap):
            if DEBUG and name in dbg:
                cx.dma(SP, dbg[name], ap, tile, load=False)

        ln_sm = []
        for i in range(8):
            ln_sm.append((Tile("lnsm%d" % i), sbuf(es, "lnst%d" % i, [128, 2, 6], F32), sbuf(es, "lnmv%d" % i, [128, 4], F32)))
        ln_ring = Ring(ln_sm)

        def ln_stats(xt, xap):
            t, st, mv = ln_ring.next()
            cx.op(DVE, lambda: nc.vector.bn_stats(out=st[:, 0, :], in_=xap[:, 0:512]), [xt], [t])
            cx.op(DVE, lambda: nc.vector.bn_stats(out=st[:, 1, :], in_=xap[:, 512:1024]), [xt, t], [t])
            cx.op(DVE, lambda: nc.vector.bn_aggr(out=mv[:, 0:2], in_=st[:, :, :].rearrange("p a b -> p (a b)")), [t], [t])
            cx.op(POOL, lambda: nc.gpsimd.tensor_scalar(out=mv[:, 2:3], in0=mv[:, 1:2], scalar1=LN_EPS, scalar2=None, op0=ALU.add), [t], [t])
            cx.op(POOL, lambda: nc.gpsimd.tensor_tensor(out=mv[:, 2:3], in0=mv[:, 2:3], in1=mhalf[:, :], op=ALU.pow), [t, miscT], [t])
            cx.op(POOL, lambda: nc.gpsimd.tensor_tensor(out=mv[:, 3:4], in0=mv[:, 0:1], in1=mv[:, 2:3], op=ALU.mult), [t], [t])
            cx.op(POOL, lambda: nc.gpsimd.tensor_scalar(out=mv[:, 3:4], in0=mv[:, 3:4], scalar1=-1.0, scalar2=None, op0=ALU.mult), [t], [t])
            return t, mv[:, 2:3], mv[:, 3:4]

        def cast_load(dst_fn, src_fn, ncols, tile):
            c0 = 0
            while c0 < ncols:
                w = min(2048, ncols - c0)
                cx.dma(POOL, dst_fn(c0, w), src_fn(c0, w), tile, load=True)
                c0 += w

        cast_load(lambda c0, w: cst[:, c0:c0 + w], lambda c0, w: cst_in[:, c0:c0 + w], C_COLS, cstT)
        cx.op(POOL, lambda: nc.gpsimd.memset(mhalf[:, :], -0.5), [], [miscT])
        cx.dma(SP, expsink[:, :], attn_sink[0:1, :].partition_broadcast(128), miscT, load=True)
        cx.op(ACT, lambda: nc.scalar.activation(out=expsink[:, :], in_=expsink[:, :], func=AF.Exp), [miscT], [miscT])

        sw = ExitStack()
        sw.__enter__()
        Win = sbuf(sw, "Win", [128, 8, WIN_COLS], BF16)
        WinT = Tile("Win", multi=True)
        Wab = sbuf(sw, "Wab", [128, 8, D], BF16)
        WabT = Tile("Wab", multi=True)
        Wpl = sbuf(sw, "Wpl", [128, 4, 256], BF16)
        WplT = Tile("Wpl")
        with ExitStack() as s0:
            ccT = sbuf(s0, "ccT_sb", [128, 8, 2], F32)
            cth = sbuf(s0, "cth", [128, 8, 2], F32)
            siluT = sbuf(s0, "siluT", [128, 8, 2], BF16)
            badaT = sbuf(s0, "badaT", [128, 48], F32)
            brow = sbuf(s0, "brow", [1, 2, D], F32)
            grow = sbuf(s0, "grow", [1, 2, D], F32)
            growT = Tile("grow")
            s0T = Tile("s0")
            wsl = [(Tile("wada%d" % i, multi=True), sbuf(s0, "wada%d" % i, [128, 8, 512], BF16)) for i in range(2)]
            stq = Ring([(Tile("stq%d" % i), sbuf(s0, "stq%d" % i, [128, 2048], F32)) for i in range(5)])
            cast_engs = Ring([DVE, ACT])

            def stage_cast(dst_ap, src_ap, dtile, shape3=None):
                st_t, st = stq.next()
                w = 1
                for d_ in dst_ap.shape[1:]:
                    w *= d_
                sv = st[:, 0:w]
                if shape3 is not None:
                    sv = sv.rearrange("p (k n) -> p k n", k=shape3)
                cx.dma(SP, sv, src_ap, st_t, load=True)
                e = cast_engs.next()
                if e is ACT:
                    cx.op(ACT, lambda: nc.scalar.copy(out=dst_ap, in_=sv), [st_t], [dtile])
                elif e is DVE:
                    cx.op(DVE, lambda: nc.vector.tensor_copy(out=dst_ap, in_=sv), [st_t], [dtile])
                else:
                    cx.op(POOL, lambda: nc.gpsimd.tensor_copy(out=dst_ap, in_=sv), [st_t], [dtile])

            cx.dma(SP, ccT[:, :, :], ccT_in[:, :, :], s0T, load=True)
            cx.dma(SP, badaT[:, :], b_adaT_in[:, :], s0T, load=True)
            cx.dma(SP, brow[:, 0, :], b_ada[0:1, 2 * D:3 * D], s0T, load=True)
            cx.dma(SP, brow[:, 1, :], b_ada[0:1, 5 * D:6 * D], s0T, load=True)
            cx.op(ACT, lambda: nc.scalar.activation(out=cth[:, :, :], in_=ccT[:, :, :], func=AF.Tanh, scale=0.5), [s0T], [s0T])
            cx.op(DVE, lambda: nc.vector.scalar_tensor_tensor(out=cth[:, :, :], in0=cth[:, :, :], scalar=1.0, in1=ccT[:, :, :], op0=ALU.add, op1=ALU.mult), [s0T], [s0T])
            cx.op(DVE, lambda: nc.vector.tensor_scalar(out=siluT[:, :, :], in0=cth[:, :, :], scalar1=0.5, scalar2=None, op0=ALU.mult), [s0T], [s0T])
            w_ada_v = w_ada.rearrange("(kc p) n -> p kc n", p=128)
            w_in_v = w_in_p.rearrange("(kc p) n -> p kc n", p=128)
            w_ab_v = w_ab.rearrange("(kc p) n -> p kc n", p=128)
            bmod_t, bmod = banks[0]
            brw_t, brw = banks[1]

            def load_win():
                for kc in range(8):
                    c0 = 0
                    while c0 < WIN_COLS:
                        w = min(2048, WIN_COLS - c0)
                        stage_cast(Win[:, kc, c0:c0 + w], w_in_v[:, kc, c0:c0 + w], WinT)
                        c0 += w

            def load_wab():
                for kc in range(0, 8, 2):
                    stage_cast(Wab[:, kc:kc + 2, :], w_ab_v[:, kc:kc + 2, :], WabT, shape3=2)

            for pc in range(12):
                if pc == 4:
                    load_win()
                wt, wtile = wsl[pc % 2]
                for h in range(2):
                    stage_cast(wtile[:, :, h * 256:(h + 1) * 256], w_ada_v[:, :, pc * 512 + h * 256:pc * 512 + (h + 1) * 256], wt, shape3=8)
                for mm in range(4):
                    m = pc * 4 + mm
                    for kc in range(8):
                        cx.op(PE, lambda: nc.tensor.matmul(bmod[:, 2 * m:2 * m + 2], lhsT=wtile[:, kc, mm * 128:(mm + 1) * 128],
                                                           rhs=siluT[:, kc, :], start=(kc == 0), stop=(kc == 7)),
                              [wt, s0T], [bmod_t])
                if pc in (4, 5, 10, 11):
                    for kc in range(8):
                        cx.op(PE, lambda: nc.tensor.matmul(brw[0:2, :], lhsT=siluT[:, kc, :], rhs=wtile[:, kc, :],
                                                           start=(kc == 0), stop=(kc == 7)), [wt, s0T], [brw_t])
                    bi = 0 if pc < 6 else 1
                    hf = pc % 2
                    cx.op(DVE, lambda: nc.vector.tensor_tensor(out=grow[0:1, bi, hf * 512:(hf + 1) * 512], in0=brw[0:1, :],
                                                               in1=brow[0:1, bi, hf * 512:(hf + 1) * 512], op=ALU.add),
                          [brw_t, s0T], [growT])
            load_wab()
            cx.op(DVE, lambda: nc.vector.tensor_tensor(out=modT[:, :, :], in0=bmod[:, 0:96].rearrange("p (m j) -> p m j", j=2),
                                                       in1=badaT[:, :].unsqueeze(2).to_broadcast([128, 48, 2]), op=ALU.add),
                  [bmod_t, s0T], [modTt])
            cx.op(DVE, lambda: nc.vector.tensor_scalar(out=modT[:, 8:16, :], in0=modT[:, 8:16, :], scalar1=1.0, scalar2=None, op0=ALU.add), [modTt], [modTt])
            cx.op(DVE, lambda: nc.vector.tensor_scalar(out=modT[:, 32:40, :], in0=modT[:, 32:40, :], scalar1=1.0, scalar2=None, op0=ALU.add), [modTt], [modTt])
            dump("modT", modTt, modT[:, :, :].rearrange("p m j -> p (m j)"))
            cx.dma(SP, g_scr[0:1, :], grow[0:1, 0, :], growT, load=False)
            cx.dma(SP, g_scr[1:2, :], grow[0:1, 1, :], growT, load=False)
            wpf = sbuf(s0, "wpf", [128, 4, 256], F32)
            psb = sbuf(s0, "psb", [128, D], F32)
            s1T = Tile("s1")
            cx.dma(SP, wpf[:, :, :], w_pool.rearrange("g c n -> c g n"), s1T, load=True)
            cx.dma(SP, psb[:, :], pool_scale[0:1, :].partition_broadcast(128), s1T, load=True)
            cx.op(DVE, lambda: nc.vector.tensor_tensor(out=Wpl[:, :, :], in0=wpf[:, :, :],
                                                       in1=psb[:, :].rearrange("p (g n) -> p g n", g=4), op=ALU.mult), [s1T], [WplT])
            cx.barrier()
            if level == 0:
                sw.__exit__(None, None, None)
                raise _Stop()

        with ExitStack() as sa:
            def stopat(label):
                if STOPAT == label:
                    cx.barrier()
                    raise _Stop()

            stopat("w")
            NXS = 2
            xsl = [(Tile("xs%d" % i), sbuf(sa, "xs%d" % i, [128, D], F32)) for i in range(NXS)]
            xhl = [(Tile("xh%d" % i), sbuf(sa, "xh%d" % i, [128, D], BF16)) for i in range(2)]
            uTl = [(Tile("uT%d" % i), sbuf(sa, "uT%d" % i, [128, 8, SBW], BF16)) for i in range(3)]
            QTl = [(Tile("QT%d" % i), sbuf(sa, "QT%d" % i, [128, 8, SBW], BF16)) for i in range(2)]
            KTb = sbuf(sa, "KTb", [128, RING, 2, 4, 128], BF16)
            KTt = [Tile("KT%d" % i) for i in range(RING)]
            Vb = sbuf(sa, "Vb", [128, RING, 4, 65], BF16)
            Vt = [Tile("V%d" % i) for i in range(RING)]
            Pb = sbuf(sa, "Pb", [128, RING, 512], BF16)
            Pt = [Tile("P%d" % i) for i in range(RING)]
            KcT = sbuf(sa, "KcT", [128, 2, 4, CTX], BF16)
            KcTt = Tile("KcT")
            Vcb = sbuf(sa, "Vcb", [128, 2, 4, 65], BF16)
            Vct = [Tile("Vc%d" % i) for i in range(2)]
            NPT = 10
            PTb = sbuf(sa, "PTb", [128, NPT, 512], BF16)
            PTr = Ring([(Tile("PT%d" % i), PTb[:, i, :]) for i in range(NPT)])
            Obl = [(Tile("Ob%d" % i), sbuf(sa, "Ob%d" % i, [128, HEADS, HD], BF16)) for i in range(2)]
            OTt, OT = Tile("OT"), sbuf(sa, "OT", [128, 8, SBW], BF16)
            PLt, PL = Tile("PL"), sbuf(sa, "PL", [128, 4, SBW], BF16)
            mTl = [(Tile("mT%d" % i), sbuf(sa, "mT%d" % i, [128, 8, SBW], BF16)) for i in range(2)]
            zbr = Ring([(Tile("zb%d" % i), sbuf(sa, "zb%d" % i, [128, SBW], BF16)) for i in range(3)])
            t1r = Ring([(Tile("t1%d" % i), sbuf(sa, "t1%d" % i, [128, SBW], F32)) for i in range(3)])
            t2r = Ring([(Tile("t2%d" % i), sbuf(sa, "t2%d" % i, [128, SBW], F32)) for i in range(2)])
            rtl = [(Tile("rt%d" % i), sbuf(sa, "rt%d" % i, [128, 2, SBW], F32)) for i in range(3)]
            tgr = Ring([(Tile("tg%d" % i), sbuf(sa, "tg%d" % i, [128, SBW], F32)) for i in range(4)])
            u1r = Ring([(Tile("u1%d" % i), sbuf(sa, "u1%d" % i, [128, SBW], F32)) for i in range(4)])
            denr = Ring([(Tile("den%d" % i), sbuf(sa, "den%d" % i, [128, 8], F32)) for i in range(2)])
            proj = Ring(banks[0:4])
            STr = Ring(banks[4:6])
            Or = Ring(banks[6:8])

            for i in range(RING):
                cx.op(POOL, lambda: nc.gpsimd.memset(Vb[:, i, :, 64:65], 1.0), [], [Vt[i]])
                cx.op(POOL, lambda: nc.gpsimd.memset(KTb[:, i, :, :, :], 0.0), [], [KTt[i]])
            cx.op(POOL, lambda: nc.gpsimd.memset(KcT[:, :, :, :], 0.0), [], [KcTt])
            for i in range(2):
                cx.op(POOL, lambda: nc.gpsimd.memset(Vcb[:, i, :, 64:65], 1.0), [], [Vct[i]])

            def x_rows(e):
                if e == 0:
                    return x_halo[0:128, :]
                if e == NEXT - 1:
                    return x_halo[128:256, :]
                return x_main[(e - 1) * 128:e * 128, :]

            xs_ctr = [0]
            pending_x = {}

            def issue_x(key, src_ap):
                i = xs_ctr[0] % NXS
                xs_ctr[0] += 1
                t, tl = xsl[i]
                cx.dma(SP, tl[:, :], src_ap, t, load=True)
                pending_x[key] = (t, tl)

            xh_ctr = [0]

            def ln_steps(key, uTt, uT, col0, j):
                stt_ = {}

                def f_stats():
                    xt, xtile = pending_x.pop(key)
                    stt_["x"] = (xt, xtile)
                    stt_["ln"] = ln_stats(xt, xtile)

                def f_hat():
                    xt, xtile = stt_["x"]
                    st, rstd, nmr = stt_["ln"]
                    ht, htile = xhl[xh_ctr[0] % 2]
                    xh_ctr[0] += 1
                    stt_["h"] = (ht, htile)
                    cx.op(ACT, lambda: nc.scalar.activation(out=htile[:, 0:512], in_=xtile[:, 0:512], func=AF.Identity, bias=nmr, scale=rstd), [xt, st], [ht])

                def f_hat_b():
                    xt, xtile = stt_["x"]
                    st, rstd, nmr = stt_["ln"]
                    ht, htile = stt_["h"]
                    cx.op(ACT, lambda: nc.scalar.activation(out=htile[:, 512:1024], in_=xtile[:, 512:1024], func=AF.Identity, bias=nmr, scale=rstd), [xt, st], [ht])

                def f_tr():
                    ht, htile = stt_["h"]
                    TRt, TRb = proj.next()
                    TRv = TRb[:, :].bitcast(BF16)
                    stt_["tr"] = (TRt, TRv)
                    for c in range(8):
                        cx.op(PE, lambda: nc.tensor.transpose(TRv[:, c * 128:(c + 1) * 128], htile[:, c * 128:(c + 1) * 128], ident), [ht, cstT], [TRt])

                def f_mod():
                    TRt, TRv = stt_["tr"]
                    for c in range(8):
                        cx.op(DVE, lambda: nc.vector.tensor_scalar(out=uT[:, c, col0:col0 + 128], in0=TRv[:, c * 128:(c + 1) * 128],
                                                                   scalar1=modT[:, 8 + c, j:j + 1], scalar2=modT[:, c, j:j + 1],
                                                                   op0=ALU.mult, op1=ALU.add), [TRt, modTt], [uTt])
                def f_trmod():
                    f_tr()
                    f_mod()
                return [f_stats, None, f_hat, f_hat_b, f_trmod, None, None, None]

            def ln_modulate(key, uTt, uT, col0, j):
                for f in ln_steps(key, uTt, uT, col0, j):
                    if f is not None:
                        f()

            def proj_fm(uTt, uT, n, off):
                pt, pb = proj.next()
                for kc in range(8):
                    cx.op(PE, lambda: nc.tensor.matmul(pb[:, 0:n], lhsT=Win[:, kc, off:off + 128], rhs=uT[:, kc, 0:n],
                                                       start=(kc == 0), stop=(kc == 7)), [WinT, uTt], [pt])
                return pt, pb

            def rope_front(uTt, uT, n, off, rtt, rt):
                pt, pb = proj_fm(uTt, uT, n, off)
                zt, zb = zbr.next()
                cx.op(ACT, lambda: nc.scalar.copy(out=zb[:, 0:n], in_=pb[:, 0:n]), [pt], [zt])
                t1t, t1 = t1r.next()
                cx.op(DVE, lambda: nc.vector.tensor_tensor(out=t1[:, 0:n], in0=pb[:, 0:n], in1=rt[:, 0, 0:n], op=ALU.mult), [pt, rtt], [t1t])
                return (zt, zb, t1t, t1)

            def rope_back(st_, n, rtt, rt, dsts):
                zt, zb, t1t, t1 = st_
                p2t, p2b = proj.next()
                cx.op(PE, lambda: nc.tensor.matmul(p2b[:, 0:n], lhsT=perm, rhs=zb[:, 0:n], start=True, stop=True), [zt, cstT], [p2t])
                t2t, t2 = t2r.next()
                cx.op(DVE, lambda: nc.vector.tensor_tensor(out=t2[:, 0:n], in0=p2b[:, 0:n], in1=rt[:, 1, 0:n], op=ALU.mult), [p2t, rtt], [t2t])
                for (dt_, dap, c0, w, p0, p1) in dsts:
                    cx.op(POOL, lambda: nc.gpsimd.tensor_tensor(out=dap, in0=t1[p0:p1, c0:c0 + w], in1=t2[p0:p1, c0:c0 + w], op=ALU.add), [t1t, t2t], [dt_])

            def rope_many(uTt, uT, n, rtt, rt, jobs):
                prev = None
                for (off, dsts) in jobs:
                    cur = (rope_front(uTt, uT, n, off, rtt, rt), dsts)
                    if prev is not None:
                        rope_back(prev[0], n, rtt, rt, prev[1])
                    prev = cur
                if prev is not None:
                    rope_back(prev[0], n, rtt, rt, prev[1])

            def v_block(uTt, uT, col0, vt, vap):
                pt, pb = proj.next()
                for kc in range(8):
                    cx.op(PE, lambda: nc.tensor.matmul(pb[:, 0:256], lhsT=uT[:, kc, col0:col0 + 128], rhs=Win[:, kc, V_OFF:V_OFF + 256],
                                                       start=(kc == 0), stop=(kc == 7)), [WinT, uTt], [pt])
                cx.op(ACT, lambda: nc.scalar.copy(out=vap, in_=pb[:, 0:256].rearrange("p (g d) -> p g d", g=4)), [pt], [vt])

            def p_block(uTt, uT, col0, ptile, pap):
                pt, pb = proj.next()
                for kc in range(8):
                    cx.op(PE, lambda: nc.tensor.matmul(pb[:, 0:512], lhsT=uT[:, kc, col0:col0 + 128], rhs=Win[:, kc, P_OFF:P_OFF + 512],
                                                       start=(kc == 0), stop=(kc == 7)), [WinT, uTt], [pt])
                cx.op(DVE, lambda: nc.vector.tensor_copy(out=pap, in_=pb[:, 0:512]), [pt], [ptile])

            uct, uc = uTl[1]
            for i in range(2):
                issue_x(("c", i), ctx_in[i * 128:(i + 1) * 128, :])
            stopat("ms")
            def run_zip(lists):
                k = 0
                while any(k < len(l_) for l_ in lists):
                    for l_ in lists:
                        if k < len(l_) and l_[k] is not None:
                            l_[k]()
                    k += 1

            run_zip([ln_steps(("c", i), uct, uc, i * 128, 1) for i in range(2)])
            stopat("ctxln")

            def ctx_jobs():
                jobs = []
                for g in range(4):
                    def fk(g=g):
                        pt, pb = proj_fm(uct, uc, CTX, K_OFF + g * 128)
                        cx.op(ACT, lambda: nc.scalar.copy(out=KcT[0:64, 0, g, :], in_=pb[0:64, 0:CTX]), [pt], [KcTt])
                        cx.op(ACT, lambda: nc.scalar.copy(out=KcT[64:128, 1, g, :], in_=pb[64:128, 0:CTX]), [pt], [KcTt])
                    jobs.append(fk)
                for i in range(2):
                    jobs.append(lambda i=i: v_block(uct, uc, i * 128, Vct[i], Vcb[:, i, :, 0:64]))
                return jobs

            def sb_blocks(s):
                if s == -1:
                    return [0]
                if s == NSB:
                    return [NEXT - 1]
                return [1 + NB * s + i for i in range(NB)]

            def issue_loads(s):
                bl = sb_blocks(s)
                for e in bl:
                    issue_x(("x", e), x_rows(e))
                rtt, rt = rtl[(s + 3) % 3]
                n = len(bl) * 128
                cx.dma(SP, rt[:, :, 0:n], rope_tab[:, :, bl[0] * 128:bl[0] * 128 + n], rtt, load=True)

            def lnA_steps(s):
                bl = sb_blocks(s)
                uTt, uT = uTl[(s + 3) % 3]
                steps = [lambda: issue_loads(s)]
                for i, e in enumerate(bl):
                    steps += ln_steps(("x", e), uTt, uT, i * 128, 0)
                return steps

            def projA_jobs(s):
                bl = sb_blocks(s)
                n = len(bl) * 128
                main = 0 <= s < NSB
                uTt, uT = uTl[(s + 3) % 3]
                rtt, rt = rtl[(s + 3) % 3]
                hold = {"prev": None}
                jobs = []

                def rope_job(off, dsts):
                    def f():
                        cur = (rope_front(uTt, uT, n, off, rtt, rt), dsts)
                        if hold["prev"] is not None:
                            rope_back(hold["prev"][0], n, rtt, rt, hold["prev"][1])
                        hold["prev"] = cur
                    return f

                def rope_flush():
                    if hold["prev"] is not None:
                        rope_back(hold["prev"][0], n, rtt, rt, hold["prev"][1])
                        hold["prev"] = None

                for g in range(4):
                    dsts = []
                    for i, e in enumerate(bl):
                        dsts.append((KTt[e % RING], KTb[0:64, e % RING, 0, g, :], i * 128, 128, 0, 64))
                        dsts.append((KTt[e % RING], KTb[64:128, e % RING, 1, g, :], i * 128, 128, 64, 128))
                    jobs.append(rope_job(K_OFF + g * 128, dsts))
                jobs.append(rope_flush)
                for i, e in enumerate(bl):
                    jobs.append(lambda i=i, e=e: v_block(uTt, uT, i * 128, Vt[e % RING], Vb[:, e % RING, :, 0:64]))
                for i, e in enumerate(bl):
                    jobs.append(lambda i=i, e=e: p_block(uTt, uT, i * 128, Pt[e % RING], Pb[:, e % RING, :]))
                if main:
                    qt, q = QTl[s % 2]
                    for c in range(8):
                        jobs.append(rope_job(Q_OFF + c * 128, [(qt, q[:, c, 0:n], 0, n, 0, 128)]))
                    jobs.append(rope_flush)
                return jobs

            def projA(s):
                for f in projA_jobs(s):
                    f()

            def pooling(s):
                for i, e in enumerate(sb_blocks(s)):
                    j = e - 1
                    var = 0 if j == 0 else (2 if j == NBLK - 1 else 1)
                    pt, pb = proj.next()
                    for g in range(4):
                        for rel in range(3):
                            ee = e - 1 + rel
                            cx.op(PE, lambda: nc.tensor.matmul(pb[:, g * 128:(g + 1) * 128], lhsT=Pb[:, ee % RING, g * 128:(g + 1) * 128],
                                                               rhs=amat_ap(var, rel, g), start=(rel == 0), stop=(rel == 2)),
                                  [Pt[ee % RING], cstT], [pt])
                    cx.op(ACT, lambda: nc.scalar.copy(out=PL[:, :, i * 128:(i + 1) * 128], in_=pb[:, 0:512].rearrange("p (g t) -> p g t", g=4)), [pt], [PLt])

            def att_S(s, i, g, between=None):
                qt, q = QTl[s % 2]
                e = sb_blocks(s)[i]
                j = e - 1
                c0 = i * 128
                srcs = [
                    (KTt[(e - 1) % RING], KTb[:, (e - 1) % RING, :, g, :], Vt[(e - 1) % RING], Vb[:, (e - 1) % RING, g, :], mask_ap(0 if j == 0 else 1)),
                    (KTt[e % RING], KTb[:, e % RING, :, g, :], Vt[e % RING], Vb[:, e % RING, g, :], None),
                    (KTt[(e + 1) % RING], KTb[:, (e + 1) % RING, :, g, :], Vt[(e + 1) % RING], Vb[:, (e + 1) % RING, g, :], mask_ap(3 if j == NBLK - 1 else 2)),
                    (KcTt, KcT[:, :, g, 0:128], Vct[0], Vcb[:, 0, g, :], None),
                    (KcTt, KcT[:, :, g, 128:256], Vct[1], Vcb[:, 1, g, :], None),
                ]
                pts = []
                for kbi, (kt, kap, vt, vap, mk) in enumerate(srcs):
                    if between is not None and kbi in (2, 4):
                        between()
                    stt, stb = STr.next()
                    cx.op(PE, lambda: nc.tensor.matmul(stb[:, 0:256], lhsT=kap[:, 0, :], rhs=q[:, 2 * g:2 * g + 2, c0:c0 + 128],
                                                       start=True, stop=True, skip_group_check=True), [kt, qt], [stt])
                    cx.op(PE, lambda: nc.tensor.matmul(stb[:, 256:512], lhsT=kap[:, 1, :], rhs=q[:, 2 * g:2 * g + 2, c0:c0 + 128],
                                                       start=True, stop=True, skip_group_check=True), [kt, qt], [stt])
                    ptt, ptb = PTr.next()
                    cx.op(ACT, lambda: nc.scalar.activation(out=ptb, in_=stb[:, 0:512], func=AF.Exp, scale=0.125), [stt], [ptt])
                    if mk is not None:
                        cx.op(POOL, lambda: nc.gpsimd.tensor_tensor(out=ptb.rearrange("p (c q) -> p c q", c=4), in0=ptb.rearrange("p (c q) -> p c q", c=4),
                                                                    in1=mk.unsqueeze(1).to_broadcast([128, 4, 128]), op=ALU.mult), [ptt, cstT], [ptt])
                    pts.append((ptt, ptb, vt, vap))
                return pts

            def att_PV(s, i, g, pts):
                obt, ob = Obl[i % 2]
                ot, obk = Or.next()
                ov = obk[:, 0:260].rearrange("p (c d) -> p c d", d=65)
                for cb in range(4):
                    for kb, (ptt, ptb, vt, vap) in enumerate(pts):
                        cx.op(PE, lambda: nc.tensor.matmul(ov[:, cb, :], lhsT=ptb[:, cb * 128:(cb + 1) * 128], rhs=vap,
                                                           start=(kb == 0), stop=(kb == 4)), [ptt, vt], [ot])
                dt_, den = denr.next()
                cx.op(DVE, lambda: nc.vector.tensor_tensor(out=den[:, 0:4], in0=ov[:, :, 64], in1=expsink[:, 4 * g:4 * g + 4], op=ALU.add), [ot, miscT], [dt_])
                cx.op(DVE, lambda: nc.vector.reciprocal(out=den[:, 4:8], in_=den[:, 0:4]), [dt_], [dt_])
                cx.op(DVE, lambda: nc.vector.tensor_tensor(out=ob[:, 4 * g:4 * g + 4, :], in0=ov[:, :, 0:64],
                                                           in1=den[:, 4:8].unsqueeze(2).to_broadcast([128, 4, 64]), op=ALU.mult), [ot, dt_], [obt])

            def att_T(s, i):
                obt, ob = Obl[i % 2]
                c0 = i * 128
                obf = ob[:, :, :].rearrange("p h d -> p (h d)")
                TRt, TRb = proj.next()
                TRv = TRb[:, :].bitcast(BF16)
                for c in range(8):
                    cx.op(PE, lambda: nc.tensor.transpose(TRv[:, c * 128:(c + 1) * 128], obf[:, c * 128:(c + 1) * 128], ident), [obt, cstT], [TRt])
                cx.op(ACT, lambda: nc.scalar.copy(out=OT[:, :, c0:c0 + 128], in_=TRv[:, :].rearrange("p (c t) -> p c t", c=8)), [TRt], [OTt])

            def attention(s, pend, fill):
                units = [(i, g) for i in range(NB) for g in range(4)]
                prev = None
                nfill = len(fill)
                def one_fill():
                    if fill:
                        fill.pop(0)()

                pend_T = None
                for ui, (i, g) in enumerate(units):
                    if ui == 4:
                        while len(fill) > nfill - 7 and fill:
                            fill.pop(0)()
                    pts = att_S(s, i, g, between=one_fill)
                    if pend_T is not None:
                        pend_T[1] -= 1
                        if pend_T[1] == 0:
                            att_T(s, pend_T[0])
                            pend_T = None
                    if prev is not None:
                        pi, pg, ppts = prev
                        att_PV(s, pi, pg, ppts)
                        if pg == 3:
                            pend_T = [pi, 2]
                    prev = (i, g, pts)
                    if ui < 4:
                        one_fill()
                    run_steps_a(pend, 2)
                pi, pg, ppts = prev
                att_PV(s, pi, pg, ppts)
                for _ in range(3):
                    one_fill()
                att_T(s, pi)

            def run_steps_a(q_, n_):
                for _ in range(n_):
                    if q_:
                        f = q_.pop(0)
                        if f is not None:
                            f()

            mt_pending = []

            def merge(s):
                uTt, uT = uTl[(s + 3) % 3]
                mt, mT = mTl[s % 2]
                for m in range(8):
                    tgs = []
                    for a, off in enumerate((GA_OFF, GP_OFF)):
                        tgt, tg = tgr.next()
                        pt, pb = proj_fm(uTt, uT, SBW, off + m * 128)
                        cx.op(ACT, lambda: nc.scalar.activation(out=tg[:, :], in_=pb[:, 0:SBW], func=AF.Tanh, scale=0.5), [pt], [tgt])
                        tgs.append((tgt, tg))
                    u1t, u1 = u1r.next()
                    u2t_, u2_ = u1r.next()
                    pt, pb = proj.next()
                    for c in range(8):
                        cx.op(PE, lambda: nc.tensor.matmul(pb[:, 0:SBW], lhsT=Wab[:, c, m * 128:(m + 1) * 128], rhs=OT[:, c, :],
                                                           start=(c == 0), stop=(c == 7)), [WabT, OTt], [pt])
                    cx.op(DVE, lambda: nc.vector.scalar_tensor_tensor(out=u1[:, :], in0=tgs[0][1][:, :], scalar=1.0, in1=pb[:, 0:SBW],
                                                                      op0=ALU.add, op1=ALU.mult), [tgs[0][0], pt], [u1t])
                    pt, pb = proj.next()
                    g = m // 2
                    cx.op(PE, lambda: nc.tensor.matmul(pb[:, 0:SBW], lhsT=Wpl[:, g, (m % 2) * 128:(m % 2) * 128 + 128], rhs=PL[:, g, :],
                                                       start=True, stop=True), [WplT, PLt], [pt])
                    cx.op(DVE, lambda: nc.vector.scalar_tensor_tensor(out=u2_[:, :], in0=tgs[1][1][:, :], scalar=1.0, in1=pb[:, 0:SBW],
                                                                      op0=ALU.add, op1=ALU.mult), [tgs[1][0], pt], [u2t_])
                    cx.op(POOL, lambda: nc.gpsimd.tensor_tensor(out=mT[:, m, :], in0=u1[:, :], in1=u2_[:, :], op=ALU.add), [u1t, u2t_], [mt])
                mt_pending.append(lambda: cx.dma(SP, mt_scr[s], mT[:, :, :].rearrange("p c t -> p (c t)"), mt, load=False))

            stopat("ctx")
            def jobs_with(jobs, pend_, k_):
                for jb in jobs:
                    jb()
                    run_steps_a(pend_, k_)

            pend0 = lnA_steps(-1) + lnA_steps(0)
            jobs_with(ctx_jobs(), pend0, 3)
            run_steps_a(pend0, len(pend0))
            pend1 = lnA_steps(1)
            jobs_with(projA_jobs(-1), pend1, 1)
            jobs_with(projA_jobs(0), pend1, 1)
            run_steps_a(pend1, len(pend1))
            if DEBUG:
                dump("uT0", uTl[0][0], uTl[0][1][:, :, :].rearrange("p c t -> p (c t)"))
                dump("QT0", QTl[0][0], QTl[0][1][:, :, :].rearrange("p c t -> p (c t)"))
                dump("KT1", KTt[1], KTb[:, 1, 0, :, :].rearrange("p g t -> p (g t)"))
                dump("V1", Vt[1], Vb[:, 1, :, :].rearrange("p g d -> p (g d)"))
            if level == 1 and nsb_run == 0:
                cx.barrier()
                raise _Stop()
            for s in range(nsb_run):
                pend = lnA_steps(s + 2) if s + 2 <= NSB else []
                fill = projA_jobs(s + 1)
                while mt_pending:
                    mt_pending.pop(0)()
                attention(s, pend, fill)
                while fill:
                    fill.pop(0)()
                pooling(s)
                merge(s)
                run_steps_a(pend, len(pend))
                if DEBUG and s == 0:
                    dump("OT0", OTt, OT[:, :, :].rearrange("p c t -> p (c t)"))
                    dump("PL0", PLt, PL[:, :, :].rearrange("p g t -> p (g t)"))
                    dump("mt", mTl[0][0], mTl[0][1][:, :, :].rearrange("p c t -> p (c t)"))
            while mt_pending:
                mt_pending.pop(0)()
            cx.barrier()
            if level == 1:
                raise _Stop()

        sw.__exit__(None, None, None)

        with ExitStack() as sbc:
            W1 = sbuf(sbc, "W1", [128, 8, DFF], BF16)
            W1T = Tile("W1")
            W2 = sbuf(sbc, "W2", [128, 32, D], BF16)
            W2T = Tile("W2")
            lng = sbuf(sbc, "lng", [128, D], F32)
            lnb = sbuf(sbc, "lnb", [128, D], F32)
            lnT = Tile("lnT")

            def bcast_row(ri, dst, dstT, scale):
                cx.dma(SP, dst[:, :], g_scr[ri:ri + 1, :].partition_broadcast(128), dstT, load=True)
                if scale != 1.0:
                    cx.op(ACT, lambda: nc.scalar.mul(out=dst[:, :], in_=dst[:, :], mul=scale), [dstT], [dstT])

            with ExitStack() as sb_:
                Wout = sbuf(sb_, "Wout", [128, 8, D], BF16)
                WoutTk = [Tile("Wout%d" % i) for i in range(8)]
                g1b = sbuf(sb_, "g1b", [128, D], F32)
                g1bT = Tile("g1b")
                stg = [(Tile("stg%d" % i), sbuf(sb_, "stg%d" % i, [128, D], F32)) for i in range(2)]
                mll = [(Tile("ml%d" % i), sbuf(sb_, "ml%d" % i, [128, 8, SBW], BF16)) for i in range(2)]
                NXB = 5
                xbl = [(Tile("xb%d" % i), sbuf(sb_, "xb%d" % i, [128, D], F32)) for i in range(NXB)]
                bcast_row(0, g1b, g1bT, 0.5)
                cx.dma(SP, lng[:, :], ln1_g[0:1, :].partition_broadcast(128), lnT, load=True)
                cx.dma(SP, lnb[:, :], ln1_b[0:1, :].partition_broadcast(128), lnT, load=True)
                w_out_v = w_out.rearrange("(kc p) n -> p kc n", p=128)
                for kc in range(8):
                    st_t, st_ = stg[kc % 2]
                    cx.dma(SP, st_[:, :], w_out_v[:, kc, :], st_t, load=True)
                    if kc % 2 == 0:
                        cx.op(DVE, lambda: nc.vector.tensor_tensor(out=Wout[:, kc, :], in0=st_[:, :], in1=g1b[:, :], op=ALU.mult), [st_t, g1bT], [WoutTk[kc]])
                    else:
                        cx.op(POOL, lambda: nc.gpsimd.tensor_tensor(out=Wout[:, kc, :], in0=st_[:, :], in1=g1b[:, :], op=ALU.mult), [st_t, g1bT], [WoutTk[kc]])
                w1_v = w_mlp_in.rearrange("(kc p) n -> p kc n", p=128)
                w2_v = w_mlp_out.rearrange("(kc p) n -> p kc n", p=128)
                w2_jobs = list(range(32))
                w1_jobs = [(kc, q) for kc in range(8) for q in range(4)]
                stg_ctr = [8]

                def w_job():
                    for _ in range(1):
                        if w1_jobs:
                            kc, q = w1_jobs.pop(0)
                            st_t, st_ = stg[stg_ctr[0] % 2]
                            stg_ctr[0] += 1
                            cx.dma(SP, st_[:, :], w1_v[:, kc, q * 1024:(q + 1) * 1024], st_t, load=True)
                            cx.op(ACT, lambda: nc.scalar.copy(out=W1[:, kc, q * 1024:(q + 1) * 1024], in_=st_[:, :]), [st_t], [W1T])
                        elif w2_jobs:
                            c = w2_jobs.pop(0)
                            st_t, st_ = stg[stg_ctr[0] % 2]
                            stg_ctr[0] += 1
                            cx.dma(SP, st_[:, :], w2_v[:, c, :], st_t, load=True)
                            cx.op(ACT, lambda: nc.scalar.copy(out=W2[:, c, :], in_=st_[:, :]), [st_t], [W2T])

                projB = Ring(banks[0:8])
                xb_ctr = [0]

                def blockB_steps(s_, i, mlt, ml):
                    r0 = (s_ * NB + i) * 128
                    xt, xb = xbl[xb_ctr[0] % NXB]
                    xb_ctr[0] += 1
                    st = {}

                    def head():
                        cx.dma(SP, xb[:, :], x_main[r0:r0 + 128, :], xt, load=True)
                        for hf in range(2):
                            pt, pb = projB.next()
                            for kc in range(8):
                                cx.op(PE, lambda: nc.tensor.matmul(pb[:, 0:512], lhsT=ml[:, kc, i * 128:(i + 1) * 128], rhs=Wout[:, kc, hf * 512:(hf + 1) * 512],
                                                                   start=(kc == 0), stop=(kc == 7)), [mlt, WoutTk[kc]], [pt])
                            cx.op(DVE, lambda: nc.vector.scalar_tensor_tensor(out=xb[:, hf * 512:(hf + 1) * 512], in0=xb[:, hf * 512:(hf + 1) * 512], scalar=ALPHA,
                                                                              in1=pb[:, 0:512], op0=ALU.mult, op1=ALU.add), [xt, pt], [xt])

                    def s_stats():
                        st["ln"] = ln_stats(xt, xb)

                    def s_norm():
                        t_, rstd, nmr = st["ln"]
                        cx.op(ACT, lambda: nc.scalar.activation(out=xb[:, :], in_=xb[:, :], func=AF.Identity, bias=nmr, scale=rstd), [xt, t_], [xt])

                    def s_mul():
                        cx.op(DVE, lambda: nc.vector.tensor_tensor(out=xb[:, 0:512], in0=xb[:, 0:512], in1=lng[:, 0:512], op=ALU.mult), [xt, lnT], [xt])
                        cx.op(POOL, lambda: nc.gpsimd.tensor_tensor(out=xb[:, 512:1024], in0=xb[:, 512:1024], in1=lng[:, 512:1024], op=ALU.mult), [xt, lnT], [xt])

                    def s_add():
                        cx.op(POOL, lambda: nc.gpsimd.tensor_tensor(out=xb[:, :], in0=xb[:, :], in1=lnb[:, :], op=ALU.add), [xt, lnT], [xt])

                    def s_store():
                        cx.dma(SP, x1_scr[r0:r0 + 128, :], xb[:, :], xt, load=False)
                        if DEBUG and s_ == 0:
                            cx.dma(SP, dbg["x1"][i * 128:(i + 1) * 128, :], xb[:, :], xt, load=False)

                    return head, [s_stats, None, None, s_norm, None, s_mul, None, s_add, None, None, None, s_store]

                active = []
                for s in range(NSB):
                    mlt, ml = mll[s % 2]
                    cx.dma(SP, ml[:, :, :].rearrange("p c t -> p (c t)"), mt_scr[s], mlt, load=True)
                    for i in range(NB):
                        head, steps = blockB_steps(s, i, mlt, ml)
                        head()
                        for st_ in active:
                            for _ in range(3):
                                if st_:
                                    f_ = st_.pop(0)
                                    if f_ is not None:
                                        f_()
                        active = [a for a in active if a]
                        active.append(steps)
                        w_job()
                        w_job()
                while active:
                    for st_ in active:
                        if st_:
                            f_ = st_.pop(0)
                            if f_ is not None:
                                f_()
                    active = [a for a in active if a]
                while w2_jobs or w1_jobs:
                    w_job()
                cx.barrier()
                if level == 2:
                    raise _Stop()

            with ExitStack() as sc:
                cx.dma(SP, lng[:, :], ln2_g[0:1, :].partition_broadcast(128), lnT, load=True)
                cx.dma(SP, lnb[:, :], ln2_b[0:1, :].partition_broadcast(128), lnT, load=True)
                g2b = sbuf(sc, "g2b", [128, D], F32)
                g2bT = Tile("g2b")
                bcast_row(1, g2b, g2bT, 1.0)
                zr = Ring([(Tile("zt%d" % i), sbuf(sc, "zt%d" % i, [128, 512], F32)) for i in range(4)])
                NXC = 6
                xcl = [(Tile("xc%d" % i), sbuf(sc, "xc%d" % i, [128, D], F32)) for i in range(NXC)]
                xhc = [(Tile("xhc%d" % i), sbuf(sc, "xhc%d" % i, [128, D], BF16)) for i in range(2)]
                u2l = [(Tile("u2%d" % i), sbuf(sc, "u2%d" % i, [128, 8, SBW], BF16)) for i in range(2)]
                rlr = Ring([(Tile("rl%d" % i), sbuf(sc, "rl%d" % i, [128, SBW], F32)) for i in range(4)])
                hr = Ring([(Tile("h%d" % i), sbuf(sc, "h%d" % i, [128, SBW], BF16)) for i in range(6)])
                projC = Ring(banks[0:4])
                acc = banks[4:8]
                xc_ctr = [0]
                loaded = {}

                def load_sb(s):
                    for i in range(NB):
                        r0 = (s * NB + i) * 128
                        xt, xc = xcl[xc_ctr[0] % NXC]
                        xc_ctr[0] += 1
                        cx.dma(SP, xc[:, :], x1_scr[r0:r0 + 128, :], xt, load=True)
                        loaded[(s, i)] = (xt, xc)

                load_sb(0)
                xh_c = [0]
                from collections import deque
                DSK = 3

                def interleave(lists):
                    out_ = []
                    k = 0
                    while any(k < len(l_) for l_ in lists):
                        for l_ in lists:
                            if k < len(l_):
                                out_.append(l_[k])
                        k += 1
                    return out_

                def lnmod_steps(s_):
                    u2t, u2 = u2l[s_ % 2]
                    steps = []
                    for i in range(NB):
                        xt, xc = loaded[(s_, i)]
                        stt_ = {}

                        def f_stats(xt=xt, xc=xc, stt_=stt_):
                            stt_["ln"] = ln_stats(xt, xc)

                        def f_hat(xt=xt, xc=xc, stt_=stt_):
                            t_, rstd, nmr = stt_["ln"]
                            ht, htile = xhc[xh_c[0] % 2]
                            xh_c[0] += 1
                            stt_["h"] = (ht, htile)
                            cx.op(ACT, lambda: nc.scalar.activation(out=htile[:, :], in_=xc[:, :], func=AF.Identity, bias=nmr, scale=rstd), [xt, t_], [ht])

                        def f_tr(stt_=stt_):
                            ht, htile = stt_["h"]
                            TRt, TRb = projC.next()
                            TRv = TRb[:, :].bitcast(BF16)
                            stt_["tr"] = (TRt, TRv)
                            for c in range(8):
                                cx.op(PE, lambda: nc.tensor.transpose(TRv[:, c * 128:(c + 1) * 128], htile[:, c * 128:(c + 1) * 128], ident), [ht, cstT], [TRt])

                        def f_mod(i=i, stt_=stt_):
                            TRt, TRv = stt_["tr"]
                            for c in range(8):
                                cx.op(DVE, lambda: nc.vector.tensor_scalar(out=u2[:, c, i * 128:(i + 1) * 128], in0=TRv[:, c * 128:(c + 1) * 128],
                                                                           scalar1=modT[:, 32 + c, 0:1], scalar2=modT[:, 24 + c, 0:1],
                                                                           op0=ALU.mult, op1=ALU.add), [TRt, modTt], [u2t])
                        steps.append((f_stats, f_hat, f_tr, f_mod))
                    slots = [None] * (14 * len(steps) + 8)
                    for b_, (a_, h_, t_, m_) in enumerate(steps):
                        o_ = 14 * b_
                        slots[o_], slots[o_ + 11], slots[o_ + 15], slots[o_ + 17] = a_, h_, t_, m_
                    return slots

                def fin_steps(s_):
                    z_steps, steps_all = [], []
                    for i in range(NB):
                        steps = []
                        xt, xc = loaded.pop((s_, i))
                        r0 = (s_ * NB + i) * 128
                        stt_ = {}
                        for hf in range(2):
                            zst = {}

                            def f_z(i=i, hf=hf, zst=zst):
                                at, ab = acc[i * 2 + hf]
                                zt_, zz = zr.next()
                                zst["z"] = (zt_, zz)
                                cx.op(DVE, lambda: nc.vector.tensor_tensor(out=zz[:, :], in0=ab[:, 0:512], in1=g2b[:, hf * 512:(hf + 1) * 512], op=ALU.mult), [at, g2bT], [zt_])

                            def f_z2(hf=hf, xt=xt, xc=xc, zst=zst):
                                zt_, zz = zst["z"]
                                cx.op(DVE, lambda: nc.vector.scalar_tensor_tensor(out=xc[:, hf * 512:(hf + 1) * 512], in0=xc[:, hf * 512:(hf + 1) * 512], scalar=ALPHA,
                                                                                  in1=zz[:, :], op0=ALU.mult, op1=ALU.add), [xt, zt_], [xt])
                            z_steps.append(f_z)
                            steps.append(f_z2)
                            steps.append(None)
                            steps.append(None)

                        def f_stats(xt=xt, xc=xc, stt_=stt_):
                            stt_["ln"] = ln_stats(xt, xc)

                        def f_norm(xt=xt, xc=xc, stt_=stt_):
                            t_, rstd, nmr = stt_["ln"]
                            cx.op(ACT, lambda: nc.scalar.activation(out=xc[:, :], in_=xc[:, :], func=AF.Identity, bias=nmr, scale=rstd), [xt, t_], [xt])

                        def f_mul(xt=xt, xc=xc):
                            cx.op(DVE, lambda: nc.vector.tensor_tensor(out=xc[:, 0:512], in0=xc[:, 0:512], in1=lng[:, 0:512], op=ALU.mult), [xt, lnT], [xt])
                            cx.op(POOL, lambda: nc.gpsimd.tensor_tensor(out=xc[:, 512:1024], in0=xc[:, 512:1024], in1=lng[:, 512:1024], op=ALU.mult), [xt, lnT], [xt])

                        def f_add(xt=xt, xc=xc):
                            cx.op(POOL, lambda: nc.gpsimd.tensor_tensor(out=xc[:, :], in0=xc[:, :], in1=lnb[:, :], op=ALU.add), [xt, lnT], [xt])

                        def f_store(xt=xt, xc=xc, r0=r0):
                            cx.dma(SP, out[r0:r0 + 128, :], xc[:, :], xt, load=False)
                        steps += [f_stats] + [None] * 5 + [f_norm] + [None] * 2 + [f_mul] + [None] * 2 + [f_add] + [None] * 4 + [f_store]
                        steps_all.append(steps)
                    return z_steps, interleave(steps_all)

                def run_steps(q, n):
                    for _ in range(n):
                        if q:
                            f = q.popleft()
                            if f is not None:
                                f()

                q_pre = deque(lnmod_steps(0))
                run_steps(q_pre, len(q_pre))
                pend = deque()
                for s in range(NSB):
                    if s + 1 < NSB:
                        load_sb(s + 1)
                    u2t, u2 = u2l[s % 2]
                    if s + 1 < NSB:
                        pend.extend(lnmod_steps(s + 1))
                    hs = {}
                    for c in range(32 + DSK):
                        if c < 32:
                            pt, pb = projC.next()
                            for kc in range(8):
                                cx.op(PE, lambda: nc.tensor.matmul(pb[:, 0:SBW], lhsT=W1[:, kc, c * 128:(c + 1) * 128], rhs=u2[:, kc, :],
                                                                   start=(kc == 0), stop=(kc == 7)), [W1T, u2t], [pt])
                            rt_, rl = rlr.next()
                            cx.op(ACT, lambda: nc.scalar.activation(out=rl[:, :], in_=pb[:, 0:SBW], func=AF.Relu), [pt], [rt_])
                            ht_, hh = hr.next()
                            cx.op(DVE, lambda: nc.vector.tensor_tensor(out=hh[:, :], in0=pb[:, 0:SBW], in1=rl[:, :], op=ALU.mult), [pt, rt_], [ht_])
                            hs[c] = (ht_, hh)
                        cc = c - DSK
                        if cc >= 0:
                            ht_, hh = hs.pop(cc)
                            for i in range(NB):
                                for hf in range(2):
                                    at, ab = acc[i * 2 + hf]
                                    cx.op(PE, lambda: nc.tensor.matmul(ab[:, 0:512], lhsT=hh[:, i * 128:(i + 1) * 128], rhs=W2[:, cc, hf * 512:(hf + 1) * 512],
                                                                       start=(cc == 0), stop=(cc == 31)), [ht_, W2T], [at])
                        run_steps(pend, 3)
                    z_steps, f_steps = fin_steps(s)
                    if s + 1 < NSB:
                        for f in z_steps:
                            f()
                        rest = deque(f_steps)
                        run_steps(pend, len(pend))
                        pend = rest
                    else:
                        for f in z_steps:
                            f()
                        run_steps(pend, len(pend))
                        pend = deque(f_steps)
                        run_steps(pend, len(pend))
                cx.barrier([SP])


def _const_pack(core):
    half = core % 2
    cst = np.zeros((128, C_COLS), np.float32)
    cst[:, C_ID:C_ID + 128] = np.eye(128, dtype=np.float32)
    cst[:, C_ID4:C_ID4 + 512] = np.tile(np.eye(128, dtype=np.float32), (1, 4))
    qi = np.arange(128)[:, None]
    ki = np.arange(128)[None, :]
    kk = np.arange(128)[:, None]
    qq = np.arange(128)[None, :]
    prev_mid = np.where(kk >= qq, 1.0, 0.0).astype(np.float32)
    next_mid = np.where(kk <= qq, 1.0, 0.0).astype(np.float32)
    allneg = np.zeros((128, 128), np.float32)
    masks = [allneg if half == 0 else prev_mid, prev_mid, next_mid, allneg if half == 1 else next_mid]
    for i, m in enumerate(masks):
        cst[:, C_MASK + i * 128:C_MASK + (i + 1) * 128] = m
    pm = np.zeros((128, 128), np.float32)
    for m_ in range(128):
        d = m_ % 64
        partner = d + 16 if (d % 32) < 16 else d - 16
        pm[(m_ // 64) * 64 + partner, m_] = 1.0
    cst[:, C_PERM:C_PERM + 128] = pm
    start = half * TOK
    for var in range(3):
        j = 0 if var == 0 else (NBLK - 1 if var == 2 else 5)
        base = start + j * 128
        for g, w in enumerate((2, 4, 8, 16)):
            T = base + np.arange(128)
            lo = np.clip(T - w // 2, 0, SEQ)
            hi = np.clip(T + w // 2, 0, SEQ)
            cnt = (hi - lo).astype(np.float32)
            for rel in range(3):
                Tp = base + (rel - 1) * 128 + np.arange(128)
                inwin = (Tp[:, None] >= lo[None, :]) & (Tp[:, None] < hi[None, :])
                A = np.where(inwin, 1.0 / cnt[None, :], 0.0).astype(np.float32)
                if rel == 1:
                    A = A - np.eye(128, dtype=np.float32)
                o = C_AMAT + ((var * 3 + rel) * 4 + g) * 128
                cst[:, o:o + 128] = A
    return cst


def _rope_tab(core):
    half = core % 2
    t = half * TOK - 128 + np.arange(NEXT * 128)
    t = np.clip(t, 0, SEQ - 1)
    rows = (t // 64).astype(np.float64)
    cols = (t % 64).astype(np.float64)
    inv = 1.0 / (10000.0 ** (np.arange(16, dtype=np.float64) / 16.0))
    tab = np.zeros((128, 2, NEXT * 128), np.float32)
    for p in range(128):
        d = p % 64
        pos = rows if d < 32 else cols
        i = d % 16
        sign = -1.0 if (d % 32) < 16 else 1.0
        ang = pos * inv[i]
        tab[p, 0] = np.cos(ang).astype(np.float32)
        tab[p, 1] = (sign * np.sin(ang)).astype(np.float32)
    return tab


_NC_CACHE = {}


def make_in_maps(x, c, ctx, c_ctx, w_ada, b_ada, w_in, w_attn_branch, w_pool, pool_scale, attn_sink, w_out,
                 ln1_g, ln1_b, w_mlp_in, w_mlp_out, ln2_g, ln2_b, cores=None):
    f = lambda a: np.ascontiguousarray(np.asarray(a, dtype=np.float32))
    x, c, ctx, c_ctx = f(x), f(c), f(ctx), f(c_ctx)
    w_ada, b_ada, w_in = f(w_ada)[0], f(b_ada), f(w_in)[0]
    cols = []
    cols.append(w_in[:, 2048:4096])
    for cc in range(8):
        g, r = cc // 2, cc % 2
        h0, h1 = 4 * g + r, 4 * g + 2 + r
        cols.append(w_in[:, h0 * 64:(h0 + 1) * 64])
        cols.append(w_in[:, h1 * 64:(h1 + 1) * 64])
    for g in range(4):
        kg = w_in[:, 1024 + g * 64:1024 + (g + 1) * 64]
        cols.append(kg)
        cols.append(kg)
    cols.append(w_in[:, 1280:1536])
    cols.append(w_in[:, 1536:2048])
    w_in_p = np.ascontiguousarray(np.concatenate(cols, axis=1))
    assert w_in_p.shape == (D, WIN_COLS)
    b_adaT = np.ascontiguousarray(b_ada[0].reshape(48, 128).T)
    shared = {
        "w_ada": w_ada, "b_adaT": b_adaT, "b_ada": b_ada, "w_in_p": w_in_p,
        "w_ab": f(w_attn_branch)[0], "w_pool": f(w_pool)[0], "pool_scale": f(pool_scale),
        "attn_sink": f(attn_sink), "w_out": f(w_out)[0], "ln1_g": f(ln1_g), "ln1_b": f(ln1_b),
        "w_mlp_in": f(w_mlp_in)[0], "w_mlp_out": f(w_mlp_out)[0], "ln2_g": f(ln2_g), "ln2_b": f(ln2_b),
    }
    in_maps = []
    for core in (range(NCORES) if cores is None else cores):
        b, half = core // 2, core % 2
        t0 = half * TOK
        halo = np.zeros((256, D), np.float32)
        if half == 1:
            halo[0:128] = x[b, t0 - 128:t0]
        else:
            halo[128:256] = x[b, t0 + TOK:t0 + TOK + 128]
        cc2 = np.stack([c[b], c_ctx], axis=1)
        ccT = np.ascontiguousarray(cc2.reshape(8, 128, 2).transpose(1, 0, 2))
        m = dict(shared)
        m.update({
            "x_main": np.ascontiguousarray(x[b, t0:t0 + TOK]), "x_halo": halo, "ctx_in": np.ascontiguousarray(ctx[b]),
            "ccT": ccT, "rope_tab": _rope_tab(core), "cst": _const_pack(core),
        })
        in_maps.append(m)
    return in_maps


def kernel(x, c, ctx, c_ctx, w_ada, b_ada, w_in, w_attn_branch, w_pool, pool_scale, attn_sink, w_out,
           ln1_g, ln1_b, w_mlp_in, w_mlp_out, ln2_g, ln2_b):
    in_maps = make_in_maps(x, c, ctx, c_ctx, w_ada, b_ada, w_in, w_attn_branch, w_pool, pool_scale, attn_sink, w_out,
                           ln1_g, ln1_b, w_mlp_in, w_mlp_out, ln2_g, ln2_b)
    if "nc" not in _NC_CACHE:
        _NC_CACHE["nc"] = build_program()
    res = run_bass_kernel_spmd(_NC_CACHE["nc"], in_maps, core_ids=list(range(NCORES)))
    _NC_CACHE["last"] = res
    outp = np.empty((4, SEQ, D), np.float32)
    for core in range(NCORES):
        b, half = core // 2, core % 2
        outp[b, half * TOK:(half + 1) * TOK] = res.results[core]["out"]
    return outp
```
